# Optimizing a Trainium2 kernel written in Bass

```python
import jax, jax.numpy as jnp
from jax import lax
import numpy as np

D_MODEL = 1024
BATCH = 32
SEQ = 2048
DEPTH = 2

CHUNK = 64
Q_BLOCK = 128
N_BRANCH = 3

POOL_WINDOWS = (2, 4, 8, 16)
POOL_GROUPS = 4
POOL_WIDTH = D_MODEL
POOL_GW = POOL_WIDTH // POOL_GROUPS

MLA_HEADS = 8
QK_NOPE = 128
QK_ROPE = 64
V_DIM = 128
Q_LORA = D_MODEL // 4
KV_LORA = D_MODEL // 8
MLA_WIDTH = MLA_HEADS * V_DIM
ROPE_THETA = 10000.0

SG_WIDTH = D_MODEL
SG_BLOCK = 128
SG_GROUPS = 8
SG_GW = SG_WIDTH // SG_GROUPS

D_FF = 2816
CONV_W = 3

OFF_POOL = N_BRANCH * D_MODEL
OFF_CQ = OFF_POOL + POOL_WIDTH
OFF_CKV = OFF_CQ + Q_LORA
OFF_KR = OFF_CKV + KV_LORA
OFF_SG = OFF_KR + QK_ROPE
IN_WIDTH = OFF_SG + 2 * SG_WIDTH

ALPHA = (2 * DEPTH) ** 0.25
BETA = (8 * DEPTH) ** -0.25
LN_EPS = 1e-5
RMS_EPS = 1e-6

kernel_name = "hybrid_pool_mla_sgu_convffn_deepnorm"


def _layer_norm(x):
    xf = x.astype(jnp.float32)
    mu = jnp.mean(xf, axis=-1, keepdims=True)
    var = jnp.mean(jnp.square(xf - mu), axis=-1, keepdims=True)
    return ((xf - mu) * lax.rsqrt(var + LN_EPS)).astype(x.dtype)


def _rms_norm(x, g):
    xf = x.astype(jnp.float32)
    y = xf * lax.rsqrt(jnp.mean(jnp.square(xf), axis=-1, keepdims=True) + RMS_EPS)
    return y.astype(x.dtype) * g


def _rope_tables(pos):
    freqs = ROPE_THETA ** (-jnp.arange(0, QK_ROPE, 2, dtype=jnp.float32) / QK_ROPE)
    ang = pos.astype(jnp.float32)[..., None] * freqs
    return jnp.cos(ang), jnp.sin(ang)


def _apply_rope(x, cos, sin):
    xf = x.astype(jnp.float32)
    x1, x2 = jnp.split(xf, 2, axis=-1)
    out = jnp.concatenate([x1 * cos - x2 * sin, x2 * cos + x1 * sin], axis=-1)
    return out.astype(x.dtype)


def _modulate(x, shift, scale):
    return _layer_norm(x) * (1.0 + scale[:, None, :]) + shift[:, None, :]


def _pool_mixer(a, w_pool, s_pool):
    B, S, _ = a.shape
    af = a.astype(jnp.float32)
    cs = jnp.pad(jnp.cumsum(af, axis=1), ((0, 0), (1, 0), (0, 0)))
    outs = []
    for g, w in enumerate(POOL_WINDOWS):
        c_g = cs[..., g * POOL_GW:(g + 1) * POOL_GW]
        lag = jnp.pad(c_g, ((0, 0), (w, 0), (0, 0)))[:, :S + 1]
        cnt = jnp.minimum(jnp.arange(1, S + 1), w).astype(jnp.float32)[None, :, None]
        mean = (c_g[:, 1:] - lag[:, 1:]) / cnt
        outs.append(mean - af[..., g * POOL_GW:(g + 1) * POOL_GW])
    pooled = jnp.stack(outs, axis=2).astype(a.dtype)
    mixed = jnp.einsum('bsgc,gcd->bsgd', pooled, w_pool)
    return mixed.reshape(B, S, POOL_WIDTH) * s_pool


def _mla(cq, ckv, kr, cos, sin, g_q, w_uq, g_kv, w_ukv):
    B, S, _ = cq.shape
    q = jnp.einsum('bsr,rhd->bshd', _rms_norm(cq, g_q), w_uq)
    kv = jnp.einsum('bsr,rhd->bshd', _rms_norm(ckv, g_kv), w_ukv)
    q_nope, q_rope = q[..., :QK_NOPE], q[..., QK_NOPE:]
    k_nope, v = kv[..., :QK_NOPE], kv[..., QK_NOPE:]
    q_rope = _apply_rope(q_rope, cos[:, :, None, :], sin[:, :, None, :])
    k_rope = _apply_rope(kr, cos, sin)
    scale = (QK_NOPE + QK_ROPE) ** -0.5
    neg = jnp.finfo(jnp.float32).min
    outs = []
    for i in range(S // Q_BLOCK):
        q0, q1 = i * Q_BLOCK, (i + 1) * Q_BLOCK
        s = (jnp.einsum('bqhd,bkhd->bhqk', q_nope[:, q0:q1], k_nope[:, :q1])
             + jnp.einsum('bqhd,bkd->bhqk', q_rope[:, q0:q1], k_rope[:, :q1]))
        s = s.astype(jnp.float32) * scale
        qc = (q0 + jnp.arange(Q_BLOCK)) // CHUNK
        kc = jnp.arange(q1) // CHUNK
        s = jnp.where(kc[None, :] <= qc[:, None], s, neg)
        p = jax.nn.softmax(s, axis=-1).astype(v.dtype)
        outs.append(jnp.einsum('bhqk,bkhd->bqhd', p, v[:, :q1]))
    o = jnp.concatenate(outs, axis=1)
    return o.reshape(B, S, MLA_WIDTH)


def _sgu(uv, g_sg, b_sg, w_s, b_s):
    uv = jax.nn.gelu(uv, approximate=False)
    u, v = jnp.split(uv, 2, axis=-1)
    v = _layer_norm(v) * g_sg + b_sg
    B, S, _ = v.shape
    nb = S // SG_BLOCK
    v = v.reshape(B, nb, SG_BLOCK, SG_GROUPS, SG_GW)
    mask = jnp.tril(jnp.ones((SG_BLOCK, SG_BLOCK), dtype=bool))
    ws = jnp.where(mask[None], w_s, jnp.zeros_like(w_s))
    z = jnp.einsum('gts,bnsgc->bntgc', ws, v) + b_s[:, :, None]
    return u * z.reshape(B, S, SG_WIDTH)


def _token_mixer(h, cos, sin, w_in, w_pool, s_pool, g_q, w_uq, g_kv, w_ukv,
                 g_sg, b_sg, w_s, b_s, w_branch, w_o):
    B, S, _ = h.shape
    p = h @ w_in
    gates, a, cq, ckv, kr, uv = jnp.split(p, [OFF_POOL, OFF_CQ, OFF_CKV, OFF_KR, OFF_SG], axis=-1)
    gates = jax.nn.sigmoid(gates).reshape(B, S, N_BRANCH, D_MODEL)
    y_a = _pool_mixer(a, w_pool, s_pool) @ w_branch[0]
    y_b = _mla(cq, ckv, kr, cos, sin, g_q, w_uq, g_kv, w_ukv) @ w_branch[1]
    y_c = _sgu(uv, g_sg, b_sg, w_s, b_s) @ w_branch[2]
    merged = gates[:, :, 0] * y_a + gates[:, :, 1] * y_b + gates[:, :, 2] * y_c
    return merged @ w_o


def _conv_ffn(h, w_up, conv_w, conv_b, w_down):
    z = h @ w_up
    S = z.shape[1]
    zp = jnp.pad(z, ((0, 0), (CONV_W - 1, 0), (0, 0)))
    zc = conv_b + conv_w[0] * zp[:, 0:S]
    for k in range(1, CONV_W):
        zc = zc + conv_w[k] * zp[:, k:k + S]
    val, gate = jnp.split(zc, 2, axis=-1)
    return (jax.nn.silu(gate) * val) @ w_down


def setup_inputs(seed: int = 0) -> dict:
    key = jax.random.key(seed)
    ks = jax.random.split(key, 26)
    f32 = jnp.float32

    def nrm(k, shape, scale):
        return jax.random.normal(k, shape, f32) * scale

    L = DEPTH
    pos = (jax.random.randint(ks[2], (BATCH, 1), 0, 8192, dtype=jnp.int32)
           + jnp.arange(SEQ, dtype=jnp.int32)[None, :])
    return {
        "x": nrm(ks[0], (BATCH, SEQ, D_MODEL), 1.0),
        "c": nrm(ks[1], (BATCH, D_MODEL), 1.0),
        "pos": pos,
        "w_ada": nrm(ks[3], (L, D_MODEL, 6 * D_MODEL), 0.5 * D_MODEL ** -0.5),
        "b_ada": nrm(ks[4], (L, 6 * D_MODEL), 0.02),
        "w_in": nrm(ks[5], (L, D_MODEL, IN_WIDTH), D_MODEL ** -0.5),
        "w_pool": nrm(ks[6], (L, POOL_GROUPS, POOL_GW, POOL_GW), POOL_GW ** -0.5),
        "s_pool": 1.0 + nrm(ks[7], (L, POOL_WIDTH), 0.05),
        "g_q": 1.0 + nrm(ks[8], (L, Q_LORA), 0.05),
        "w_uq": nrm(ks[9], (L, Q_LORA, MLA_HEADS, QK_NOPE + QK_ROPE), Q_LORA ** -0.5),
        "g_kv": 1.0 + nrm(ks[10], (L, KV_LORA), 0.05),
        "w_ukv": nrm(ks[11], (L, KV_LORA, MLA_HEADS, QK_NOPE + V_DIM), KV_LORA ** -0.5),
        "g_sg": 1.0 + nrm(ks[12], (L, SG_WIDTH), 0.05),
        "b_sg": nrm(ks[13], (L, SG_WIDTH), 0.02),
        "w_s": nrm(ks[14], (L, SG_GROUPS, SG_BLOCK, SG_BLOCK), SG_BLOCK ** -0.5),
        "b_s": 1.0 + nrm(ks[15], (L, SG_BLOCK, SG_GROUPS), 0.02),
        "w_branch": nrm(ks[16], (L, N_BRANCH, D_MODEL, D_MODEL), D_MODEL ** -0.5),
        "w_o": nrm(ks[17], (L, D_MODEL, D_MODEL), BETA * D_MODEL ** -0.5),
        "ln_t_g": 1.0 + nrm(ks[18], (L, D_MODEL), 0.05),
        "ln_t_b": nrm(ks[19], (L, D_MODEL), 0.02),
        "w_up": nrm(ks[20], (L, D_MODEL, 2 * D_FF), D_MODEL ** -0.5),
        "conv_w": nrm(ks[21], (L, CONV_W, 2 * D_FF), CONV_W ** -0.5),
        "conv_b": nrm(ks[22], (L, 2 * D_FF), 0.02),
        "w_down": nrm(ks[23], (L, D_FF, D_MODEL), BETA * D_FF ** -0.5),
        "ln_f_g": 1.0 + nrm(ks[24], (L, D_MODEL), 0.05),
        "ln_f_b": nrm(ks[25], (L, D_MODEL), 0.02),
    }


def reference(x, c, pos, w_ada, b_ada, w_in, w_pool, s_pool, g_q, w_uq, g_kv, w_ukv,
              g_sg, b_sg, w_s, b_s, w_branch, w_o, ln_t_g, ln_t_b,
              w_up, conv_w, conv_b, w_down, ln_f_g, ln_f_b):
    cos, sin = _rope_tables(pos)
    c_act = jax.nn.silu(c)
    for l in range(DEPTH):
        mod = c_act @ w_ada[l] + b_ada[l]
        sh_t, sc_t, gt_t, sh_f, sc_f, gt_f = jnp.split(mod, 6, axis=-1)
        h = _modulate(x, sh_t, sc_t)
        y = _token_mixer(h, cos, sin, w_in[l], w_pool[l], s_pool[l], g_q[l], w_uq[l],
                         g_kv[l], w_ukv[l], g_sg[l], b_sg[l], w_s[l], b_s[l],
                         w_branch[l], w_o[l])
        x = _layer_norm(ALPHA * x + gt_t[:, None, :] * y) * ln_t_g[l] + ln_t_b[l]
        h = _modulate(x, sh_f, sc_f)
        y = _conv_ffn(h, w_up[l], conv_w[l], conv_b[l], w_down[l])
        x = _layer_norm(ALPHA * x + gt_f[:, None, :] * y) * ln_f_g[l] + ln_f_b[l]
    return x
```

```python
import numpy as np
import concourse.bass as bass
import concourse.mybir as mybir
from concourse.bass_utils import run_bass_kernel_spmd

F32 = mybir.dt.float32
BF16 = mybir.dt.bfloat16
I32 = mybir.dt.int32
AF = mybir.ActivationFunctionType
ALU = mybir.AluOpType

D = 1024
SEQ = 2048
BATCH = 32
DEPTH = 2
NCORES = 8
SPC = BATCH // NCORES
T = 512
NG = SEQ // T
D_FF = 2816
NF = D_FF // 128
OFF_POOL = 3 * D
OFF_CQ = OFF_POOL + D
OFF_CKV = OFF_CQ + 256
OFF_KR = OFF_CKV + 128
OFF_SG = OFF_KR + 64
IN_WIDTH = OFF_SG + 2 * D
ALPHA = (2 * DEPTH) ** 0.25
LN_EPS = 1e-5
RMS_EPS = 1e-6
ATT_SCALE = 192 ** -0.5
POOL_W = (2, 2, 4, 4, 8, 8, 16, 16)

V_SPOOL = 0
V_GQ = 8
V_GKV = 10
V_LNTG = 11
V_LNTB = 19
V_LNFG = 27
V_LNFB = 35
V_CW = 43
V_CB = V_CW + 132
V_BADA = V_CB + 44
NV = V_BADA + 48
C_IDENT = 0
C_MASK = 128
C_FREQ = 256
C_SGN = 257
C_CORR = 258
NCONST = C_CORR + 64

SEM_EPOCH = 8000


class Buf:
    __slots__ = ("w", "r", "name")

    def __init__(self, name=""):
        self.w = {}
        self.r = {}
        self.name = name


class Eng:
    def __init__(self, fw, name, is_pe=False):
        self.fw = fw
        self.name = name
        self.is_pe = is_pe
        self.prog = []
        self.waited = {}
        self.sems = []
        self.count = 0
        self._new_sem()

    def _new_sem(self):
        s = self.fw.new_sem(f"{self.name}_p{len(self.sems)}")
        self.sems.append(s)
        self.sem = s
        self.count = 0


class DmaSem:
    def __init__(self, fw, name):
        self.sem = fw.new_sem(name)
        self.value = 0


class FW:
    def __init__(self, nc, same_engine_sync=True):
        self.nc = nc
        self.same_sync = same_engine_sync
        self.pe = Eng(self, "pe", is_pe=True)
        self.act = Eng(self, "act")
        self.dve = Eng(self, "dve")
        self.pool = Eng(self, "pool")
        self.sp = Eng(self, "sp")
        self.dma_pool = {}
        self.retired = []
        self.n_inst = 0

    def new_sem(self, name):
        return self.nc.alloc_semaphore(name=name)

    def _collect(self, eng, reads, writes):
        need = {}

        def add(d, raw):
            for s, v in d.items():
                if s in eng.sems:
                    if not (raw and self.same_sync and not eng.is_pe):
                        continue
                if need.get(s, 0) < v:
                    need[s] = v

        for b in reads:
            add(b.w, True)
        for b in writes:
            add(b.w, True)
            add(b.r, False)
        waits = []
        for s, v in need.items():
            if eng.waited.get(s, 0) < v:
                eng.waited[s] = v
                waits.append((s, v))
        return waits

    @staticmethod
    def _update(stamp, reads, writes):
        s, v = stamp
        for b in writes:
            b.w = {s: v}
            b.r = {}
        for b in reads:
            if b in writes:
                continue
            b.r[s] = v

    def op(self, eng, fn, reads=(), writes=(), signal=True):
        reads = list(reads)
        writes = list(writes)
        waits = self._collect(eng, reads, writes)
        sem = eng.sem
        stamp = (sem, eng.count + 1)
        self.n_inst += 1

        def emit(h, waits=waits, fn=fn, signal=signal, sem=sem):
            for s, v in waits:
                h.wait_ge(s, v)
            ins = fn(h)
            if signal:
                ins.then_inc(sem, 1)

        eng.prog.append(emit)
        self._update(stamp, reads, writes)
        if signal:
            eng.count += 1
            if eng.count >= SEM_EPOCH:
                eng._new_sem()
        return stamp

    def dma(self, eng, out_ap, in_ap, reads=(), writes=(), dsem=None, **kw):
        reads = list(reads)
        writes = list(writes)
        if dsem is None:
            dsem = self.get_dma_sem(eng)
        waits = self._collect(eng, reads, writes)
        if dsem.value > 0 and eng.waited.get(dsem.sem, 0) < dsem.value:
            eng.waited[dsem.sem] = dsem.value
            waits.append((dsem.sem, dsem.value))
        dsem.value += 16
        stamp = (dsem.sem, dsem.value)
        self.n_inst += 1

        def emit(h, waits=waits, out_ap=out_ap, in_ap=in_ap, kw=kw, sem=dsem.sem):
            for s, v in waits:
                h.wait_ge(s, v)
            h.dma_start(out=out_ap, in_=in_ap, **kw).then_inc(sem, 16)

        eng.prog.append(emit)
        self._update(stamp, reads, writes)
        return stamp

    def get_dma_sem(self, eng, n=8):
        key = eng.name
        if key not in self.dma_pool:
            self.dma_pool[key] = [[DmaSem(self, f"dma_{key}_{i}") for i in range(n)], 0]
        lst = self.dma_pool[key]
        i = lst[1] % n
        d = lst[0][i]
        if d.value >= 2048:
            self.retired.append(d)
            d = DmaSem(self, f"dma_{key}_{i}_{len(self.retired)}")
            lst[0][i] = d
        lst[1] += 1
        return d

    def final_fence(self, eng):
        sems = [d for lst, _ in self.dma_pool.values() for d in lst if d.value] + list(self.retired)

        def emit(h, sems=sems):
            for d in sems:
                h.wait_ge(d.sem, d.value)

        eng.prog.append(emit)

    def emit(self):
        with self.nc.Block() as block:
            @block.tensor
            def _(h):
                for f in self.pe.prog:
                    f(h)

            @block.scalar
            def _(h):
                for f in self.act.prog:
                    f(h)

            @block.vector
            def _(h):
                for f in self.dve.prog:
                    f(h)

            @block.gpsimd
            def _(h):
                for f in self.pool.prog:
                    f(h)

            @block.sync
            def _(h):
                for f in self.sp.prog:
                    f(h)


class Region:
    def __init__(self, nc, name, npages, page_words):
        self.t = nc.alloc_sbuf_tensor(name, [128, npages * page_words], F32)
        self.pw = page_words
        self.np = npages
        self.bufs = [Buf(f"{name}{i}") for i in range(npages)]

    def f32(self, page, n, off=0, parts=128):
        a = page * self.pw + off
        return self.t[0:parts, a:a + n]

    def bf16(self, page, n, off=0, parts=128):
        a = page * self.pw
        words = (off + n + 1) // 2
        v = self.t[0:parts, a:a + words].bitcast(BF16)
        return v[:, off:off + n]

    def b(self, page, npg=1):
        return self.bufs[page:page + npg]


class PsumPool:
    def __init__(self, nc):
        self.t = nc.alloc_psum_tensor("ps", [128, 8, 512], F32)
        self.bufs = [Buf(f"ps{i}") for i in range(8)]
        self.free = {i: i for i in range(8)}
        self.clock = 8

    def get(self):
        b = min(self.free, key=lambda k: self.free[k])
        del self.free[b]
        return b

    def get_pair(self):
        cands = [i for i in (0, 2, 4, 6) if i in self.free and i + 1 in self.free]
        assert cands, "no free psum pair"
        b = min(cands, key=lambda k: max(self.free[k], self.free[k + 1]))
        del self.free[b]
        del self.free[b + 1]
        return b

    def put(self, b):
        self.clock += 1
        self.free[b] = self.clock


def build_program(nseq=SPC, ngroups=NG, nlayers=DEPTH, dbg=None):
    nc = bass.Bass("TRN2", target_bir_lowering=False)
    dt_in = lambda name, shape, dt=F32: nc.dram_tensor(name, list(shape), dt, kind="ExternalInput").ap()
    xT = dt_in("xT", [nseq, D, SEQ])
    cT = dt_in("cT", [128, 8, nseq])
    pos = dt_in("pos", [nseq, SEQ], I32)
    consts = dt_in("consts", [128, NCONST])
    vecs = dt_in("vecs", [DEPTH, 128, NV])
    w_ada = dt_in("w_ada", [DEPTH, D, 6 * D])
    w_in = dt_in("w_in", [DEPTH, D, IN_WIDTH])
    w_krs = dt_in("w_krs", [DEPTH, D, 64])
    w_pool = dt_in("w_pool", [DEPTH, 4, 256, 256])
    w_uqn = dt_in("w_uqn", [DEPTH, 256, 1024])
    w_uqr = dt_in("w_uqr", [DEPTH, 256, 512])
    w_uqrs = dt_in("w_uqrs", [DEPTH, 256, 512])
    w_ukT = dt_in("w_ukT", [DEPTH, 128, 1024])
    w_uv = dt_in("w_uv", [DEPTH, 128, 1024])
    wsT = dt_in("wsT", [DEPTH, 128, 1024])
    bsT = dt_in("bsT", [DEPTH, 1, 1024])
    gsg = dt_in("gsg", [DEPTH, 1, 1024])
    bsg = dt_in("bsg", [DEPTH, 1, 1024])
    w_branch = dt_in("w_branch", [DEPTH, 3, D, D])
    w_o = dt_in("w_o", [DEPTH, D, D])
    w_up = dt_in("w_up", [DEPTH, D, 2 * D_FF])
    w_down = dt_in("w_down", [DEPTH, D_FF, D])
    outT = nc.dram_tensor("outT", [nseq, D, SEQ], F32, kind="ExternalOutput").ap()
    dbg_out = None
    if dbg is not None:
        dbg_out = nc.dram_tensor("dbg", [128, 8, T], F32, kind="ExternalOutput").ap()

    fw = FW(nc)
    pe, act, dve, pool, sp = fw.pe, fw.act, fw.dve, fw.pool, fw.sp
    A = nc.alloc_sbuf_tensor

    x_sb = A("x_sb", [128, 8, T], F32)
    b_x = [Buf(f"x{c}") for c in range(8)]
    h_sb = A("h_sb", [128, 8, T], BF16)
    b_h = [Buf(f"h{c}") for c in range(8)]
    MB = Region(nc, "mb", 12, 512)
    merged = lambda d: MB.f32(d, 512)
    b_merged = lambda d: MB.b(d)
    branch = lambda c, lo=0, hi=T: MB.bf16(8 + c // 2, hi - lo, (c % 2) * 512 + lo)
    b_branch = lambda c: MB.b(8 + c // 2)
    actT = lambda f: MB.bf16(f // 2, 512, (f % 2) * 512)
    b_actT = lambda f: MB.b(f // 2)
    AR = Region(nc, "ar", 14, 544)
    st_mean = A("st_mean", [128, T], F32); b_mean = Buf("mean")
    st_var = A("st_var", [128, T], F32); b_var = Buf("var")
    st_rstd = A("st_rstd", [128, T], F32); b_rstd = Buf("rstd")
    NSLOT = 4
    RING = Region(nc, "ring", NSLOT, 2048)
    ring_i = [0]
    wpool_sb = A("wpool_sb", [128, 2, 4, 256], BF16); b_wpool = Buf("wpool")
    wuqn_sb = A("wuqn_sb", [128, 2, 1024], BF16); b_wuqn = Buf("wuqn")
    wuqr_sb = A("wuqr_sb", [128, 2, 512], BF16); b_wuqr = Buf("wuqr")
    wuqrs_sb = A("wuqrs_sb", [128, 2, 512], BF16); b_wuqrs = Buf("wuqrs")
    wukT_sb = A("wukT_sb", [128, 1024], BF16); b_wukT = Buf("wukT")
    wuv_sb = A("wuv_sb", [128, 1024], BF16); b_wuv = Buf("wuv")
    wsT_sb = A("wsT_sb", [128, 1024], BF16); b_wsT = Buf("wsT")
    bsT_sb = A("bsT_sb", [1, 1024], BF16); b_bsT = Buf("bsT")
    gsg_sb = A("gsg_sb", [128, 1024], F32); b_gsg = Buf("gsg")
    bsg_sb = A("bsg_sb", [128, 1024], F32); b_bsg = Buf("bsg")
    vecs_sb = A("vecs_sb", [128, DEPTH, NV], F32); b_vecs = Buf("vecs")
    mod_sb = A("mod_sb", [128, DEPTH * 48 * nseq], F32); b_mod = Buf("mod")
    c_sb = A("c_sb", [128, 8, nseq], F32); b_c = Buf("c")
    const_sb = A("const_sb", [128, NCONST], F32); b_const = Buf("const")
    ident_bf = A("ident_bf", [128, 128], BF16)
    mask_bf = A("mask_bf", [128, 128], BF16)
    ones_bf = A("ones_bf", [128, 128], BF16)
    onesm_bf = A("onesm_bf", [128, 128], BF16)
    ones_f = A("ones_f", [128, 128], F32)
    eps_ln = A("eps_ln", [128, 1], F32)
    eps_rms = A("eps_rms", [128, 1], F32)
    b_cst2 = Buf("cst2")
    ckvnT = [A(f"ckvnT{l}", [128, SEQ], BF16) for l in range(DEPTH)]
    krotT = [A(f"krotT{l}", [64, SEQ], BF16) for l in range(DEPTH)]
    Vc = [A(f"Vc{l}", [128, SEQ // 128, 128], BF16) for l in range(DEPTH)]
    b_kc = [[Buf(f"kc{l}_{j}") for j in range(NG)] for l in range(DEPTH)]
    b_kr = [[Buf(f"kr{l}_{j}") for j in range(NG)] for l in range(DEPTH)]
    b_vc = [[Buf(f"vc{l}_{j}") for j in range(NG)] for l in range(DEPTH)]
    cos2 = A("cos2", [64, T], F32); b_cos = Buf("cos")
    sinpm = A("sinpm", [64, T], F32); b_sin = Buf("sin")
    cqn = A("cqn", [128, 2, T], BF16); b_cqn = Buf("cqn")
    phalo = [A(f"phalo{l}", [128, 8, 16], F32) for l in range(DEPTH)]
    b_phalo = [Buf(f"phalo{l}") for l in range(DEPTH)]
    zhalo = [A(f"zhalo{l}", [128, 2 * NF, 2], F32) for l in range(DEPTH)]
    b_zhalo = [Buf(f"zhalo{l}") for l in range(DEPTH)]
    PS = PsumPool(nc)

    def ACT(out, in_, func, reads, writes, **kw):
        fw.op(act, lambda h: h.activation(out=out, in_=in_, func=func, **kw), reads, writes)

    def TT(out, in0, in1, op, reads, writes, eng=dve):
        fw.op(eng, lambda h: h.tensor_tensor(out=out, in0=in0, in1=in1, op=op), reads, writes)

    def STT(out, in0, scalar, in1, op0, op1, reads, writes, eng=dve):
        fw.op(eng, lambda h: h.scalar_tensor_tensor(out=out, in0=in0, scalar=scalar, in1=in1, op0=op0, op1=op1),
              reads, writes)

    def TS(out, in0, s1, s2, op0, op1, reads, writes, eng=dve):
        if s2 is None:
            fw.op(eng, lambda h: h.tensor_scalar(out=out, in0=in0, scalar1=s1, scalar2=None, op0=op0), reads, writes)
        else:
            fw.op(eng, lambda h: h.tensor_scalar(out=out, in0=in0, scalar1=s1, scalar2=s2, op0=op0, op1=op1),
                  reads, writes)

    def CP(out, in_, reads, writes, eng=dve):
        fw.op(eng, lambda h: h.tensor_copy(out=out, in_=in_), reads, writes)

    def MM(out, lhsT, rhs, start, stop, reads, writes, signal=None):
        signal = True
        fw.op(pe, lambda h: h.matmul(out, lhsT=lhsT, rhs=rhs, start=start, stop=stop), reads, writes, signal=signal)

    def psb(b, lo=0, hi=T, parts=128):
        return PS.t[0:parts, b, lo:hi]

    def wload(src_ap, ncols_total, view_shape=None, extra=None):
        slot = ring_i[0] % NSLOT
        ring_i[0] += 1
        nel = 1
        for s_ in src_ap.shape[1:]:
            nel *= s_
        dst = RING.bf16(slot, nel)
        if len(src_ap.shape) == 3:
            dst = dst.rearrange("p (k n) -> p k n", k=src_ap.shape[1])
        fw.dma(pool, dst, src_ap, writes=RING.b(slot))
        return slot

    def ring_view(slot, k, n, tot=None):
        tot = tot or n
        v = RING.bf16(slot, k * tot).rearrange("p (k n) -> p k n", k=k)
        return v

    def mcol(l, kind, c, s):
        i = ((l * 6 + kind) * 8 + c) * nseq + s
        return mod_sb[:, i:i + 1]

    def vcol(l, off, c=0):
        return vecs_sb[:, l, off + c:off + c + 1]

    fw.dma(sp, const_sb[:], consts, writes=[b_const])
    fw.dma(sp, vecs_sb[:], vecs.rearrange("l p n -> p l n"), writes=[b_vecs])
    fw.dma(sp, c_sb[:], cT, writes=[b_c])
    CP(ident_bf[:], const_sb[:, C_IDENT:C_IDENT + 128], [b_const], [b_cst2])
    CP(mask_bf[:], const_sb[:, C_MASK:C_MASK + 128], [b_const], [b_cst2])
    fw.op(dve, lambda h: h.memset(ones_bf[:], 1.0), writes=[b_cst2])
    fw.op(dve, lambda h: h.memset(onesm_bf[:], 1.0 / D), writes=[b_cst2])
    fw.op(dve, lambda h: h.memset(ones_f[:], 1.0), writes=[b_cst2])
    fw.op(dve, lambda h: h.memset(eps_ln[:], LN_EPS / (ALPHA * ALPHA)), writes=[b_cst2])
    fw.op(dve, lambda h: h.memset(eps_rms[:], RMS_EPS), writes=[b_cst2])
    eps_ln1 = A("eps_ln1", [128, 1], F32)
    fw.op(dve, lambda h: h.memset(eps_ln1[:], LN_EPS), writes=[b_cst2])

    cact = A("cact", [128, 8, nseq], F32); b_cact = Buf("cact")
    ACT(cact[:], c_sb[:], AF.Silu, [b_c], [b_cact])
    pm = PS.get()
    for l in range(DEPTH):
        for blk in range(24):
            slot = ring_i[0] % NSLOT
            ring_i[0] += 1
            stage = RING.f32(slot, 2048).rearrange("p (k n) -> p k n", k=8)
            fw.dma(sp, stage, w_ada[l].rearrange("(kc p) n -> p kc n", p=128)[:, :, blk * 256:(blk + 1) * 256],
                   writes=RING.b(slot))
            for jj in range(2):
                j = blk * 2 + jj
                col = (l * 48 + j) * nseq
                for kc in range(8):
                    MM(PS.t[:, pm, col:col + nseq], stage[:, kc, jj * 128:(jj + 1) * 128], cact[:, kc, :],
                       kc == 0, kc == 7, RING.b(slot) + [b_cact], [PS.bufs[pm]])
    for l in range(DEPTH):
        n48 = 48 * nseq
        TT(mod_sb[:, l * n48:(l + 1) * n48].rearrange("p (j s) -> p j s", s=nseq),
           PS.t[:, pm, l * n48:(l + 1) * n48].rearrange("p (j s) -> p j s", s=nseq),
           vecs_sb[:, l, V_BADA:V_BADA + 48].unsqueeze(2).to_broadcast([128, 48, nseq]),
           ALU.add, [PS.bufs[pm], b_vecs], [b_mod])
        for kind in (1, 4):
            a0 = (l * 6 + kind) * 8 * nseq
            TS(mod_sb[:, a0:a0 + 8 * nseq], mod_sb[:, a0:a0 + 8 * nseq], 1.0, None, ALU.add, None, [b_mod], [b_mod])
        for kind in (2, 5):
            a0 = (l * 6 + kind) * 8 * nseq
            TS(mod_sb[:, a0:a0 + 8 * nseq], mod_sb[:, a0:a0 + 8 * nseq], 1.0 / ALPHA, None, ALU.mult, None,
               [b_mod], [b_mod])
    PS.put(pm)

    def load_wpool(l):
        for gq in range(4):
            fw.dma(pool, wpool_sb[:, :, gq, :], w_pool[l, gq].rearrange("(i p) d -> p i d", p=128), writes=[b_wpool])

    def load_attw(l):
        fw.dma(pool, wuqn_sb[:], w_uqn[l].rearrange("(i p) n -> p i n", p=128), writes=[b_wuqn])
        fw.dma(pool, wuqr_sb[:], w_uqr[l].rearrange("(i p) n -> p i n", p=128), writes=[b_wuqr])
        fw.dma(pool, wuqrs_sb[:], w_uqrs[l].rearrange("(i p) n -> p i n", p=128), writes=[b_wuqrs])
        fw.dma(pool, wukT_sb[:], w_ukT[l], writes=[b_wukT])
        fw.dma(pool, wuv_sb[:], w_uv[l], writes=[b_wuv])

    def load_sguw(l):
        fw.dma(pool, wsT_sb[:], wsT[l], writes=[b_wsT])
        fw.dma(pool, bsT_sb[:], bsT[l], writes=[b_bsT])
        fw.dma(sp, gsg_sb[:], gsg[l].partition_broadcast(128), writes=[b_gsg])
        fw.dma(sp, bsg_sb[:], bsg[l].partition_broadcast(128), writes=[b_bsg])
        TT(wsT_sb[:].rearrange("p (g t) -> p g t", g=8), wsT_sb[:].rearrange("p (g t) -> p g t", g=8),
           mask_bf[:].unsqueeze(1).to_broadcast([128, 8, 128]), ALU.mult, [b_wsT, b_cst2], [b_wsT])

    def ln_stats(eps_tile):
        xb = lambda c: AR.bf16(c // 2, T, (c % 2) * T)
        xq = lambda c: AR.bf16(4 + c // 2, T, (c % 2) * T)
        pmn = PS.get()
        psq = PS.get()
        for c in range(8):
            ACT(xb(c), x_sb[:, c, :], AF.Copy, [b_x[c]], AR.b(c // 2))
            ACT(xq(c), x_sb[:, c, :], AF.Square, [b_x[c]], AR.b(4 + c // 2))
            MM(psb(pmn), onesm_bf[:], xb(c), c == 0, c == 7, AR.b(c // 2) + [b_cst2], [PS.bufs[pmn]])
            MM(psb(psq), onesm_bf[:], xq(c), c == 0, c == 7, AR.b(4 + c // 2) + [b_cst2], [PS.bufs[psq]])
        CP(st_mean[:], psb(pmn), [PS.bufs[pmn]], [b_mean])
        TT(st_var[:], st_mean[:], st_mean[:], ALU.mult, [b_mean], [b_var])
        TT(st_var[:], psb(psq), st_var[:], ALU.subtract, [PS.bufs[psq], b_var], [b_var])
        PS.put(pmn)
        PS.put(psq)
        ACT(st_var[:], st_var[:], AF.Sqrt, [b_var, b_cst2], [b_var], bias=eps_tile[:, 0:1])
        fw.op(dve, lambda h: h.reciprocal(out=st_rstd[:], in_=st_var[:]), [b_var], [b_rstd])

    def ln_apply(out_fn, out_bufs, scale_fn, bias_fn, extra_reads):
        for c in range(8):
            pg = 8 + (c % 4)
            t = AR.f32(pg, T)
            TT(t, x_sb[:, c, :], st_mean[:], ALU.subtract, [b_x[c], b_mean], AR.b(pg))
            TT(t, t, st_rstd[:], ALU.mult, AR.b(pg) + [b_rstd], AR.b(pg))
            ACT(out_fn(c), t, AF.Identity, AR.b(pg) + extra_reads, out_bufs(c), scale=scale_fn(c), bias=bias_fn(c))

    def ln_mod(l, s, kind):
        ln_stats(eps_ln1)
        ln_apply(lambda c: h_sb[:, c, :], lambda c: [b_h[c]],
                 lambda c: mcol(l, 3 * kind + 1, c, s), lambda c: mcol(l, 3 * kind, c, s), [b_mod])

    def post_ln(l, goff, boff):
        ln_stats(eps_ln)
        ln_apply(lambda c: x_sb[:, c, :], lambda c: [b_x[c]],
                 lambda c: vcol(l, goff, c), lambda c: vcol(l, boff, c), [b_vecs])

    def proj8(w_slot, col0, rhs_fn, rhs_bufs, ncols=512):
        b = PS.get()
        wv = ring_view(w_slot, 8, ncols)
        for k in range(8):
            MM(psb(b), wv[:, k, col0:col0 + 128], rhs_fn(k), k == 0, k == 7,
               RING.b(w_slot) + rhs_bufs(k), [PS.bufs[b]])
        return b

    def w_cols(w_l, c0, n):
        return w_l.rearrange("(kc p) n -> p kc n", p=128)[:, :, c0:c0 + n]

    def branch_out(l, bi, first):
        for half in range(2):
            ws = wload(w_cols(w_branch[l, bi], half * 512, 512), 512)
            gs = wload(w_cols(w_in[l], bi * D + half * 512, 512), 512)
            for dd in range(4):
                d = half * 4 + dd
                by = proj8(ws, dd * 128, lambda k: branch(k), lambda k: b_branch(k))
                bg = proj8(gs, dd * 128, lambda k: h_sb[:, k, :], lambda k: [b_h[k]])
                pg = 12 + (d % 2)
                sig = AR.f32(pg, T)
                ACT(sig, psb(bg), AF.Sigmoid, [PS.bufs[bg]], AR.b(pg))
                PS.put(bg)
                if first:
                    TT(merged(d), psb(by), sig, ALU.mult, [PS.bufs[by]] + AR.b(pg), b_merged(d))
                else:
                    TT(sig, psb(by), sig, ALU.mult, [PS.bufs[by]] + AR.b(pg), AR.b(pg))
                    TT(merged(d), merged(d), sig, ALU.add, b_merged(d) + AR.b(pg), b_merged(d))
                PS.put(by)

    TWO_PI = 2.0 * np.pi
    MAGIC = 12582912.0
    CW1 = 6.28125
    CW2 = float(np.float32(TWO_PI - CW1))
    CW3 = float(TWO_PI - CW1 - np.float64(np.float32(TWO_PI - CW1)))
    PI_LO = 3.1415925

    def rope_tables(s, g):
        t0 = g * T
        pi_ = AR.t[0:64, 0:T].bitcast(I32)
        fw.dma(sp, pi_, pos[s:s + 1, t0:t0 + T].partition_broadcast(64), writes=AR.b(0))
        ang = AR.f32(1, T, parts=64)
        CP(ang, pi_, AR.b(0), AR.b(1))
        TS(ang, ang, const_sb[0:64, C_FREQ:C_FREQ + 1], None, ALU.mult, None, AR.b(1) + [b_const], AR.b(1))
        for which, outt, ob in ((0, sinpm, b_sin), (1, cos2, b_cos)):
            kk = AR.f32(2, T, parts=64)
            r = AR.f32(3, T, parts=64)
            if which == 0:
                TS(kk, ang, float(1.0 / TWO_PI), MAGIC, ALU.mult, ALU.add, AR.b(1), AR.b(2))
            else:
                TS(kk, ang, float(1.0 / TWO_PI), 0.25, ALU.mult, ALU.add, AR.b(1), AR.b(2))
                TS(kk, kk, MAGIC, None, ALU.add, None, AR.b(2), AR.b(2))
            TS(kk, kk, -MAGIC, None, ALU.add, None, AR.b(2), AR.b(2))
            STT(r, kk, -CW1, ang, ALU.mult, ALU.add, AR.b(1) + AR.b(2), AR.b(3))
            STT(r, kk, -CW2, r, ALU.mult, ALU.add, AR.b(2) + AR.b(3), AR.b(3))
            STT(r, kk, -CW3, r, ALU.mult, ALU.add, AR.b(2) + AR.b(3), AR.b(3))
            if which == 1:
                TS(r, r, float(np.pi / 2), None, ALU.add, None, AR.b(3), AR.b(3))
            TS(r, r, PI_LO, -PI_LO, ALU.min, ALU.max, AR.b(3), AR.b(3))
            ACT(outt[:], r, AF.Sin, AR.b(3), [ob])
        TS(sinpm[:], sinpm[:], const_sb[0:64, C_SGN:C_SGN + 1], None, ALU.mult, None, [b_sin, b_const], [b_sin])

    def mixer(l, s, g, nxt_l):
        t0 = g * T
        ln_mod(l, s, 0)

        for half in range(2):
            wa = wload(w_cols(w_in[l], OFF_POOL + half * 512, 512), 512)
            for cc in range(4):
                c = half * 4 + cc
                w = POOL_W[c]
                ba = proj8(wa, cc * 128, lambda k: h_sb[:, k, :], lambda k: [b_h[k]])
                P0, P1, P2 = 0 + 3 * (c % 2), 1 + 3 * (c % 2), 2 + 3 * (c % 2)
                a = AR.f32(P0, 528)
                CP(a[:, 0:16], phalo[l][:, c, :], [b_phalo[l]], AR.b(P0))
                ACT(a[:, 16:528], psb(ba), AF.Copy, [PS.bufs[ba]], AR.b(P0))
                PS.put(ba)
                CP(phalo[l][:, c, :], a[:, 512:528], AR.b(P0), [b_phalo[l]])
                cur, curp = a, P0
                sh = 1
                lo = 1
                others = [P1, P2]
                oi = 0
                while sh < w:
                    np_ = others[oi % 2]
                    oi += 1
                    nt = AR.f32(np_, 528)
                    TT(nt[:, lo:528], cur[:, lo:528], cur[:, lo - sh:528 - sh], ALU.add, AR.b(curp), AR.b(np_))
                    cur, curp = nt, np_
                    sh *= 2
                    lo = lo + sh
                if g == 0:
                    wi = {2: 0, 4: 1, 8: 2, 16: 3}[w]
                    TT(cur[:, 16:32], cur[:, 16:32], const_sb[:, C_CORR + wi * 16:C_CORR + wi * 16 + 16], ALU.mult,
                       AR.b(curp) + [b_const], AR.b(curp))
                STT(branch(c), cur[:, 16:528], 1.0 / w, a[:, 16:528], ALU.mult, ALU.subtract,
                    AR.b(curp) + AR.b(P0), b_branch(c))
        for gq in range(4):
            bm = []
            for j in range(2):
                b = PS.get()
                for i in range(2):
                    MM(psb(b), wpool_sb[:, i, gq, j * 128:(j + 1) * 128], branch(2 * gq + i), i == 0, i == 1,
                       [b_wpool] + b_branch(2 * gq + i), [PS.bufs[b]])
                bm.append(b)
            for j in range(2):
                ACT(branch(2 * gq + j), psb(bm[j]), AF.Identity, [PS.bufs[bm[j]], b_vecs], b_branch(2 * gq + j),
                    scale=vcol(l, V_SPOOL, 2 * gq + j))
                PS.put(bm[j])
        if nxt_l is not None:
            load_wpool(nxt_l)
        branch_out(l, 0, True)

        wq = wload(w_cols(w_in[l], OFF_CQ, 448), 448)
        wqv = RING.bf16(wq, 8 * 448).rearrange("p (k n) -> p k n", k=8)
        wk = wload(w_krs[l].rearrange("(kc p) n -> p kc n", p=128), 64)
        wkv = RING.bf16(wk, 8 * 64).rearrange("p (k n) -> p k n", k=8)
        hb = lambda k: [b_h[k]]
        bq = []
        pms = PS.get()
        for i in range(2):
            b = PS.get()
            for k in range(8):
                MM(psb(b), wqv[:, k, i * 128:(i + 1) * 128], h_sb[:, k, :], k == 0, k == 7, RING.b(wq) + hb(k), [PS.bufs[b]])
            sq = AR.f32(i, T)
            ACT(sq, psb(b), AF.Square, [PS.bufs[b]], AR.b(i))
            MM(psb(pms), ones_f[:], sq, i == 0, i == 1, AR.b(i) + [b_cst2], [PS.bufs[pms]])
            bq.append(b)
        rq = AR.f32(2, T)
        ACT(rq, psb(pms), AF.Sqrt, [PS.bufs[pms], b_cst2], AR.b(2), scale=1.0 / 256, bias=eps_rms[:, 0:1])
        PS.put(pms)
        fw.op(dve, lambda h: h.reciprocal(out=rq, in_=rq), AR.b(2), AR.b(2))
        for i in range(2):
            STT(cqn[:, i, :], psb(bq[i]), vcol(l, V_GQ, i), rq, ALU.mult, ALU.mult,
                [PS.bufs[bq[i]], b_vecs] + AR.b(2), [b_cqn])
            PS.put(bq[i])
        b = PS.get()
        pms = PS.get()
        for k in range(8):
            MM(psb(b), wqv[:, k, 256:384], h_sb[:, k, :], k == 0, k == 7, RING.b(wq) + hb(k), [PS.bufs[b]])
        sq = AR.f32(3, T)
        ACT(sq, psb(b), AF.Square, [PS.bufs[b]], AR.b(3))
        MM(psb(pms), ones_f[:], sq, True, True, AR.b(3) + [b_cst2], [PS.bufs[pms]])
        rk = AR.f32(4, T)
        ACT(rk, psb(pms), AF.Sqrt, [PS.bufs[pms], b_cst2], AR.b(4), scale=1.0 / 128, bias=eps_rms[:, 0:1])
        PS.put(pms)
        fw.op(dve, lambda h: h.reciprocal(out=rk, in_=rk), AR.b(4), AR.b(4))
        STT(ckvnT[l][:, t0:t0 + T], psb(b), vcol(l, V_GKV, 0), rk, ALU.mult, ALU.mult,
            [PS.bufs[b], b_vecs] + AR.b(4), [b_kc[l][g]])
        PS.put(b)
        for tt in range(4):
            b = PS.get()
            tp = PS.t[:, b, 0:64].bitcast(BF16)
            fw.op(pe, lambda h, tp=tp, tt=tt: h.transpose(tp, ckvnT[l][:, t0 + tt * 128:t0 + (tt + 1) * 128], ident_bf[:]),
                  [b_kc[l][g], b_cst2], [PS.bufs[b]])
            CP(Vc[l][:, 4 * g + tt, :], tp, [PS.bufs[b]], [b_vc[l][g]])
            PS.put(b)
        b1 = PS.get()
        b2 = PS.get()
        for k in range(8):
            MM(psb(b1, parts=64), wqv[:, k, 384:448], h_sb[:, k, :], k == 0, k == 7, RING.b(wq) + hb(k), [PS.bufs[b1]])
        for k in range(8):
            MM(psb(b2, parts=64), wkv[:, k, :], h_sb[:, k, :], k == 0, k == 7, RING.b(wk) + hb(k), [PS.bufs[b2]])
        t1 = AR.f32(5, T, parts=64)
        t2 = AR.f32(6, T, parts=64)
        TT(t1, psb(b1, parts=64), cos2[:], ALU.mult, [PS.bufs[b1], b_cos], AR.b(5))
        TT(t2, psb(b2, parts=64), sinpm[:], ALU.mult, [PS.bufs[b2], b_sin], AR.b(6))
        PS.put(b1)
        PS.put(b2)
        TT(krotT[l][:, t0:t0 + T], t1, t2, ALU.add, AR.b(5) + AR.b(6), [b_kr[l][g]])

        ntile = 4 * g + 4
        for hh in range(8):
            par = hh % 2
            b = PS.get()
            for i in range(2):
                MM(psb(b), wuqn_sb[:, i, hh * 128:(hh + 1) * 128], cqn[:, i, :], i == 0, i == 1, [b_wuqn, b_cqn], [PS.bufs[b]])
            qn = AR.bf16(0, T, par * T)
            ACT(qn, psb(b), AF.Copy, [PS.bufs[b]], AR.b(0))
            PS.put(b)
            b = PS.get()
            MM(psb(b), wukT_sb[:, hh * 128:(hh + 1) * 128], qn, True, True, [b_wukT] + AR.b(0), [PS.bufs[b]])
            qp = AR.bf16(1, T, par * T)
            ACT(qp, psb(b), AF.Copy, [PS.bufs[b]], AR.b(1))
            PS.put(b)
            b1 = PS.get()
            b2 = PS.get()
            for i in range(2):
                MM(psb(b1, parts=64), wuqr_sb[:, i, hh * 64:(hh + 1) * 64], cqn[:, i, :], i == 0, i == 1, [b_wuqr, b_cqn], [PS.bufs[b1]])
            for i in range(2):
                MM(psb(b2, parts=64), wuqrs_sb[:, i, hh * 64:(hh + 1) * 64], cqn[:, i, :], i == 0, i == 1, [b_wuqrs, b_cqn], [PS.bufs[b2]])
            t1 = AR.f32(5, T, parts=64)
            t2 = AR.f32(6, T, parts=64)
            TT(t1, psb(b1, parts=64), cos2[:], ALU.mult, [PS.bufs[b1], b_cos], AR.b(5))
            TT(t2, psb(b2, parts=64), sinpm[:], ALU.mult, [PS.bufs[b2], b_sin], AR.b(6))
            PS.put(b1)
            PS.put(b2)
            qr = AR.bf16(2, T, par * T, parts=64)
            TT(qr, t1, t2, ALU.add, AR.b(5) + AR.b(6), AR.b(2))
            po = PS.get()
            pd = PS.get()
            j = 0
            it = 0
            while j < ntile:
                npair = 2 if (j + 1 < 4 * g) else 1
                ppg = 8 + 2 * (it % 2)
                it += 1
                if npair == 2:
                    bs = PS.get_pair()
                else:
                    bs = PS.get()
                for jj in range(npair):
                    jt = j + jj
                    qlo = max(0, jt - 4 * g) * 128
                    kg = jt // 4
                    MM(PS.t[:, bs + jj, qlo:T], ckvnT[l][:, jt * 128:(jt + 1) * 128], qp[:, qlo:T], True, False,
                       [b_kc[l][kg]] + AR.b(1), [PS.bufs[bs + jj]])
                    MM(PS.t[:, bs + jj, qlo:T], krotT[l][:, jt * 128:(jt + 1) * 128], qr[:, qlo:T], False, True,
                       [b_kr[l][kg]] + AR.b(2), [PS.bufs[bs + jj]])
                qlo = max(0, j - 4 * g) * 128 if npair == 1 else 0
                if npair == 2:
                    pT = AR.t[:, ppg * 544:ppg * 544 + 512].bitcast(BF16).rearrange("p (a n) -> p a n", a=2)
                    ACT(pT, PS.t[:, bs:bs + 2, :], AF.Exp, [PS.bufs[bs], PS.bufs[bs + 1]], AR.b(ppg), scale=ATT_SCALE)
                    pTs = [pT[:, 0, :], pT[:, 1, :]]
                else:
                    pT1 = AR.bf16(ppg, T)
                    ACT(pT1[:, qlo:T], PS.t[:, bs, qlo:T], AF.Exp, [PS.bufs[bs]], AR.b(ppg), scale=ATT_SCALE)
                    if j >= 4 * g:
                        fw.op(dve, lambda h, pT1=pT1, qlo=qlo: h.memset(pT1[64:128, qlo:qlo + 64], 0.0), [], AR.b(ppg))
                    pTs = [pT1]
                for jj in range(npair):
                    PS.put(bs + jj)
                for jj in range(npair):
                    jt = j + jj
                    ql = max(0, jt - 4 * g) * 128
                    kg = jt // 4
                    MM(PS.t[:, po, ql:T], Vc[l][:, jt, :], pTs[jj][:, ql:T], jt == 0, jt == ntile - 1,
                       [b_vc[l][kg]] + AR.b(ppg), [PS.bufs[po]], signal=False)
                    MM(PS.t[:, pd, ql:T], ones_bf[:], pTs[jj][:, ql:T], jt == 0, jt == ntile - 1,
                       [b_cst2] + AR.b(ppg), [PS.bufs[pd]], signal=True)
                j += npair
            rden = AR.f32(7, T)
            fw.op(dve, lambda h, rden=rden, pd=pd: h.reciprocal(out=rden, in_=psb(pd)), [PS.bufs[pd]], AR.b(7))
            PS.put(pd)
            op_ = AR.bf16(3, T, par * T)
            TT(op_, psb(po), rden, ALU.mult, [PS.bufs[po]] + AR.b(7), AR.b(3))
            PS.put(po)
            b = PS.get()
            MM(psb(b), wuv_sb[:, hh * 128:(hh + 1) * 128], op_, True, True, [b_wuv] + AR.b(3), [PS.bufs[b]])
            ACT(branch(hh), psb(b), AF.Copy, [PS.bufs[b]], b_branch(hh))
            PS.put(b)
        if nxt_l is not None:
            load_attw(nxt_l)
        branch_out(l, 1, False)

        wv_s = [wload(w_cols(w_in[l], OFF_SG + D + nn * 512, 512), 512) for nn in range(2)]
        vn_all = lambda tb: AR.bf16(4 + tb, 1024)
        for tb in range(4):
            bp = PS.get_pair()
            for nn in range(2):
                wv = ring_view(wv_s[nn], 8, 512)
                for k in range(8):
                    MM(PS.t[:, bp + nn, :], h_sb[:, k, tb * 128:(tb + 1) * 128], wv[:, k, :], k == 0, k == 7,
                       RING.b(wv_s[nn]) + [b_h[k]], [PS.bufs[bp + nn]])
            gpg = 0 + 2 * (tb % 2)
            gv = AR.t[:, gpg * 544:gpg * 544 + 1024]
            ACT(gv.rearrange("p (a n) -> p a n", a=2), PS.t[:, bp:bp + 2, :], AF.Gelu,
                [PS.bufs[bp], PS.bufs[bp + 1]], AR.b(gpg, 2))
            PS.put(bp)
            PS.put(bp + 1)
            stt_ = AR.f32(12, 16)
            for i in range(2):
                fw.op(dve, lambda h, i=i, gv=gv, stt_=stt_: h.bn_stats(out=stt_[:, i * 6:(i + 1) * 6], in_=gv[:, i * 512:(i + 1) * 512]),
                      AR.b(gpg, 2), AR.b(12))
            mvv = AR.f32(12, 2, off=16)
            fw.op(dve, lambda h, mvv=mvv, stt_=stt_: h.bn_aggr(out=mvv, in_=stt_[:, 0:12].rearrange("p (a n) -> p a n", a=2)),
                  AR.b(12), AR.b(12))
            rs = AR.f32(12, 1, off=20)
            ACT(rs, mvv[:, 1:2], AF.Sqrt, AR.b(12) + [b_cst2], AR.b(12), bias=eps_ln1[:, 0:1])
            fw.op(dve, lambda h, rs=rs: h.reciprocal(out=rs, in_=rs), AR.b(12), AR.b(12))
            TS(gv, gv, mvv[:, 0:1], rs, ALU.subtract, ALU.mult, AR.b(gpg, 2) + AR.b(12), AR.b(gpg, 2))
            TT(gv, gv, gsg_sb[:], ALU.mult, AR.b(gpg, 2) + [b_gsg], AR.b(gpg, 2))
            TT(vn_all(tb), gv, bsg_sb[:], ALU.add, AR.b(gpg, 2) + [b_bsg], AR.b(4 + tb))
        for half in range(2):
            wu = wload(w_cols(w_in[l], OFF_SG + half * 512, 512), 512)
            for gg in range(4):
                gq = half * 4 + gg
                bz = PS.get()
                for tb in range(4):
                    MM(PS.t[:, bz, tb * 128:(tb + 1) * 128], vn_all(tb)[:, gq * 128:(gq + 1) * 128],
                       wsT_sb[:, gq * 128:(gq + 1) * 128], True, False, AR.b(4 + tb) + [b_wsT], [PS.bufs[bz]])
                    MM(PS.t[:, bz, tb * 128:(tb + 1) * 128], ones_bf[0:1, :], bsT_sb[0:1, gq * 128:(gq + 1) * 128],
                       False, True, [b_cst2, b_bsT], [PS.bufs[bz]], signal=(tb == 3))
                bu = proj8(wu, gg * 128, lambda k: h_sb[:, k, :], lambda k: [b_h[k]])
                pg = 0 + (gq % 2)
                ug = AR.f32(pg, T)
                ACT(ug, psb(bu), AF.Gelu, [PS.bufs[bu]], AR.b(pg))
                PS.put(bu)
                TT(branch(gq), ug, psb(bz), ALU.mult, AR.b(pg) + [PS.bufs[bz]], b_branch(gq))
                PS.put(bz)
        if nxt_l is not None:
            load_sguw(nxt_l)
        branch_out(l, 2, False)

        for d in range(8):
            CP(branch(d), merged(d), b_merged(d), b_branch(d))
        for half in range(2):
            wo = wload(w_cols(w_o[l], half * 512, 512), 512)
            for dd in range(4):
                d = half * 4 + dd
                by = proj8(wo, dd * 128, lambda k: branch(k), lambda k: b_branch(k))
                STT(x_sb[:, d, :], psb(by), mcol(l, 2, d, s), x_sb[:, d, :], ALU.mult, ALU.add,
                    [PS.bufs[by], b_mod, b_x[d]], [b_x[d]])
                PS.put(by)
        post_ln(l, V_LNTG, V_LNTB)

    def ffn(l, s, g):
        ln_mod(l, s, 1)
        nblk = (NF + 3) // 4
        for fb in range(nblk):
            nfc = min(4, NF - fb * 4)
            wvs = wload(w_cols(w_up[l], fb * 512, nfc * 128), nfc * 128)
            wgs = wload(w_cols(w_up[l], D_FF + fb * 512, nfc * 128), nfc * 128)
            for ff in range(nfc):
                f = fb * 4 + ff
                zc = []
                for which, wsl in ((0, wvs), (1, wgs)):
                    ch = which * NF + f
                    wv = ring_view(wsl, 8, nfc * 128)
                    b = PS.get()
                    for k in range(8):
                        MM(psb(b), wv[:, k, ff * 128:(ff + 1) * 128], h_sb[:, k, :], k == 0, k == 7,
                           RING.b(wsl) + [b_h[k]], [PS.bufs[b]])
                    pz = 0 + 2 * which + 4 * (f % 2)
                    pa = pz + 1
                    zs = AR.f32(pz, 514)
                    acc = AR.f32(pa, T)
                    CP(zs[:, 0:2], zhalo[l][:, ch, :], [b_zhalo[l]], AR.b(pz))
                    ACT(zs[:, 2:514], psb(b), AF.Copy, [PS.bufs[b]], AR.b(pz))
                    ACT(acc, psb(b), AF.Identity, [PS.bufs[b], b_vecs], AR.b(pa),
                        scale=vcol(l, V_CW + 2 * 2 * NF, ch), bias=vcol(l, V_CB, ch))
                    PS.put(b)
                    CP(zhalo[l][:, ch, :], zs[:, 512:514], AR.b(pz), [b_zhalo[l]])
                    STT(acc, zs[:, 1:513], vcol(l, V_CW + 2 * NF, ch), acc, ALU.mult, ALU.add,
                        AR.b(pz) + AR.b(pa) + [b_vecs], AR.b(pa))
                    STT(acc, zs[:, 0:512], vcol(l, V_CW, ch), acc, ALU.mult, ALU.add,
                        AR.b(pz) + AR.b(pa) + [b_vecs], AR.b(pa))
                    zc.append((acc, pa))
                (av, pav), (ag, pag) = zc
                ACT(ag, ag, AF.Silu, AR.b(pag), AR.b(pag))
                TT(actT(f), ag, av, ALU.mult, AR.b(pag) + AR.b(pav), b_actT(f))
        for d in range(8):
            wd = wload(w_down[l].rearrange("(fc p) n -> p fc n", p=128)[:, :, d * 128:(d + 1) * 128], 128)
            wv = RING.bf16(wd, NF * 128).rearrange("p (k n) -> p k n", k=NF)
            b = PS.get()
            for f in range(NF):
                MM(psb(b), wv[:, f, :], actT(f), f == 0, f == NF - 1, RING.b(wd) + b_actT(f), [PS.bufs[b]])
            STT(x_sb[:, d, :], psb(b), mcol(l, 5, d, s), x_sb[:, d, :], ALU.mult, ALU.add,
                [PS.bufs[b], b_mod, b_x[d]], [b_x[d]])
            PS.put(b)
        post_ln(l, V_LNFG, V_LNFB)

    steps = [(s, g, l) for s in range(nseq) for g in range(ngroups) for l in range(nlayers)]
    load_wpool(0)
    load_attw(0)
    load_sguw(0)
    for idx, (s, g, l) in enumerate(steps):
        nxt_l = steps[idx + 1][2] if idx + 1 < len(steps) else None
        if l == 0:
            t0 = g * T
            if g == 0:
                for ll in range(nlayers):
                    fw.op(dve, lambda h, ll=ll: h.memset(phalo[ll][:], 0.0), [], [b_phalo[ll]])
                    fw.op(dve, lambda h, ll=ll: h.memset(zhalo[ll][:], 0.0), [], [b_zhalo[ll]])
            fw.dma(sp, x_sb[:], xT[s].rearrange("(kc p) n -> p kc n", p=128)[:, :, t0:t0 + T], writes=b_x)
            rope_tables(s, g)
        mixer(l, s, g, nxt_l)
        if dbg == (s, g, l, "mix"):
            fw.dma(sp, dbg_out, x_sb[:], reads=b_x)
        ffn(l, s, g)
        if dbg == (s, g, l, "ffn"):
            fw.dma(sp, dbg_out, x_sb[:], reads=b_x)
        if l == nlayers - 1:
            fw.dma(sp, outT[s].rearrange("(kc p) n -> p kc n", p=128)[:, :, t0:t0 + T], x_sb[:], reads=b_x)
    fw.final_fence(sp)
    fw.emit()
    return nc, fw


def _host_prep(inp):
    f = lambda a: np.ascontiguousarray(np.asarray(a, dtype=np.float32))
    L = DEPTH
    w_in = np.asarray(inp["w_in"], np.float32)
    w_uq = np.asarray(inp["w_uq"], np.float32)
    w_ukv = np.asarray(inp["w_ukv"], np.float32)
    perm = np.concatenate([np.arange(32, 64), np.arange(0, 32)])
    shared = {
        "w_ada": f(inp["w_ada"]),
        "w_in": f(w_in),
        "w_krs": f(w_in[:, :, OFF_KR:OFF_KR + 64][:, :, perm]),
        "w_pool": f(inp["w_pool"]),
        "w_uqn": f(w_uq[:, :, :, 0:128].reshape(L, 256, 1024)),
        "w_uqr": f(w_uq[:, :, :, 128:192].reshape(L, 256, 512)),
        "w_uqrs": f(w_uq[:, :, :, 128:192][:, :, :, perm].reshape(L, 256, 512)),
        "w_ukT": f(np.transpose(w_ukv[:, :, :, 0:128], (0, 3, 2, 1)).reshape(L, 128, 1024)),
        "w_uv": f(w_ukv[:, :, :, 128:256].reshape(L, 128, 1024)),
        "wsT": f(np.transpose(np.asarray(inp["w_s"], np.float32), (0, 3, 1, 2)).reshape(L, 128, 1024)),
        "bsT": f(np.transpose(np.asarray(inp["b_s"], np.float32), (0, 2, 1)).reshape(L, 1, 1024)),
        "gsg": f(np.asarray(inp["g_sg"], np.float32).reshape(L, 1, 1024)),
        "bsg": f(np.asarray(inp["b_sg"], np.float32).reshape(L, 1, 1024)),
        "w_branch": f(inp["w_branch"]),
        "w_o": f(inp["w_o"]),
        "w_up": f(inp["w_up"]),
        "w_down": f(inp["w_down"]),
    }
    pp = lambda v, n: np.transpose(np.asarray(v, np.float32).reshape(L, n, 128), (0, 2, 1))
    vecs = np.zeros((L, 128, NV), np.float32)
    vecs[:, :, V_SPOOL:V_SPOOL + 8] = pp(inp["s_pool"], 8)
    vecs[:, :, V_GQ:V_GQ + 2] = pp(inp["g_q"], 2)
    vecs[:, :, V_GKV:V_GKV + 1] = pp(inp["g_kv"], 1)
    vecs[:, :, V_LNTG:V_LNTG + 8] = pp(inp["ln_t_g"], 8)
    vecs[:, :, V_LNTB:V_LNTB + 8] = pp(inp["ln_t_b"], 8)
    vecs[:, :, V_LNFG:V_LNFG + 8] = pp(inp["ln_f_g"], 8)
    vecs[:, :, V_LNFB:V_LNFB + 8] = pp(inp["ln_f_b"], 8)
    cw = np.asarray(inp["conv_w"], np.float32)
    for k in range(3):
        vecs[:, :, V_CW + k * 44:V_CW + (k + 1) * 44] = pp(cw[:, k], 44)
    vecs[:, :, V_CB:V_CB + 44] = pp(inp["conv_b"], 44)
    vecs[:, :, V_BADA:V_BADA + 48] = pp(inp["b_ada"], 48)
    shared["vecs"] = vecs
    consts = np.zeros((128, NCONST), np.float32)
    consts[:, C_IDENT:C_IDENT + 128] = np.eye(128, dtype=np.float32)
    consts[:, C_MASK:C_MASK + 128] = np.triu(np.ones((128, 128), np.float32))
    freqs = (10000.0 ** (-np.arange(0, 64, 2, dtype=np.float32) / 64)).astype(np.float32)
    consts[0:64, C_FREQ] = np.concatenate([freqs, freqs])
    consts[0:32, C_SGN] = -1.0
    consts[32:64, C_SGN] = 1.0
    for wi, w in enumerate((2, 4, 8, 16)):
        tt = np.arange(16)
        consts[:, C_CORR + wi * 16:C_CORR + (wi + 1) * 16] = (w / np.minimum(tt + 1, w)).astype(np.float32)[None, :]
    shared["consts"] = consts
    return shared


_CACHE = {}


def kernel(**inputs):
    x = np.asarray(inputs["x"], np.float32)
    c = np.asarray(inputs["c"], np.float32)
    pos = np.asarray(inputs["pos"], np.int32)
    shared = _host_prep(inputs)
    key = "full"
    if key not in _CACHE:
        _CACHE[key] = build_program()
    nc, _ = _CACHE[key]
    in_maps = []
    for core in range(NCORES):
        b0 = core * SPC
        m = dict(shared)
        m["xT"] = np.ascontiguousarray(np.transpose(x[b0:b0 + SPC], (0, 2, 1)))
        m["cT"] = np.ascontiguousarray(np.transpose(c[b0:b0 + SPC].reshape(SPC, 8, 128), (2, 1, 0)))
        m["pos"] = np.ascontiguousarray(pos[b0:b0 + SPC])
        in_maps.append(m)
    res = run_bass_kernel_spmd(nc, in_maps, core_ids=list(range(NCORES)))
    out = np.empty((BATCH, SEQ, D), np.float32)
    for core in range(NCORES):
        o = res.results[core]["outT"]
        out[core * SPC:(core + 1) * SPC] = np.transpose(o, (0, 2, 1))
    return out
```

```python
import numpy as np
import concourse.bass as bass
import concourse.mybir as mybir
from concourse.bass_utils import run_bass_kernel_spmd

F32 = mybir.dt.float32
BF16 = mybir.dt.bfloat16
I32 = mybir.dt.int32
AF = mybir.ActivationFunctionType
ALU = mybir.AluOpType

D = 1024
SEQ = 2048
BATCH = 32
DEPTH = 2
NCORES = 8
SPC = BATCH // NCORES
T = 512
NG = SEQ // T
D_FF = 2816
NF = D_FF // 128
OFF_POOL = 3 * D
OFF_CQ = OFF_POOL + D
OFF_CKV = OFF_CQ + 256
OFF_KR = OFF_CKV + 128
OFF_SG = OFF_KR + 64
IN_WIDTH = OFF_SG + 2 * D
ALPHA = (2 * DEPTH) ** 0.25
LN_EPS = 1e-5
RMS_EPS = 1e-6
ATT_SCALE = 192 ** -0.5
POOL_W = (2, 2, 4, 4, 8, 8, 16, 16)

V_SPOOL = 0
V_GQ = 8
V_GKV = 10
V_LNTG = 11
V_LNTB = 19
V_LNFG = 27
V_LNFB = 35
V_CW = 43
V_CB = V_CW + 132
V_BADA = V_CB + 44
NV = V_BADA + 48
C_IDENT = 0
C_MASK = 128
C_FREQ = 256
C_SGN = 257
C_CORR = 258
NCONST = C_CORR + 64

SEM_EPOCH = 8000


class Buf:
    __slots__ = ("w", "r", "name")

    def __init__(self, name=""):
        self.w = {}
        self.r = {}
        self.name = name


class Eng:
    def __init__(self, fw, name, is_pe=False):
        self.fw = fw
        self.name = name
        self.is_pe = is_pe
        self.prog = []
        self.waited = {}
        self.sems = []
        self.count = 0
        self._new_sem()

    def _new_sem(self):
        s = self.fw.new_sem(f"{self.name}_p{len(self.sems)}")
        self.sems.append(s)
        self.sem = s
        self.count = 0


class DmaSem:
    def __init__(self, fw, name):
        self.sem = fw.new_sem(name)
        self.value = 0


class FW:
    def __init__(self, nc, same_engine_sync=True):
        self.nc = nc
        self.same_sync = same_engine_sync
        self.pe = Eng(self, "pe", is_pe=True)
        self.act = Eng(self, "act")
        self.dve = Eng(self, "dve")
        self.pool = Eng(self, "pool")
        self.sp = Eng(self, "sp")
        self.dma_pool = {}
        self.retired = []
        self.n_inst = 0
        self.phase = 'setup'
        self.phases = {}

    def new_sem(self, name):
        return self.nc.alloc_semaphore(name=name)

    def _collect(self, eng, reads, writes):
        need = {}

        def add(d, raw):
            for s, v in d.items():
                if s in eng.sems:
                    if not (raw and self.same_sync and not eng.is_pe):
                        continue
                if need.get(s, 0) < v:
                    need[s] = v

        for b in reads:
            add(b.w, True)
        for b in writes:
            add(b.w, True)
            add(b.r, False)
        waits = []
        for s, v in need.items():
            if eng.waited.get(s, 0) < v:
                eng.waited[s] = v
                waits.append((s, v))
        return waits

    @staticmethod
    def _update(stamp, reads, writes):
        s, v = stamp
        for b in writes:
            b.w = {s: v}
            b.r = {}
        for b in reads:
            if b in writes:
                continue
            b.r[s] = v

    def op(self, eng, fn, reads=(), writes=(), signal=True):
        reads = list(reads)
        writes = list(writes)
        waits = self._collect(eng, reads, writes)
        sem = eng.sem
        stamp = (sem, eng.count + 1)
        self.n_inst += 1

        def emit(h, waits=waits, fn=fn, signal=signal, sem=sem):
            for s, v in waits:
                h.wait_ge(s, v)
            ins = fn(h)
            if signal:
                ins.then_inc(sem, 1)

        eng.prog.append(emit)
        self.phases.setdefault(eng.name, []).append(self.phase)
        self._update(stamp, reads, writes)
        if signal:
            eng.count += 1
            if eng.count >= SEM_EPOCH:
                eng._new_sem()
        return stamp

    def dma(self, eng, out_ap, in_ap, reads=(), writes=(), dsem=None, **kw):
        reads = list(reads)
        writes = list(writes)
        if dsem is None:
            dsem = self.get_dma_sem(eng)
        waits = self._collect(eng, reads, writes)
        if dsem.value > 0 and eng.waited.get(dsem.sem, 0) < dsem.value:
            eng.waited[dsem.sem] = dsem.value
            waits.append((dsem.sem, dsem.value))
        dsem.value += 16
        stamp = (dsem.sem, dsem.value)
        self.n_inst += 1

        def emit(h, waits=waits, out_ap=out_ap, in_ap=in_ap, kw=kw, sem=dsem.sem):
            for s, v in waits:
                h.wait_ge(s, v)
            h.dma_start(out=out_ap, in_=in_ap, **kw).then_inc(sem, 16)

        eng.prog.append(emit)
        self._update(stamp, reads, writes)
        return stamp

    def get_dma_sem(self, eng, n=8):
        key = eng.name
        if key not in self.dma_pool:
            self.dma_pool[key] = [[DmaSem(self, f"dma_{key}_{i}") for i in range(n)], 0]
        lst = self.dma_pool[key]
        i = lst[1] % n
        d = lst[0][i]
        if d.value >= 2048:
            self.retired.append(d)
            d = DmaSem(self, f"dma_{key}_{i}_{len(self.retired)}")
            lst[0][i] = d
        lst[1] += 1
        return d

    def final_fence(self, eng):
        sems = [d for lst, _ in self.dma_pool.values() for d in lst if d.value] + list(self.retired)

        def emit(h, sems=sems):
            for d in sems:
                h.wait_ge(d.sem, d.value)

        eng.prog.append(emit)

    def emit(self):
        with self.nc.Block() as block:
            @block.tensor
            def _(h):
                for f in self.pe.prog:
                    f(h)

            @block.scalar
            def _(h):
                for f in self.act.prog:
                    f(h)

            @block.vector
            def _(h):
                for f in self.dve.prog:
                    f(h)

            @block.gpsimd
            def _(h):
                for f in self.pool.prog:
                    f(h)

            @block.sync
            def _(h):
                for f in self.sp.prog:
                    f(h)


class Region:
    def __init__(self, nc, name, npages, page_words):
        self.t = nc.alloc_sbuf_tensor(name, [128, npages * page_words], F32)
        self.pw = page_words
        self.np = npages
        self.bufs = [Buf(f"{name}{i}") for i in range(npages)]

    def f32(self, page, n, off=0, parts=128):
        a = page * self.pw + off
        return self.t[0:parts, a:a + n]

    def bf16(self, page, n, off=0, parts=128):
        a = page * self.pw
        words = (off + n + 1) // 2
        v = self.t[0:parts, a:a + words].bitcast(BF16)
        return v[:, off:off + n]

    def b(self, page, npg=1):
        return self.bufs[page:page + npg]


class PsumPool:
    def __init__(self, nc):
        self.t = nc.alloc_psum_tensor("ps", [128, 8, 512], F32)
        self.bufs = [Buf(f"ps{i}") for i in range(8)]
        self.free = {i: i for i in range(8)}
        self.clock = 8

    def get(self):
        b = min(self.free, key=lambda k: self.free[k])
        del self.free[b]
        return b

    def get_pair(self):
        cands = [i for i in (0, 2, 4, 6) if i in self.free and i + 1 in self.free]
        assert cands, "no free psum pair"
        b = min(cands, key=lambda k: max(self.free[k], self.free[k + 1]))
        del self.free[b]
        del self.free[b + 1]
        return b

    def put(self, b):
        self.clock += 1
        self.free[b] = self.clock


def build_program(nseq=SPC, ngroups=NG, nlayers=DEPTH, dbg=None):
    nc = bass.Bass("TRN2", target_bir_lowering=False)
    dt_in = lambda name, shape, dt=F32: nc.dram_tensor(name, list(shape), dt, kind="ExternalInput").ap()
    xT = dt_in("xT", [nseq, D, SEQ])
    cT = dt_in("cT", [128, 8, nseq])
    pos = dt_in("pos", [nseq, SEQ], I32)
    consts = dt_in("consts", [128, NCONST])
    vecs = dt_in("vecs", [DEPTH, 128, NV])
    w_ada = dt_in("w_ada", [DEPTH, D, 6 * D])
    w_in = dt_in("w_in", [DEPTH, D, IN_WIDTH])
    w_krs = dt_in("w_krs", [DEPTH, D, 64])
    w_pool = dt_in("w_pool", [DEPTH, 4, 256, 256])
    w_uqn = dt_in("w_uqn", [DEPTH, 256, 1024])
    w_uqr = dt_in("w_uqr", [DEPTH, 256, 512])
    w_uqrs = dt_in("w_uqrs", [DEPTH, 256, 512])
    w_ukT = dt_in("w_ukT", [DEPTH, 128, 1024])
    w_uv = dt_in("w_uv", [DEPTH, 128, 1024])
    wsT = dt_in("wsT", [DEPTH, 128, 1024])
    bsT = dt_in("bsT", [DEPTH, 1, 1024])
    gsg = dt_in("gsg", [DEPTH, 1, 1024])
    bsg = dt_in("bsg", [DEPTH, 1, 1024])
    w_branch = dt_in("w_branch", [DEPTH, 3, D, D])
    w_o = dt_in("w_o", [DEPTH, D, D])
    w_up = dt_in("w_up", [DEPTH, D, 2 * D_FF])
    w_down = dt_in("w_down", [DEPTH, D_FF, D])
    outT = nc.dram_tensor("outT", [nseq, D, SEQ], F32, kind="ExternalOutput").ap()
    dbg_out = None
    if dbg is not None:
        dbg_out = nc.dram_tensor("dbg", [128, 8, T], F32, kind="ExternalOutput").ap()

    fw = FW(nc)
    pe, act, dve, pool, sp = fw.pe, fw.act, fw.dve, fw.pool, fw.sp
    A = nc.alloc_sbuf_tensor

    xs_t = [A(f"x_sb{i}", [128, 8, T], F32) for i in range(2)]
    xs_b = [[Buf(f"x{i}_{c}") for c in range(8)] for i in range(2)]
    XC = [xs_t[0], xs_b[0]]
    h_sb = A("h_sb", [128, 8, T], BF16)
    b_h = [Buf(f"h{c}") for c in range(8)]
    MB = Region(nc, "mb", 12, 512)
    merged = lambda d: MB.f32(d, 512)
    b_merged = lambda d: MB.b(d)
    branch = lambda c, lo=0, hi=T: MB.bf16(8 + c // 2, hi - lo, (c % 2) * 512 + lo)
    b_branch = lambda c: MB.b(8 + c // 2)
    actT = lambda f: MB.bf16(f // 2, 512, (f % 2) * 512)
    b_actT = lambda f: MB.b(f // 2)
    AR = Region(nc, "ar", 14, 544)
    st_mean = A("st_mean", [128, T], F32); b_mean = Buf("mean")
    st_var = A("st_var", [128, T], F32); b_var = Buf("var")
    st_rstd = A("st_rstd", [128, T], F32); b_rstd = Buf("rstd")
    NSLOT = 5
    RING = Region(nc, "ring", NSLOT, 2048)
    ring_i = [0]
    wpool_sb = A("wpool_sb", [128, 2, 4, 256], BF16); b_wpool = Buf("wpool")
    wuqn_sb = A("wuqn_sb", [128, 2, 1024], BF16); b_wuqn = Buf("wuqn")
    wuqr_sb = A("wuqr_sb", [128, 2, 512], BF16); b_wuqr = Buf("wuqr")
    wuqrs_sb = A("wuqrs_sb", [128, 2, 512], BF16); b_wuqrs = Buf("wuqrs")
    wukT_sb = A("wukT_sb", [128, 1024], BF16); b_wukT = Buf("wukT")
    wuv_sb = A("wuv_sb", [128, 1024], BF16); b_wuv = Buf("wuv")
    wsT_sb = A("wsT_sb", [128, 1024], BF16); b_wsT = Buf("wsT")
    bsT_sb = A("bsT_sb", [1, 1024], BF16); b_bsT = Buf("bsT")
    gsg_sb = A("gsg_sb", [128, 1024], F32); b_gsg = Buf("gsg")
    bsg_sb = A("bsg_sb", [128, 1024], F32); b_bsg = Buf("bsg")
    vecs_sb = A("vecs_sb", [128, DEPTH, NV], F32); b_vecs = Buf("vecs")
    mod_sb = A("mod_sb", [128, DEPTH * 48 * nseq], F32); b_mod = Buf("mod")
    c_sb = A("c_sb", [128, 8, nseq], F32); b_c = Buf("c")
    const_sb = A("const_sb", [128, NCONST], F32); b_const = Buf("const")
    ident_bf = A("ident_bf", [128, 128], BF16)
    mask_bf = A("mask_bf", [128, 128], BF16)
    ones_bf = A("ones_bf", [128, 128], BF16)
    onesm_bf = A("onesm_bf", [128, 128], BF16)
    ones_f = A("ones_f", [128, 128], F32)
    eps_ln = A("eps_ln", [128, 1], F32)
    eps_rms = A("eps_rms", [128, 1], F32)
    b_cst2 = Buf("cst2")
    ckvnT = [A(f"ckvnT{l}", [128, SEQ], BF16) for l in range(DEPTH)]
    krotT = [A(f"krotT{l}", [64, SEQ], BF16) for l in range(DEPTH)]
    Vc = [A(f"Vc{l}", [128, SEQ // 128, 128], BF16) for l in range(DEPTH)]
    b_kc = [[Buf(f"kc{l}_{j}") for j in range(NG)] for l in range(DEPTH)]
    b_kr = [[Buf(f"kr{l}_{j}") for j in range(NG)] for l in range(DEPTH)]
    b_vc = [[Buf(f"vc{l}_{j}") for j in range(NG)] for l in range(DEPTH)]
    cos2 = A("cos2", [64, T], F32); b_cos = Buf("cos")
    sinpm = A("sinpm", [64, T], F32); b_sin = Buf("sin")
    cqn = A("cqn", [128, 2, T], BF16); b_cqn = Buf("cqn")
    phalo = [A(f"phalo{l}", [128, 8, 16], F32) for l in range(DEPTH)]
    b_phalo = [Buf(f"phalo{l}") for l in range(DEPTH)]
    zhalo = [A(f"zhalo{l}", [128, 2 * NF, 2], F32) for l in range(DEPTH)]
    b_zhalo = [[Buf(f"zhalo{l}_{c}") for c in range(2 * NF)] for l in range(DEPTH)]
    PS = PsumPool(nc)

    def ACT(out, in_, func, reads, writes, **kw):
        fw.op(act, lambda h: h.activation(out=out, in_=in_, func=func, **kw), reads, writes)

    def TT(out, in0, in1, op, reads, writes, eng=dve):
        fw.op(eng, lambda h: h.tensor_tensor(out=out, in0=in0, in1=in1, op=op), reads, writes)

    def STT(out, in0, scalar, in1, op0, op1, reads, writes, eng=dve):
        fw.op(eng, lambda h: h.scalar_tensor_tensor(out=out, in0=in0, scalar=scalar, in1=in1, op0=op0, op1=op1),
              reads, writes)

    def TS(out, in0, s1, s2, op0, op1, reads, writes, eng=dve):
        if s2 is None:
            fw.op(eng, lambda h: h.tensor_scalar(out=out, in0=in0, scalar1=s1, scalar2=None, op0=op0), reads, writes)
        else:
            fw.op(eng, lambda h: h.tensor_scalar(out=out, in0=in0, scalar1=s1, scalar2=s2, op0=op0, op1=op1),
                  reads, writes)

    def CP(out, in_, reads, writes, eng=dve):
        fw.op(eng, lambda h: h.tensor_copy(out=out, in_=in_), reads, writes)

    def MM(out, lhsT, rhs, start, stop, reads, writes, signal=None):
        signal = True
        fw.op(pe, lambda h: h.matmul(out, lhsT=lhsT, rhs=rhs, start=start, stop=stop), reads, writes, signal=signal)

    def psb(b, lo=0, hi=T, parts=128):
        return PS.t[0:parts, b, lo:hi]

    SCR = {}

    def mk_scratch(key, src2d):
        rows, cols = src2d.shape
        t = nc.dram_tensor("scr_" + "_".join(str(k) for k in key), [rows, cols], BF16, kind="Internal").ap()
        bufs = []
        for r0 in range(0, rows, 128):
            b = Buf("scr")
            fw.dma(pool, t[r0:r0 + 128, :], src2d[r0:r0 + 128, :], writes=[b])
            bufs.append(b)
        SCR[key] = (t, bufs)

    def scols(key, c0, n):
        ap, bufs = SCR[key]
        return ap.rearrange("(kc p) n -> p kc n", p=128)[:, :, c0:c0 + n], bufs

    def wload(src, eng=None):
        src_ap, sbufs = src
        slot = ring_i[0] % NSLOT
        ring_i[0] += 1
        nel = 1
        for s_ in src_ap.shape[1:]:
            nel *= s_
        dst = RING.bf16(slot, nel).rearrange("p (k n) -> p k n", k=src_ap.shape[1])
        fw.dma(eng or sp, dst, src_ap, reads=sbufs, writes=RING.b(slot))
        return slot

    def ring_view(slot, k, n, tot=None):
        tot = tot or n
        v = RING.bf16(slot, k * tot).rearrange("p (k n) -> p k n", k=k)
        return v

    def mcol(l, kind, c, s):
        i = ((l * 6 + kind) * 8 + c) * nseq + s
        return mod_sb[:, i:i + 1]

    def vcol(l, off, c=0):
        return vecs_sb[:, l, off + c:off + c + 1]

    fw.dma(sp, const_sb[:], consts, writes=[b_const])
    fw.dma(sp, vecs_sb[:], vecs.rearrange("l p n -> p l n"), writes=[b_vecs])
    fw.dma(sp, c_sb[:], cT, writes=[b_c])
    CP(ident_bf[:], const_sb[:, C_IDENT:C_IDENT + 128], [b_const], [b_cst2])
    CP(mask_bf[:], const_sb[:, C_MASK:C_MASK + 128], [b_const], [b_cst2])
    fw.op(dve, lambda h: h.memset(ones_bf[:], 1.0), writes=[b_cst2])
    fw.op(dve, lambda h: h.memset(onesm_bf[:], 1.0 / D), writes=[b_cst2])
    fw.op(dve, lambda h: h.memset(ones_f[:], 1.0), writes=[b_cst2])
    fw.op(dve, lambda h: h.memset(eps_ln[:], LN_EPS / (ALPHA * ALPHA)), writes=[b_cst2])
    fw.op(dve, lambda h: h.memset(eps_rms[:], RMS_EPS), writes=[b_cst2])
    eps_ln1 = A("eps_ln1", [128, 1], F32)
    fw.op(dve, lambda h: h.memset(eps_ln1[:], LN_EPS), writes=[b_cst2])

    for l_ in range(nlayers):
        mk_scratch(("w_in", l_), w_in[l_])
        for bi_ in range(3):
            mk_scratch(("w_br", l_, bi_), w_branch[l_, bi_])
        mk_scratch(("w_o", l_), w_o[l_])
        mk_scratch(("w_up", l_), w_up[l_])
        mk_scratch(("w_down", l_), w_down[l_])

    cact = A("cact", [128, 8, nseq], F32); b_cact = Buf("cact")
    ACT(cact[:], c_sb[:], AF.Silu, [b_c], [b_cact])
    pm = PS.get()
    for l in range(DEPTH):
        for blk in range(24):
            slot = ring_i[0] % NSLOT
            ring_i[0] += 1
            stage = RING.f32(slot, 2048).rearrange("p (k n) -> p k n", k=8)
            fw.dma(sp, stage, w_ada[l].rearrange("(kc p) n -> p kc n", p=128)[:, :, blk * 256:(blk + 1) * 256],
                   writes=RING.b(slot))
            for jj in range(2):
                j = blk * 2 + jj
                col = (l * 48 + j) * nseq
                for kc in range(8):
                    MM(PS.t[:, pm, col:col + nseq], stage[:, kc, jj * 128:(jj + 1) * 128], cact[:, kc, :],
                       kc == 0, kc == 7, RING.b(slot) + [b_cact], [PS.bufs[pm]])
    for l in range(DEPTH):
        n48 = 48 * nseq
        TT(mod_sb[:, l * n48:(l + 1) * n48].rearrange("p (j s) -> p j s", s=nseq),
           PS.t[:, pm, l * n48:(l + 1) * n48].rearrange("p (j s) -> p j s", s=nseq),
           vecs_sb[:, l, V_BADA:V_BADA + 48].unsqueeze(2).to_broadcast([128, 48, nseq]),
           ALU.add, [PS.bufs[pm], b_vecs], [b_mod])
        for kind in (1, 4):
            a0 = (l * 6 + kind) * 8 * nseq
            TS(mod_sb[:, a0:a0 + 8 * nseq], mod_sb[:, a0:a0 + 8 * nseq], 1.0, None, ALU.add, None, [b_mod], [b_mod])
        for kind in (2, 5):
            a0 = (l * 6 + kind) * 8 * nseq
            TS(mod_sb[:, a0:a0 + 8 * nseq], mod_sb[:, a0:a0 + 8 * nseq], 1.0 / ALPHA, None, ALU.mult, None,
               [b_mod], [b_mod])
    PS.put(pm)

    def load_wpool(l):
        for gq in range(4):
            fw.dma(pool, wpool_sb[:, :, gq, :], w_pool[l, gq].rearrange("(i p) d -> p i d", p=128), writes=[b_wpool])

    def load_attw(l):
        fw.dma(pool, wuqn_sb[:], w_uqn[l].rearrange("(i p) n -> p i n", p=128), writes=[b_wuqn])
        fw.dma(pool, wuqr_sb[:], w_uqr[l].rearrange("(i p) n -> p i n", p=128), writes=[b_wuqr])
        fw.dma(pool, wuqrs_sb[:], w_uqrs[l].rearrange("(i p) n -> p i n", p=128), writes=[b_wuqrs])
        fw.dma(pool, wukT_sb[:], w_ukT[l], writes=[b_wukT])
        fw.dma(pool, wuv_sb[:], w_uv[l], writes=[b_wuv])

    def load_sguw(l):
        fw.dma(pool, wsT_sb[:], wsT[l], writes=[b_wsT])
        fw.dma(pool, bsT_sb[:], bsT[l], writes=[b_bsT])
        fw.dma(pool, gsg_sb[:], gsg[l].partition_broadcast(128), writes=[b_gsg])
        fw.dma(pool, bsg_sb[:], bsg[l].partition_broadcast(128), writes=[b_bsg])
        TT(wsT_sb[:].rearrange("p (g t) -> p g t", g=8), wsT_sb[:].rearrange("p (g t) -> p g t", g=8),
           mask_bf[:].unsqueeze(1).to_broadcast([128, 8, 128]), ALU.mult, [b_wsT, b_cst2], [b_wsT])

    def ln_stats(eps_tile):
        xb = lambda c: AR.bf16(c // 2, T, (c % 2) * T)
        xq = lambda c: AR.bf16(4 + c // 2, T, (c % 2) * T)
        pmn = PS.get()
        psq = PS.get()
        for pp in range(4):
            xb2 = AR.bf16(pp, 2 * T).rearrange("p (a n) -> p a n", a=2)
            xq2 = AR.bf16(4 + pp, 2 * T).rearrange("p (a n) -> p a n", a=2)
            xin = XC[0][:, 2 * pp:2 * pp + 2, :]
            rb = [XC[1][2 * pp], XC[1][2 * pp + 1]]
            fw.op(pool, lambda h, xb2=xb2, xin=xin: h.tensor_copy(out=xb2, in_=xin), rb, AR.b(pp))
            ACT(xq2, xin, AF.Square, rb, AR.b(4 + pp))
            for c in (2 * pp, 2 * pp + 1):
                MM(psb(pmn), onesm_bf[:], xb(c), c == 0, c == 7, AR.b(c // 2) + [b_cst2], [PS.bufs[pmn]])
                MM(psb(psq), onesm_bf[:], xq(c), c == 0, c == 7, AR.b(4 + c // 2) + [b_cst2], [PS.bufs[psq]])
        CP(st_mean[:], psb(pmn), [PS.bufs[pmn]], [b_mean])
        TT(st_var[:], st_mean[:], st_mean[:], ALU.mult, [b_mean], [b_var])
        TT(st_var[:], psb(psq), st_var[:], ALU.subtract, [PS.bufs[psq], b_var], [b_var])
        PS.put(pmn)
        PS.put(psq)
        ACT(st_var[:], st_var[:], AF.Sqrt, [b_var, b_cst2], [b_var], bias=eps_tile[:, 0:1])
        fw.op(dve, lambda h: h.reciprocal(out=st_rstd[:], in_=st_var[:]), [b_var], [b_rstd])

    def ln_apply(out_fn, out_bufs, scale_fn, bias_fn, extra_reads):
        for c in range(8):
            pg = 8 + (c % 4)
            t = AR.f32(pg, T)
            TT(t, XC[0][:, c, :], st_mean[:], ALU.subtract, [XC[1][c], b_mean], AR.b(pg), eng=pool)
            TT(t, t, st_rstd[:], ALU.mult, AR.b(pg) + [b_rstd], AR.b(pg))
            ACT(out_fn(c), t, AF.Identity, AR.b(pg) + extra_reads, out_bufs(c), scale=scale_fn(c), bias=bias_fn(c))

    def ln_mod(l, s, kind):
        ln_stats(eps_ln1)
        ln_apply(lambda c: h_sb[:, c, :], lambda c: [b_h[c]],
                 lambda c: mcol(l, 3 * kind + 1, c, s), lambda c: mcol(l, 3 * kind, c, s), [b_mod])

    def post_ln(l, goff, boff):
        ln_stats(eps_ln)
        ln_apply(lambda c: XC[0][:, c, :], lambda c: [XC[1][c]],
                 lambda c: vcol(l, goff, c), lambda c: vcol(l, boff, c), [b_vecs])

    def proj8(w_slot, col0, rhs_fn, rhs_bufs, ncols=512):
        b = PS.get()
        wv = ring_view(w_slot, 8, ncols)
        for k in range(8):
            MM(psb(b), wv[:, k, col0:col0 + 128], rhs_fn(k), k == 0, k == 7,
               RING.b(w_slot) + rhs_bufs(k), [PS.bufs[b]])
        return b

    def w_cols(w_l, c0, n):
        return w_l.rearrange("(kc p) n -> p kc n", p=128)[:, :, c0:c0 + n]

    def branch_out(l, bi, first):
        for half in range(2):
            ws = wload(scols(("w_br", l, bi), half * 512, 512))
            gs = wload(scols(("w_in", l), bi * D + half * 512, 512))
            for dd in range(4):
                d = half * 4 + dd
                by = proj8(ws, dd * 128, lambda k: branch(k), lambda k: b_branch(k))
                bg = proj8(gs, dd * 128, lambda k: h_sb[:, k, :], lambda k: [b_h[k]])
                pg = 12 + (d % 2)
                sig = AR.f32(pg, T)
                ACT(sig, psb(bg), AF.Sigmoid, [PS.bufs[bg]], AR.b(pg))
                PS.put(bg)
                if first:
                    TT(merged(d), psb(by), sig, ALU.mult, [PS.bufs[by]] + AR.b(pg), b_merged(d))
                else:
                    TT(sig, psb(by), sig, ALU.mult, [PS.bufs[by]] + AR.b(pg), AR.b(pg))
                    TT(merged(d), merged(d), sig, ALU.add, b_merged(d) + AR.b(pg), b_merged(d))
                PS.put(by)

    TWO_PI = 2.0 * np.pi
    MAGIC = 12582912.0
    CW1 = 6.28125
    CW2 = float(np.float32(TWO_PI - CW1))
    CW3 = float(TWO_PI - CW1 - np.float64(np.float32(TWO_PI - CW1)))
    PI_LO = 3.1415925

    def rope_tables(s, g):
        t0 = g * T
        pi_ = AR.t[0:64, 0:T].bitcast(I32)
        fw.dma(pool, pi_, pos[s:s + 1, t0:t0 + T].partition_broadcast(64), writes=AR.b(0))
        ang = AR.f32(1, T, parts=64)
        CP(ang, pi_, AR.b(0), AR.b(1))
        TS(ang, ang, const_sb[0:64, C_FREQ:C_FREQ + 1], None, ALU.mult, None, AR.b(1) + [b_const], AR.b(1))
        for which, outt, ob in ((0, sinpm, b_sin), (1, cos2, b_cos)):
            kk = AR.f32(2, T, parts=64)
            r = AR.f32(3, T, parts=64)
            if which == 0:
                TS(kk, ang, float(1.0 / TWO_PI), MAGIC, ALU.mult, ALU.add, AR.b(1), AR.b(2))
            else:
                TS(kk, ang, float(1.0 / TWO_PI), 0.25, ALU.mult, ALU.add, AR.b(1), AR.b(2))
                TS(kk, kk, MAGIC, None, ALU.add, None, AR.b(2), AR.b(2))
            TS(kk, kk, -MAGIC, None, ALU.add, None, AR.b(2), AR.b(2))
            STT(r, kk, -CW1, ang, ALU.mult, ALU.add, AR.b(1) + AR.b(2), AR.b(3))
            STT(r, kk, -CW2, r, ALU.mult, ALU.add, AR.b(2) + AR.b(3), AR.b(3))
            STT(r, kk, -CW3, r, ALU.mult, ALU.add, AR.b(2) + AR.b(3), AR.b(3))
            if which == 1:
                TS(r, r, float(np.pi / 2), None, ALU.add, None, AR.b(3), AR.b(3))
            TS(r, r, PI_LO, -PI_LO, ALU.min, ALU.max, AR.b(3), AR.b(3))
            ACT(outt[:], r, AF.Sin, AR.b(3), [ob])
        TS(sinpm[:], sinpm[:], const_sb[0:64, C_SGN:C_SGN + 1], None, ALU.mult, None, [b_sin, b_const], [b_sin])

    def mixer(l, s, g, nxt_l):
        t0 = g * T
        fw.phase = 'ln_mod_t'
        ln_mod(l, s, 0)
        fw.phase = 'pool'

        for half in range(2):
            wa = wload(scols(("w_in", l), OFF_POOL + half * 512, 512))
            for cc in range(4):
                c = half * 4 + cc
                w = POOL_W[c]
                ba = proj8(wa, cc * 128, lambda k: h_sb[:, k, :], lambda k: [b_h[k]])
                P0, P1, P2 = 0 + 3 * (c % 2), 1 + 3 * (c % 2), 2 + 3 * (c % 2)
                a = AR.f32(P0, 528)
                CP(a[:, 0:16], phalo[l][:, c, :], [b_phalo[l]], AR.b(P0))
                ACT(a[:, 16:528], psb(ba), AF.Copy, [PS.bufs[ba]], AR.b(P0))
                PS.put(ba)
                CP(phalo[l][:, c, :], a[:, 512:528], AR.b(P0), [b_phalo[l]])
                cur, curp = a, P0
                sh = 1
                lo = 1
                others = [P1, P2]
                oi = 0
                while sh < w:
                    np_ = others[oi % 2]
                    oi += 1
                    nt = AR.f32(np_, 528)
                    TT(nt[:, lo:528], cur[:, lo:528], cur[:, lo - sh:528 - sh], ALU.add, AR.b(curp), AR.b(np_))
                    cur, curp = nt, np_
                    sh *= 2
                    lo = lo + sh
                if g == 0:
                    wi = {2: 0, 4: 1, 8: 2, 16: 3}[w]
                    TT(cur[:, 16:32], cur[:, 16:32], const_sb[:, C_CORR + wi * 16:C_CORR + wi * 16 + 16], ALU.mult,
                       AR.b(curp) + [b_const], AR.b(curp))
                STT(branch(c), cur[:, 16:528], 1.0 / w, a[:, 16:528], ALU.mult, ALU.subtract,
                    AR.b(curp) + AR.b(P0), b_branch(c))
        for gq in range(4):
            bm = []
            for j in range(2):
                b = PS.get()
                for i in range(2):
                    MM(psb(b), wpool_sb[:, i, gq, j * 128:(j + 1) * 128], branch(2 * gq + i), i == 0, i == 1,
                       [b_wpool] + b_branch(2 * gq + i), [PS.bufs[b]])
                bm.append(b)
            for j in range(2):
                ACT(branch(2 * gq + j), psb(bm[j]), AF.Identity, [PS.bufs[bm[j]], b_vecs], b_branch(2 * gq + j),
                    scale=vcol(l, V_SPOOL, 2 * gq + j))
                PS.put(bm[j])
        if nxt_l is not None:
            load_wpool(nxt_l)
        fw.phase = 'bout_a'
        branch_out(l, 0, True)
        fw.phase = 'mla_proj'
        if l == 0:
            rope_tables(s, g)

        wq = wload(scols(("w_in", l), OFF_CQ, 448))
        wqv = RING.bf16(wq, 8 * 448).rearrange("p (k n) -> p k n", k=8)
        wk = wload((w_krs[l].rearrange("(kc p) n -> p kc n", p=128), []), eng=pool)
        wkv = RING.bf16(wk, 8 * 64).rearrange("p (k n) -> p k n", k=8)
        hb = lambda k: [b_h[k]]
        bq = []
        pms = PS.get()
        for i in range(2):
            b = PS.get()
            for k in range(8):
                MM(psb(b), wqv[:, k, i * 128:(i + 1) * 128], h_sb[:, k, :], k == 0, k == 7, RING.b(wq) + hb(k), [PS.bufs[b]])
            sq = AR.bf16(i, T)
            ACT(sq, psb(b), AF.Square, [PS.bufs[b]], AR.b(i))
            MM(psb(pms), ones_bf[:], sq, i == 0, i == 1, AR.b(i) + [b_cst2], [PS.bufs[pms]])
            bq.append(b)
        rq = AR.f32(2, T)
        ACT(rq, psb(pms), AF.Sqrt, [PS.bufs[pms], b_cst2], AR.b(2), scale=1.0 / 256, bias=eps_rms[:, 0:1])
        PS.put(pms)
        fw.op(dve, lambda h: h.reciprocal(out=rq, in_=rq), AR.b(2), AR.b(2))
        for i in range(2):
            STT(cqn[:, i, :], psb(bq[i]), vcol(l, V_GQ, i), rq, ALU.mult, ALU.mult,
                [PS.bufs[bq[i]], b_vecs] + AR.b(2), [b_cqn])
            PS.put(bq[i])
        b = PS.get()
        pms = PS.get()
        for k in range(8):
            MM(psb(b), wqv[:, k, 256:384], h_sb[:, k, :], k == 0, k == 7, RING.b(wq) + hb(k), [PS.bufs[b]])
        sq = AR.bf16(3, T)
        ACT(sq, psb(b), AF.Square, [PS.bufs[b]], AR.b(3))
        MM(psb(pms), ones_bf[:], sq, True, True, AR.b(3) + [b_cst2], [PS.bufs[pms]])
        rk = AR.f32(4, T)
        ACT(rk, psb(pms), AF.Sqrt, [PS.bufs[pms], b_cst2], AR.b(4), scale=1.0 / 128, bias=eps_rms[:, 0:1])
        PS.put(pms)
        fw.op(dve, lambda h: h.reciprocal(out=rk, in_=rk), AR.b(4), AR.b(4))
        STT(ckvnT[l][:, t0:t0 + T], psb(b), vcol(l, V_GKV, 0), rk, ALU.mult, ALU.mult,
            [PS.bufs[b], b_vecs] + AR.b(4), [b_kc[l][g]])
        PS.put(b)
        for tt in range(4):
            b = PS.get()
            tp = PS.t[:, b, 0:64].bitcast(BF16)
            fw.op(pe, lambda h, tp=tp, tt=tt: h.transpose(tp, ckvnT[l][:, t0 + tt * 128:t0 + (tt + 1) * 128], ident_bf[:]),
                  [b_kc[l][g], b_cst2], [PS.bufs[b]])
            CP(Vc[l][:, 4 * g + tt, :], tp, [PS.bufs[b]], [b_vc[l][g]])
            PS.put(b)
        b1 = PS.get()
        b2 = PS.get()
        for k in range(8):
            MM(psb(b1, parts=64), wqv[:, k, 384:448], h_sb[:, k, :], k == 0, k == 7, RING.b(wq) + hb(k), [PS.bufs[b1]])
        for k in range(8):
            MM(psb(b2, parts=64), wkv[:, k, :], h_sb[:, k, :], k == 0, k == 7, RING.b(wk) + hb(k), [PS.bufs[b2]])
        t1 = AR.f32(5, T, parts=64)
        t2 = AR.f32(6, T, parts=64)
        TT(t1, psb(b1, parts=64), cos2[:], ALU.mult, [PS.bufs[b1], b_cos], AR.b(5))
        TT(t2, psb(b2, parts=64), sinpm[:], ALU.mult, [PS.bufs[b2], b_sin], AR.b(6))
        PS.put(b1)
        PS.put(b2)
        TT(krotT[l][:, t0:t0 + T], t1, t2, ALU.add, AR.b(5) + AR.b(6), [b_kr[l][g]])

        ntile = 4 * g + 4
        fw.phase = 'mla_heads'
        for hh in range(8):
            par = hh % 2
            b = PS.get()
            for i in range(2):
                MM(psb(b), wuqn_sb[:, i, hh * 128:(hh + 1) * 128], cqn[:, i, :], i == 0, i == 1, [b_wuqn, b_cqn], [PS.bufs[b]])
            qn = AR.bf16(0, T, par * T)
            ACT(qn, psb(b), AF.Copy, [PS.bufs[b]], AR.b(0))
            PS.put(b)
            b = PS.get()
            MM(psb(b), wukT_sb[:, hh * 128:(hh + 1) * 128], qn, True, True, [b_wukT] + AR.b(0), [PS.bufs[b]])
            qp = AR.bf16(1, T, par * T)
            ACT(qp, psb(b), AF.Copy, [PS.bufs[b]], AR.b(1))
            PS.put(b)
            b1 = PS.get()
            b2 = PS.get()
            for i in range(2):
                MM(psb(b1, parts=64), wuqr_sb[:, i, hh * 64:(hh + 1) * 64], cqn[:, i, :], i == 0, i == 1, [b_wuqr, b_cqn], [PS.bufs[b1]])
            for i in range(2):
                MM(psb(b2, parts=64), wuqrs_sb[:, i, hh * 64:(hh + 1) * 64], cqn[:, i, :], i == 0, i == 1, [b_wuqrs, b_cqn], [PS.bufs[b2]])
            t1 = AR.f32(5, T, parts=64)
            t2 = AR.f32(6, T, parts=64)
            TT(t1, psb(b1, parts=64), cos2[:], ALU.mult, [PS.bufs[b1], b_cos], AR.b(5))
            TT(t2, psb(b2, parts=64), sinpm[:], ALU.mult, [PS.bufs[b2], b_sin], AR.b(6))
            PS.put(b1)
            PS.put(b2)
            qr = AR.bf16(2, T, par * T, parts=64)
            TT(qr, t1, t2, ALU.add, AR.b(5) + AR.b(6), AR.b(2))
            po = PS.get()
            pd = PS.get()
            j = 0
            it = 0
            while j < ntile:
                npair = 2 if (j + 1 < 4 * g) else 1
                ppg = 8 + 2 * (it % 2)
                it += 1
                if npair == 2:
                    bs = PS.get_pair()
                else:
                    bs = PS.get()
                for jj in range(npair):
                    jt = j + jj
                    qlo = max(0, jt - 4 * g) * 128
                    kg = jt // 4
                    MM(PS.t[:, bs + jj, qlo:T], ckvnT[l][:, jt * 128:(jt + 1) * 128], qp[:, qlo:T], True, False,
                       [b_kc[l][kg]] + AR.b(1), [PS.bufs[bs + jj]])
                    MM(PS.t[:, bs + jj, qlo:T], krotT[l][:, jt * 128:(jt + 1) * 128], qr[:, qlo:T], False, True,
                       [b_kr[l][kg]] + AR.b(2), [PS.bufs[bs + jj]])
                qlo = max(0, j - 4 * g) * 128 if npair == 1 else 0
                if npair == 2:
                    pT = AR.t[:, ppg * 544:ppg * 544 + 512].bitcast(BF16).rearrange("p (a n) -> p a n", a=2)
                    ACT(pT, PS.t[:, bs:bs + 2, :], AF.Exp, [PS.bufs[bs], PS.bufs[bs + 1]], AR.b(ppg), scale=ATT_SCALE)
                    pTs = [pT[:, 0, :], pT[:, 1, :]]
                else:
                    pT1 = AR.bf16(ppg, T)
                    ACT(pT1[:, qlo:T], PS.t[:, bs, qlo:T], AF.Exp, [PS.bufs[bs]], AR.b(ppg), scale=ATT_SCALE)
                    if j >= 4 * g:
                        fw.op(dve, lambda h, pT1=pT1, qlo=qlo: h.memset(pT1[64:128, qlo:qlo + 64], 0.0), [], AR.b(ppg))
                    pTs = [pT1]
                for jj in range(npair):
                    PS.put(bs + jj)
                for jj in range(npair):
                    jt = j + jj
                    ql = max(0, jt - 4 * g) * 128
                    kg = jt // 4
                    MM(PS.t[:, po, ql:T], Vc[l][:, jt, :], pTs[jj][:, ql:T], jt == 0, jt == ntile - 1,
                       [b_vc[l][kg]] + AR.b(ppg), [PS.bufs[po]], signal=False)
                    MM(PS.t[:, pd, ql:T], ones_bf[:], pTs[jj][:, ql:T], jt == 0, jt == ntile - 1,
                       [b_cst2] + AR.b(ppg), [PS.bufs[pd]], signal=True)
                j += npair
            rden = AR.f32(7, T)
            fw.op(dve, lambda h, rden=rden, pd=pd: h.reciprocal(out=rden, in_=psb(pd)), [PS.bufs[pd]], AR.b(7))
            PS.put(pd)
            op_ = AR.bf16(3, T, par * T)
            TT(op_, psb(po), rden, ALU.mult, [PS.bufs[po]] + AR.b(7), AR.b(3))
            PS.put(po)
            b = PS.get()
            MM(psb(b), wuv_sb[:, hh * 128:(hh + 1) * 128], op_, True, True, [b_wuv] + AR.b(3), [PS.bufs[b]])
            ACT(branch(hh), psb(b), AF.Copy, [PS.bufs[b]], b_branch(hh))
            PS.put(b)
        if nxt_l is not None:
            load_attw(nxt_l)
        fw.phase = 'bout_b'
        branch_out(l, 1, False)
        fw.phase = 'sgu'

        wv_s = [wload(scols(("w_in", l), OFF_SG + D + nn * 512, 512)) for nn in range(2)]
        vn_all = lambda tb: AR.bf16(4 + tb, 1024)
        for tb in range(4):
            bp = PS.get_pair()
            for nn in range(2):
                wv = ring_view(wv_s[nn], 8, 512)
                for k in range(8):
                    MM(PS.t[:, bp + nn, :], h_sb[:, k, tb * 128:(tb + 1) * 128], wv[:, k, :], k == 0, k == 7,
                       RING.b(wv_s[nn]) + [b_h[k]], [PS.bufs[bp + nn]])
            gpg = 0 + 2 * (tb % 2)
            gv = AR.t[:, gpg * 544:gpg * 544 + 1024]
            ACT(gv.rearrange("p (a n) -> p a n", a=2), PS.t[:, bp:bp + 2, :], AF.Gelu,
                [PS.bufs[bp], PS.bufs[bp + 1]], AR.b(gpg, 2))
            PS.put(bp)
            PS.put(bp + 1)
            stt_ = AR.f32(12, 16)
            for i in range(2):
                fw.op(dve, lambda h, i=i, gv=gv, stt_=stt_: h.bn_stats(out=stt_[:, i * 6:(i + 1) * 6], in_=gv[:, i * 512:(i + 1) * 512]),
                      AR.b(gpg, 2), AR.b(12))
            mvv = AR.f32(12, 2, off=16)
            fw.op(dve, lambda h, mvv=mvv, stt_=stt_: h.bn_aggr(out=mvv, in_=stt_[:, 0:12].rearrange("p (a n) -> p a n", a=2)),
                  AR.b(12), AR.b(12))
            rs = AR.f32(12, 1, off=20)
            ACT(rs, mvv[:, 1:2], AF.Sqrt, AR.b(12) + [b_cst2], AR.b(12), bias=eps_ln1[:, 0:1])
            fw.op(dve, lambda h, rs=rs: h.reciprocal(out=rs, in_=rs), AR.b(12), AR.b(12))
            TS(gv, gv, mvv[:, 0:1], rs, ALU.subtract, ALU.mult, AR.b(gpg, 2) + AR.b(12), AR.b(gpg, 2))
            TT(gv, gv, gsg_sb[:], ALU.mult, AR.b(gpg, 2) + [b_gsg], AR.b(gpg, 2))
            TT(vn_all(tb), gv, bsg_sb[:], ALU.add, AR.b(gpg, 2) + [b_bsg], AR.b(4 + tb))
        for half in range(2):
            wu = wload(scols(("w_in", l), OFF_SG + half * 512, 512))
            for gg in range(4):
                gq = half * 4 + gg
                bz = PS.get()
                for tb in range(4):
                    MM(PS.t[:, bz, tb * 128:(tb + 1) * 128], vn_all(tb)[:, gq * 128:(gq + 1) * 128],
                       wsT_sb[:, gq * 128:(gq + 1) * 128], True, False, AR.b(4 + tb) + [b_wsT], [PS.bufs[bz]])
                    MM(PS.t[:, bz, tb * 128:(tb + 1) * 128], ones_bf[0:1, :], bsT_sb[0:1, gq * 128:(gq + 1) * 128],
                       False, True, [b_cst2, b_bsT], [PS.bufs[bz]], signal=(tb == 3))
                bu = proj8(wu, gg * 128, lambda k: h_sb[:, k, :], lambda k: [b_h[k]])
                pg = 0 + (gq % 2)
                ug = AR.f32(pg, T)
                ACT(ug, psb(bu), AF.Gelu, [PS.bufs[bu]], AR.b(pg))
                PS.put(bu)
                TT(branch(gq), ug, psb(bz), ALU.mult, AR.b(pg) + [PS.bufs[bz]], b_branch(gq))
                PS.put(bz)
        if nxt_l is not None:
            load_sguw(nxt_l)
        fw.phase = 'bout_c'
        branch_out(l, 2, False)
        fw.phase = 'wo'

        for d in range(8):
            CP(branch(d), merged(d), b_merged(d), b_branch(d))
        for half in range(2):
            wo = wload(scols(("w_o", l), half * 512, 512))
            for dd in range(4):
                d = half * 4 + dd
                by = proj8(wo, dd * 128, lambda k: branch(k), lambda k: b_branch(k))
                STT(XC[0][:, d, :], psb(by), mcol(l, 2, d, s), XC[0][:, d, :], ALU.mult, ALU.add,
                    [PS.bufs[by], b_mod, XC[1][d]], [XC[1][d]])
                PS.put(by)
        fw.phase = 'post_ln_t'
        post_ln(l, V_LNTG, V_LNTB)

    def ffn(l, s, g):
        fw.phase = 'ln_mod_f'
        ln_mod(l, s, 1)
        fw.phase = 'ffn_up'
        nblk = (NF + 3) // 4

        def ffn_stage2(f):
            av = AR.f32(2 * (f % 3), T)
            ag = AR.f32(2 * (f % 3) + 1, T)
            pav, pag = 2 * (f % 3), 2 * (f % 3) + 1
            ACT(ag, ag, AF.Silu, AR.b(pag), AR.b(pag))
            TT(actT(f), ag, av, ALU.mult, AR.b(pag) + AR.b(pav), b_actT(f), eng=pool)

        for fb in range(nblk):
            nfc = min(4, NF - fb * 4)
            wvs = wload(scols(("w_up", l), fb * 512, nfc * 128))
            wgs = wload(scols(("w_up", l), D_FF + fb * 512, nfc * 128))
            for ff in range(nfc):
                f = fb * 4 + ff
                for which, wsl in ((0, wvs), (1, wgs)):
                    ch = which * NF + f
                    wv = ring_view(wsl, 8, nfc * 128)
                    b = PS.get()
                    for k in range(8):
                        MM(psb(b), wv[:, k, ff * 128:(ff + 1) * 128], h_sb[:, k, :], k == 0, k == 7,
                           RING.b(wsl) + [b_h[k]], [PS.bufs[b]])
                    pa = 2 * (f % 3) + which
                    acc = AR.f32(pa, T)
                    zh = zhalo[l][:, ch, :]
                    bzh = [b_zhalo[l][ch]]
                    w0, w1, w2 = vcol(l, V_CW, ch), vcol(l, V_CW + 2 * NF, ch), vcol(l, V_CW + 4 * NF, ch)
                    ACT(acc, psb(b), AF.Identity, [PS.bufs[b], b_vecs], AR.b(pa), scale=w2, bias=vcol(l, V_CB, ch))
                    STT(acc[:, 1:T], psb(b, 0, T - 1), w1, acc[:, 1:T], ALU.mult, ALU.add,
                        [PS.bufs[b], b_vecs] + AR.b(pa), AR.b(pa))
                    STT(acc[:, 2:T], psb(b, 0, T - 2), w0, acc[:, 2:T], ALU.mult, ALU.add,
                        [PS.bufs[b], b_vecs] + AR.b(pa), AR.b(pa))
                    STT(acc[:, 0:1], zh[:, 1:2], w1, acc[:, 0:1], ALU.mult, ALU.add, bzh + [b_vecs] + AR.b(pa), AR.b(pa))
                    STT(acc[:, 0:2], zh, w0, acc[:, 0:2], ALU.mult, ALU.add, bzh + [b_vecs] + AR.b(pa), AR.b(pa))
                    ACT(zh, psb(b, T - 2, T), AF.Copy, [PS.bufs[b]], bzh)
                    PS.put(b)
                if f > 0:
                    ffn_stage2(f - 1)
        ffn_stage2(NF - 1)
        fw.phase = 'ffn_down'
        for d in range(8):
            wd = wload(scols(("w_down", l), d * 128, 128))
            wv = RING.bf16(wd, NF * 128).rearrange("p (k n) -> p k n", k=NF)
            b = PS.get()
            for f in range(NF):
                MM(psb(b), wv[:, f, :], actT(f), f == 0, f == NF - 1, RING.b(wd) + b_actT(f), [PS.bufs[b]])
            STT(XC[0][:, d, :], psb(b), mcol(l, 5, d, s), XC[0][:, d, :], ALU.mult, ALU.add,
                [PS.bufs[b], b_mod, XC[1][d]], [XC[1][d]])
            PS.put(b)
        fw.phase = 'post_ln_f'
        post_ln(l, V_LNFG, V_LNFB)

    steps = [(s, g, l) for s in range(nseq) for g in range(ngroups) for l in range(nlayers)]
    groups = [(s, g) for s in range(nseq) for g in range(ngroups)]
    load_wpool(0)
    load_attw(0)
    load_sguw(0)

    def load_x(gi):
        s_, g_ = groups[gi]
        fw.dma(pool, xs_t[gi % 2][:], xT[s_].rearrange("(kc p) n -> p kc n", p=128)[:, :, g_ * T:(g_ + 1) * T],
               writes=xs_b[gi % 2])

    load_x(0)
    for idx, (s, g, l) in enumerate(steps):
        nxt_l = steps[idx + 1][2] if idx + 1 < len(steps) else None
        gi = s * ngroups + g
        XC[0], XC[1] = xs_t[gi % 2], xs_b[gi % 2]
        t0 = g * T
        if l == 0 and g == 0:
            for ll in range(nlayers):
                fw.op(dve, lambda h, ll=ll: h.memset(phalo[ll][:], 0.0), [], [b_phalo[ll]])
                fw.op(dve, lambda h, ll=ll: h.memset(zhalo[ll][:], 0.0), [], b_zhalo[ll])
        mixer(l, s, g, nxt_l)
        if dbg == (s, g, l, "mix"):
            fw.dma(pool, dbg_out, XC[0][:], reads=XC[1])
        if l == nlayers - 1 and gi + 1 < len(groups):
            load_x(gi + 1)
        ffn(l, s, g)
        if dbg == (s, g, l, "ffn"):
            fw.dma(pool, dbg_out, XC[0][:], reads=XC[1])
        if l == nlayers - 1:
            fw.dma(pool, outT[s].rearrange("(kc p) n -> p kc n", p=128)[:, :, t0:t0 + T], XC[0][:], reads=XC[1])
    fw.final_fence(sp)
    fw.emit()
    return nc, fw


def _host_prep(inp):
    f = lambda a: np.ascontiguousarray(np.asarray(a, dtype=np.float32))
    L = DEPTH
    w_in = np.asarray(inp["w_in"], np.float32)
    w_uq = np.asarray(inp["w_uq"], np.float32)
    w_ukv = np.asarray(inp["w_ukv"], np.float32)
    perm = np.concatenate([np.arange(32, 64), np.arange(0, 32)])
    shared = {
        "w_ada": f(inp["w_ada"]),
        "w_in": f(w_in),
        "w_krs": f(w_in[:, :, OFF_KR:OFF_KR + 64][:, :, perm]),
        "w_pool": f(inp["w_pool"]),
        "w_uqn": f(w_uq[:, :, :, 0:128].reshape(L, 256, 1024)),
        "w_uqr": f(w_uq[:, :, :, 128:192].reshape(L, 256, 512)),
        "w_uqrs": f(w_uq[:, :, :, 128:192][:, :, :, perm].reshape(L, 256, 512)),
        "w_ukT": f(np.transpose(w_ukv[:, :, :, 0:128], (0, 3, 2, 1)).reshape(L, 128, 1024)),
        "w_uv": f(w_ukv[:, :, :, 128:256].reshape(L, 128, 1024)),
        "wsT": f(np.transpose(np.asarray(inp["w_s"], np.float32), (0, 3, 1, 2)).reshape(L, 128, 1024)),
        "bsT": f(np.transpose(np.asarray(inp["b_s"], np.float32), (0, 2, 1)).reshape(L, 1, 1024)),
        "gsg": f(np.asarray(inp["g_sg"], np.float32).reshape(L, 1, 1024)),
        "bsg": f(np.asarray(inp["b_sg"], np.float32).reshape(L, 1, 1024)),
        "w_branch": f(inp["w_branch"]),
        "w_o": f(inp["w_o"]),
        "w_up": f(inp["w_up"]),
        "w_down": f(inp["w_down"]),
    }
    pp = lambda v, n: np.transpose(np.asarray(v, np.float32).reshape(L, n, 128), (0, 2, 1))
    vecs = np.zeros((L, 128, NV), np.float32)
    vecs[:, :, V_SPOOL:V_SPOOL + 8] = pp(inp["s_pool"], 8)
    vecs[:, :, V_GQ:V_GQ + 2] = pp(inp["g_q"], 2)
    vecs[:, :, V_GKV:V_GKV + 1] = pp(inp["g_kv"], 1)
    vecs[:, :, V_LNTG:V_LNTG + 8] = pp(inp["ln_t_g"], 8)
    vecs[:, :, V_LNTB:V_LNTB + 8] = pp(inp["ln_t_b"], 8)
    vecs[:, :, V_LNFG:V_LNFG + 8] = pp(inp["ln_f_g"], 8)
    vecs[:, :, V_LNFB:V_LNFB + 8] = pp(inp["ln_f_b"], 8)
    cw = np.asarray(inp["conv_w"], np.float32)
    for k in range(3):
        vecs[:, :, V_CW + k * 44:V_CW + (k + 1) * 44] = pp(cw[:, k], 44)
    vecs[:, :, V_CB:V_CB + 44] = pp(inp["conv_b"], 44)
    vecs[:, :, V_BADA:V_BADA + 48] = pp(inp["b_ada"], 48)
    shared["vecs"] = vecs
    consts = np.zeros((128, NCONST), np.float32)
    consts[:, C_IDENT:C_IDENT + 128] = np.eye(128, dtype=np.float32)
    consts[:, C_MASK:C_MASK + 128] = np.triu(np.ones((128, 128), np.float32))
    freqs = (10000.0 ** (-np.arange(0, 64, 2, dtype=np.float32) / 64)).astype(np.float32)
    consts[0:64, C_FREQ] = np.concatenate([freqs, freqs])
    consts[0:32, C_SGN] = -1.0
    consts[32:64, C_SGN] = 1.0
    for wi, w in enumerate((2, 4, 8, 16)):
        tt = np.arange(16)
        consts[:, C_CORR + wi * 16:C_CORR + (wi + 1) * 16] = (w / np.minimum(tt + 1, w)).astype(np.float32)[None, :]
    shared["consts"] = consts
    return shared


_CACHE = {}


def kernel(**inputs):
    x = np.asarray(inputs["x"], np.float32)
    c = np.asarray(inputs["c"], np.float32)
    pos = np.asarray(inputs["pos"], np.int32)
    shared = _host_prep(inputs)
    key = "full"
    if key not in _CACHE:
        _CACHE[key] = build_program()
    nc, _ = _CACHE[key]
    in_maps = []
    for core in range(NCORES):
        b0 = core * SPC
        m = dict(shared)
        m["xT"] = np.ascontiguousarray(np.transpose(x[b0:b0 + SPC], (0, 2, 1)))
        m["cT"] = np.ascontiguousarray(np.transpose(c[b0:b0 + SPC].reshape(SPC, 8, 128), (2, 1, 0)))
        m["pos"] = np.ascontiguousarray(pos[b0:b0 + SPC])
        in_maps.append(m)
    res = run_bass_kernel_spmd(nc, in_maps, core_ids=list(range(NCORES)))
    out = np.empty((BATCH, SEQ, D), np.float32)
    for core in range(NCORES):
        o = res.results[core]["outT"]
        out[core * SPC:(core + 1) * SPC] = np.transpose(o, (0, 2, 1))
    return out
```

```python
import numpy as np
import concourse.bass as bass
import concourse.mybir as mybir
from concourse.bass_utils import run_bass_kernel_spmd

F32 = mybir.dt.float32
BF16 = mybir.dt.bfloat16
I32 = mybir.dt.int32
AF = mybir.ActivationFunctionType
ALU = mybir.AluOpType

D = 1024
SEQ = 2048
BATCH = 32
DEPTH = 2
NCORES = 8
SPC = BATCH // NCORES
T = 512
NG = SEQ // T
D_FF = 2816
NF = D_FF // 128
OFF_POOL = 3 * D
OFF_CQ = OFF_POOL + D
OFF_CKV = OFF_CQ + 256
OFF_KR = OFF_CKV + 128
OFF_SG = OFF_KR + 64
IN_WIDTH = OFF_SG + 2 * D
ALPHA = (2 * DEPTH) ** 0.25
LN_EPS = 1e-5
RMS_EPS = 1e-6
ATT_SCALE = 192 ** -0.5
POOL_W = (2, 2, 4, 4, 8, 8, 16, 16)

V_SPOOL = 0
V_GQ = 8
V_GKV = 10
V_LNTG = 11
V_LNTB = 19
V_LNFG = 27
V_LNFB = 35
V_CW = 43
V_CB = V_CW + 132
V_BADA = V_CB + 44
NV = V_BADA + 48
C_IDENT = 0
C_MASK = 128
C_FREQ = 256
C_SGN = 257
C_CORR = 258
NCONST = C_CORR + 64

SEM_EPOCH = 8000


class Buf:
    __slots__ = ("w", "r", "name")

    def __init__(self, name=""):
        self.w = {}
        self.r = {}
        self.name = name


class Eng:
    def __init__(self, fw, name, is_pe=False):
        self.fw = fw
        self.name = name
        self.is_pe = is_pe
        self.prog = []
        self.waited = {}
        self.sems = []
        self.count = 0
        self._new_sem()

    def _new_sem(self):
        s = self.fw.new_sem(f"{self.name}_p{len(self.sems)}")
        self.sems.append(s)
        self.sem = s
        self.count = 0


class DmaSem:
    def __init__(self, fw, name):
        self.sem = fw.new_sem(name)
        self.value = 0


class FW:
    def __init__(self, nc, same_engine_sync=True):
        self.nc = nc
        self.same_sync = same_engine_sync
        self.pe = Eng(self, "pe", is_pe=True)
        self.act = Eng(self, "act")
        self.dve = Eng(self, "dve")
        self.pool = Eng(self, "pool")
        self.sp = Eng(self, "sp")
        self.dma_pool = {}
        self.retired = []
        self.n_inst = 0
        self.phase = 'setup'
        self.phases = {}

    def new_sem(self, name):
        return self.nc.alloc_semaphore(name=name)

    def _collect(self, eng, reads, writes):
        need = {}

        def add(d, raw):
            for s, v in d.items():
                if s in eng.sems:
                    if not (raw and self.same_sync and not eng.is_pe):
                        continue
                if need.get(s, 0) < v:
                    need[s] = v

        for b in reads:
            add(b.w, True)
        for b in writes:
            add(b.w, True)
            add(b.r, False)
        waits = []
        for s, v in need.items():
            if eng.waited.get(s, 0) < v:
                eng.waited[s] = v
                waits.append((s, v))
        return waits

    @staticmethod
    def _update(stamp, reads, writes):
        s, v = stamp
        for b in writes:
            b.w = {s: v}
            b.r = {}
        for b in reads:
            if b in writes:
                continue
            b.r[s] = v

    def op(self, eng, fn, reads=(), writes=(), signal=True):
        reads = list(reads)
        writes = list(writes)
        waits = self._collect(eng, reads, writes)
        sem = eng.sem
        stamp = (sem, eng.count + 1)
        self.n_inst += 1

        def emit(h, waits=waits, fn=fn, signal=signal, sem=sem):
            for s, v in waits:
                h.wait_ge(s, v)
            ins = fn(h)
            if signal:
                ins.then_inc(sem, 1)

        eng.prog.append(emit)
        self.phases.setdefault(eng.name, []).append(self.phase)
        self._update(stamp, reads, writes)
        if signal:
            eng.count += 1
            if eng.count >= SEM_EPOCH:
                eng._new_sem()
        return stamp

    def dma(self, eng, out_ap, in_ap, reads=(), writes=(), dsem=None, **kw):
        reads = list(reads)
        writes = list(writes)
        if dsem is None:
            dsem = self.get_dma_sem(eng)
        waits = self._collect(eng, reads, writes)
        if dsem.value > 0 and eng.waited.get(dsem.sem, 0) < dsem.value:
            eng.waited[dsem.sem] = dsem.value
            waits.append((dsem.sem, dsem.value))
        dsem.value += 16
        stamp = (dsem.sem, dsem.value)
        self.n_inst += 1

        def emit(h, waits=waits, out_ap=out_ap, in_ap=in_ap, kw=kw, sem=dsem.sem):
            for s, v in waits:
                h.wait_ge(s, v)
            h.dma_start(out=out_ap, in_=in_ap, **kw).then_inc(sem, 16)

        eng.prog.append(emit)
        self._update(stamp, reads, writes)
        return stamp

    def get_dma_sem(self, eng, n=8):
        key = eng.name
        if key not in self.dma_pool:
            self.dma_pool[key] = [[DmaSem(self, f"dma_{key}_{i}") for i in range(n)], 0]
        lst = self.dma_pool[key]
        i = lst[1] % n
        d = lst[0][i]
        if d.value >= 2048:
            self.retired.append(d)
            d = DmaSem(self, f"dma_{key}_{i}_{len(self.retired)}")
            lst[0][i] = d
        lst[1] += 1
        return d

    def final_fence(self, eng):
        sems = [d for lst, _ in self.dma_pool.values() for d in lst if d.value] + list(self.retired)

        def emit(h, sems=sems):
            for d in sems:
                h.wait_ge(d.sem, d.value)

        eng.prog.append(emit)

    def emit(self):
        with self.nc.Block() as block:
            @block.tensor
            def _(h):
                for f in self.pe.prog:
                    f(h)

            @block.scalar
            def _(h):
                for f in self.act.prog:
                    f(h)

            @block.vector
            def _(h):
                for f in self.dve.prog:
                    f(h)

            @block.gpsimd
            def _(h):
                for f in self.pool.prog:
                    f(h)

            @block.sync
            def _(h):
                for f in self.sp.prog:
                    f(h)


class Region:
    def __init__(self, nc, name, npages, page_words):
        self.t = nc.alloc_sbuf_tensor(name, [128, npages * page_words], F32)
        self.pw = page_words
        self.np = npages
        self.bufs = [Buf(f"{name}{i}") for i in range(npages)]

    def f32(self, page, n, off=0, parts=128):
        a = page * self.pw + off
        return self.t[0:parts, a:a + n]

    def bf16(self, page, n, off=0, parts=128):
        a = page * self.pw
        words = (off + n + 1) // 2
        v = self.t[0:parts, a:a + words].bitcast(BF16)
        return v[:, off:off + n]

    def b(self, page, npg=1):
        return self.bufs[page:page + npg]


class PsumPool:
    def __init__(self, nc):
        self.t = nc.alloc_psum_tensor("ps", [128, 8, 512], F32)
        self.bufs = [Buf(f"ps{i}") for i in range(8)]
        self.free = {i: i for i in range(8)}
        self.clock = 8

    def get(self):
        b = min(self.free, key=lambda k: self.free[k])
        del self.free[b]
        return b

    def get_pair(self):
        cands = [i for i in (0, 2, 4, 6) if i in self.free and i + 1 in self.free]
        assert cands, "no free psum pair"
        b = min(cands, key=lambda k: max(self.free[k], self.free[k + 1]))
        del self.free[b]
        del self.free[b + 1]
        return b

    def put(self, b):
        self.clock += 1
        self.free[b] = self.clock


def build_program(nseq=SPC, ngroups=NG, nlayers=DEPTH, dbg=None):
    nc = bass.Bass("TRN2", target_bir_lowering=False)
    dt_in = lambda name, shape, dt=F32: nc.dram_tensor(name, list(shape), dt, kind="ExternalInput").ap()
    xT = dt_in("xT", [nseq, D, SEQ])
    cT = dt_in("cT", [128, 8, nseq])
    pos = dt_in("pos", [nseq, SEQ], I32)
    consts = dt_in("consts", [128, NCONST])
    vecs = dt_in("vecs", [DEPTH, 128, NV])
    w_ada = dt_in("w_ada", [DEPTH, D, 6 * D])
    w_in = dt_in("w_in", [DEPTH, D, IN_WIDTH])
    w_krs = dt_in("w_krs", [DEPTH, D, 64])
    w_pool = dt_in("w_pool", [DEPTH, 4, 256, 256])
    w_uqn = dt_in("w_uqn", [DEPTH, 256, 1024])
    w_uqr = dt_in("w_uqr", [DEPTH, 256, 512])
    w_uqrs = dt_in("w_uqrs", [DEPTH, 256, 512])
    w_ukT = dt_in("w_ukT", [DEPTH, 128, 1024])
    w_uv = dt_in("w_uv", [DEPTH, 128, 1024])
    wsT = dt_in("wsT", [DEPTH, 128, 1024])
    bsT = dt_in("bsT", [DEPTH, 1, 1024])
    gsg = dt_in("gsg", [DEPTH, 1, 1024])
    bsg = dt_in("bsg", [DEPTH, 1, 1024])
    w_branch = dt_in("w_branch", [DEPTH, 3, D, D])
    w_o = dt_in("w_o", [DEPTH, D, D])
    w_up = dt_in("w_up", [DEPTH, D, 2 * D_FF])
    w_down = dt_in("w_down", [DEPTH, D_FF, D])
    outT = nc.dram_tensor("outT", [nseq, D, SEQ], F32, kind="ExternalOutput").ap()
    dbg_out = None
    if dbg is not None:
        dbg_out = nc.dram_tensor("dbg", [128, 8, T], F32, kind="ExternalOutput").ap()

    fw = FW(nc)
    pe, act, dve, pool, sp = fw.pe, fw.act, fw.dve, fw.pool, fw.sp
    A = nc.alloc_sbuf_tensor

    xs_t = [A(f"x_sb{i}", [128, 8, T], F32) for i in range(2)]
    xs_b = [[Buf(f"x{i}_{c}") for c in range(8)] for i in range(2)]
    XC = [xs_t[0], xs_b[0]]
    h_sb = A("h_sb", [128, 8, T], BF16)
    b_h = [Buf(f"h{c}") for c in range(8)]
    MB = Region(nc, "mb", 12, 512)
    merged = lambda d: MB.f32(d, 512)
    b_merged = lambda d: MB.b(d)
    branch = lambda c, lo=0, hi=T: MB.bf16(8 + c // 2, hi - lo, (c % 2) * 512 + lo)
    b_branch = lambda c: MB.b(8 + c // 2)
    actT = lambda f: MB.bf16(f // 2, 512, (f % 2) * 512)
    b_actT = lambda f: MB.b(f // 2)
    AR = Region(nc, "ar", 14, 544)
    st_mean = A("st_mean", [128, T], F32); b_mean = Buf("mean")
    st_var = A("st_var", [128, T], F32); b_var = Buf("var")
    st_rstd = A("st_rstd", [128, T], F32); b_rstd = Buf("rstd")
    NSLOT = 5
    RING = Region(nc, "ring", NSLOT, 2048)
    ring_i = [0]
    wpool_sb = A("wpool_sb", [128, 2, 4, 256], BF16); b_wpool = Buf("wpool")
    wuqn_sb = A("wuqn_sb", [128, 2, 1024], BF16); b_wuqn = Buf("wuqn")
    wuqr_sb = A("wuqr_sb", [128, 2, 512], BF16); b_wuqr = Buf("wuqr")
    wuqrs_sb = A("wuqrs_sb", [128, 2, 512], BF16); b_wuqrs = Buf("wuqrs")
    wukT_sb = A("wukT_sb", [128, 1024], BF16); b_wukT = Buf("wukT")
    wuv_sb = A("wuv_sb", [128, 1024], BF16); b_wuv = Buf("wuv")
    wsT_sb = A("wsT_sb", [128, 1024], BF16); b_wsT = Buf("wsT")
    bsT_sb = A("bsT_sb", [1, 1024], BF16); b_bsT = Buf("bsT")
    gsg_sb = A("gsg_sb", [128, 1024], F32); b_gsg = Buf("gsg")
    bsg_sb = A("bsg_sb", [128, 1024], F32); b_bsg = Buf("bsg")
    vecs_sb = A("vecs_sb", [128, DEPTH, NV], F32); b_vecs = Buf("vecs")
    mod_sb = A("mod_sb", [128, DEPTH * 48 * nseq], F32); b_mod = Buf("mod")
    c_sb = A("c_sb", [128, 8, nseq], F32); b_c = Buf("c")
    const_sb = A("const_sb", [128, NCONST], F32); b_const = Buf("const")
    ident_bf = A("ident_bf", [128, 128], BF16)
    mask_bf = A("mask_bf", [128, 128], BF16)
    ones_bf = A("ones_bf", [128, 128], BF16)
    onesm_bf = A("onesm_bf", [128, 128], BF16)
    ones_f = A("ones_f", [128, 128], F32)
    eps_ln = A("eps_ln", [128, 1], F32)
    eps_rms = A("eps_rms", [128, 1], F32)
    b_cst2 = Buf("cst2")
    ckvnT = [A(f"ckvnT{l}", [128, SEQ], BF16) for l in range(DEPTH)]
    krotT = [A(f"krotT{l}", [64, SEQ], BF16) for l in range(DEPTH)]
    Vc = [A(f"Vc{l}", [128, SEQ // 128, 128], BF16) for l in range(DEPTH)]
    b_kc = [[Buf(f"kc{l}_{j}") for j in range(NG)] for l in range(DEPTH)]
    b_kr = [[Buf(f"kr{l}_{j}") for j in range(NG)] for l in range(DEPTH)]
    b_vc = [[Buf(f"vc{l}_{j}") for j in range(NG)] for l in range(DEPTH)]
    cos2 = A("cos2", [64, T], F32); b_cos = Buf("cos")
    sinpm = A("sinpm", [64, T], F32); b_sin = Buf("sin")
    cqn = A("cqn", [128, 2, T], BF16); b_cqn = Buf("cqn")
    phalo = [A(f"phalo{l}", [128, 8, 16], F32) for l in range(DEPTH)]
    b_phalo = [Buf(f"phalo{l}") for l in range(DEPTH)]
    zhalo = [A(f"zhalo{l}", [128, 2 * NF, 2], F32) for l in range(DEPTH)]
    b_zhalo = [[Buf(f"zhalo{l}_{c}") for c in range(2 * NF)] for l in range(DEPTH)]
    PS = PsumPool(nc)

    def ACT(out, in_, func, reads, writes, **kw):
        fw.op(act, lambda h: h.activation(out=out, in_=in_, func=func, **kw), reads, writes)

    def TT(out, in0, in1, op, reads, writes, eng=dve):
        fw.op(eng, lambda h: h.tensor_tensor(out=out, in0=in0, in1=in1, op=op), reads, writes)

    def STT(out, in0, scalar, in1, op0, op1, reads, writes, eng=dve):
        fw.op(eng, lambda h: h.scalar_tensor_tensor(out=out, in0=in0, scalar=scalar, in1=in1, op0=op0, op1=op1),
              reads, writes)

    def TS(out, in0, s1, s2, op0, op1, reads, writes, eng=dve):
        if s2 is None:
            fw.op(eng, lambda h: h.tensor_scalar(out=out, in0=in0, scalar1=s1, scalar2=None, op0=op0), reads, writes)
        else:
            fw.op(eng, lambda h: h.tensor_scalar(out=out, in0=in0, scalar1=s1, scalar2=s2, op0=op0, op1=op1),
                  reads, writes)

    def CP(out, in_, reads, writes, eng=dve):
        fw.op(eng, lambda h: h.tensor_copy(out=out, in_=in_), reads, writes)

    def MM(out, lhsT, rhs, start, stop, reads, writes, signal=None):
        signal = True
        fw.op(pe, lambda h: h.matmul(out, lhsT=lhsT, rhs=rhs, start=start, stop=stop), reads, writes, signal=signal)

    def psb(b, lo=0, hi=T, parts=128):
        return PS.t[0:parts, b, lo:hi]

    SCR = {}

    def mk_scratch(key, src2d):
        rows, cols = src2d.shape
        t = nc.dram_tensor("scr_" + "_".join(str(k) for k in key), [rows, cols], BF16, kind="Internal").ap()
        bufs = []
        for r0 in range(0, rows, 128):
            b = Buf("scr")
            fw.dma(pool, t[r0:r0 + 128, :], src2d[r0:r0 + 128, :], writes=[b])
            bufs.append(b)
        SCR[key] = (t, bufs)

    def scols(key, c0, n):
        ap, bufs = SCR[key]
        return ap.rearrange("(kc p) n -> p kc n", p=128)[:, :, c0:c0 + n], bufs

    def wload(src, eng=None):
        src_ap, sbufs = src
        slot = ring_i[0] % NSLOT
        ring_i[0] += 1
        nel = 1
        for s_ in src_ap.shape[1:]:
            nel *= s_
        dst = RING.bf16(slot, nel).rearrange("p (k n) -> p k n", k=src_ap.shape[1])
        fw.dma(eng or sp, dst, src_ap, reads=sbufs, writes=RING.b(slot))
        return slot

    def ring_view(slot, k, n, tot=None):
        tot = tot or n
        v = RING.bf16(slot, k * tot).rearrange("p (k n) -> p k n", k=k)
        return v

    def mcol(l, kind, c, s):
        i = ((l * 6 + kind) * 8 + c) * nseq + s
        return mod_sb[:, i:i + 1]

    def vcol(l, off, c=0):
        return vecs_sb[:, l, off + c:off + c + 1]

    fw.dma(sp, const_sb[:], consts, writes=[b_const])
    fw.dma(sp, vecs_sb[:], vecs.rearrange("l p n -> p l n"), writes=[b_vecs])
    fw.dma(sp, c_sb[:], cT, writes=[b_c])
    CP(ident_bf[:], const_sb[:, C_IDENT:C_IDENT + 128], [b_const], [b_cst2])
    CP(mask_bf[:], const_sb[:, C_MASK:C_MASK + 128], [b_const], [b_cst2])
    fw.op(dve, lambda h: h.memset(ones_bf[:], 1.0), writes=[b_cst2])
    fw.op(dve, lambda h: h.memset(onesm_bf[:], 1.0 / D), writes=[b_cst2])
    fw.op(dve, lambda h: h.memset(ones_f[:], 1.0), writes=[b_cst2])
    fw.op(dve, lambda h: h.memset(eps_ln[:], LN_EPS / (ALPHA * ALPHA)), writes=[b_cst2])
    fw.op(dve, lambda h: h.memset(eps_rms[:], RMS_EPS), writes=[b_cst2])
    eps_ln1 = A("eps_ln1", [128, 1], F32)
    fw.op(dve, lambda h: h.memset(eps_ln1[:], LN_EPS), writes=[b_cst2])

    for l_ in range(nlayers):
        mk_scratch(("w_in", l_), w_in[l_])
        for bi_ in range(3):
            mk_scratch(("w_br", l_, bi_), w_branch[l_, bi_])
        mk_scratch(("w_o", l_), w_o[l_])
        mk_scratch(("w_up", l_), w_up[l_])
        mk_scratch(("w_down", l_), w_down[l_])

    cact = A("cact", [128, 8, nseq], F32); b_cact = Buf("cact")
    ACT(cact[:], c_sb[:], AF.Silu, [b_c], [b_cact])
    pm = PS.get()
    for l in range(DEPTH):
        for blk in range(24):
            slot = ring_i[0] % NSLOT
            ring_i[0] += 1
            stage = RING.f32(slot, 2048).rearrange("p (k n) -> p k n", k=8)
            fw.dma(sp, stage, w_ada[l].rearrange("(kc p) n -> p kc n", p=128)[:, :, blk * 256:(blk + 1) * 256],
                   writes=RING.b(slot))
            for jj in range(2):
                j = blk * 2 + jj
                col = (l * 48 + j) * nseq
                for kc in range(8):
                    MM(PS.t[:, pm, col:col + nseq], stage[:, kc, jj * 128:(jj + 1) * 128], cact[:, kc, :],
                       kc == 0, kc == 7, RING.b(slot) + [b_cact], [PS.bufs[pm]])
    for l in range(DEPTH):
        n48 = 48 * nseq
        TT(mod_sb[:, l * n48:(l + 1) * n48].rearrange("p (j s) -> p j s", s=nseq),
           PS.t[:, pm, l * n48:(l + 1) * n48].rearrange("p (j s) -> p j s", s=nseq),
           vecs_sb[:, l, V_BADA:V_BADA + 48].unsqueeze(2).to_broadcast([128, 48, nseq]),
           ALU.add, [PS.bufs[pm], b_vecs], [b_mod])
        for kind in (1, 4):
            a0 = (l * 6 + kind) * 8 * nseq
            TS(mod_sb[:, a0:a0 + 8 * nseq], mod_sb[:, a0:a0 + 8 * nseq], 1.0, None, ALU.add, None, [b_mod], [b_mod])
        for kind in (2, 5):
            a0 = (l * 6 + kind) * 8 * nseq
            TS(mod_sb[:, a0:a0 + 8 * nseq], mod_sb[:, a0:a0 + 8 * nseq], 1.0 / ALPHA, None, ALU.mult, None,
               [b_mod], [b_mod])
    PS.put(pm)

    def load_wpool(l):
        for gq in range(4):
            fw.dma(pool, wpool_sb[:, :, gq, :], w_pool[l, gq].rearrange("(i p) d -> p i d", p=128), writes=[b_wpool])

    def load_attw(l):
        fw.dma(pool, wuqn_sb[:], w_uqn[l].rearrange("(i p) n -> p i n", p=128), writes=[b_wuqn])
        fw.dma(pool, wuqr_sb[:], w_uqr[l].rearrange("(i p) n -> p i n", p=128), writes=[b_wuqr])
        fw.dma(pool, wuqrs_sb[:], w_uqrs[l].rearrange("(i p) n -> p i n", p=128), writes=[b_wuqrs])
        fw.dma(pool, wukT_sb[:], w_ukT[l], writes=[b_wukT])
        fw.dma(pool, wuv_sb[:], w_uv[l], writes=[b_wuv])

    def load_sguw(l):
        fw.dma(pool, wsT_sb[:], wsT[l], writes=[b_wsT])
        fw.dma(pool, bsT_sb[:], bsT[l], writes=[b_bsT])
        fw.dma(pool, gsg_sb[:], gsg[l].partition_broadcast(128), writes=[b_gsg])
        fw.dma(pool, bsg_sb[:], bsg[l].partition_broadcast(128), writes=[b_bsg])
        TT(wsT_sb[:].rearrange("p (g t) -> p g t", g=8), wsT_sb[:].rearrange("p (g t) -> p g t", g=8),
           mask_bf[:].unsqueeze(1).to_broadcast([128, 8, 128]), ALU.mult, [b_wsT, b_cst2], [b_wsT])

    def ln_stats(eps_tile):
        xb = lambda c: AR.bf16(c // 2, T, (c % 2) * T)
        xq = lambda c: AR.bf16(4 + c // 2, T, (c % 2) * T)
        pmn = PS.get()
        psq = PS.get()
        for pp in range(4):
            xb2 = AR.bf16(pp, 2 * T).rearrange("p (a n) -> p a n", a=2)
            xq2 = AR.bf16(4 + pp, 2 * T).rearrange("p (a n) -> p a n", a=2)
            xin = XC[0][:, 2 * pp:2 * pp + 2, :]
            rb = [XC[1][2 * pp], XC[1][2 * pp + 1]]
            CP(xb2, xin, rb, AR.b(pp))
            ACT(xq2, xin, AF.Square, rb, AR.b(4 + pp))
            for c in (2 * pp, 2 * pp + 1):
                MM(psb(pmn), onesm_bf[:], xb(c), c == 0, c == 7, AR.b(c // 2) + [b_cst2], [PS.bufs[pmn]])
                MM(psb(psq), onesm_bf[:], xq(c), c == 0, c == 7, AR.b(4 + c // 2) + [b_cst2], [PS.bufs[psq]])
        CP(st_mean[:], psb(pmn), [PS.bufs[pmn]], [b_mean])
        TT(st_var[:], st_mean[:], st_mean[:], ALU.mult, [b_mean], [b_var])
        TT(st_var[:], psb(psq), st_var[:], ALU.subtract, [PS.bufs[psq], b_var], [b_var])
        PS.put(pmn)
        PS.put(psq)
        ACT(st_var[:], st_var[:], AF.Ln, [b_var, b_cst2], [b_var], bias=eps_tile[:, 0:1])
        ACT(st_rstd[:], st_var[:], AF.Exp, [b_var], [b_rstd], scale=-0.5)

    def ln_apply(out_fn, out_bufs, scale_fn, bias_fn, extra_reads):
        for c in range(8):
            pg = 8 + (c % 4)
            t = AR.f32(pg, T)
            TT(t, XC[0][:, c, :], st_mean[:], ALU.subtract, [XC[1][c], b_mean], AR.b(pg), eng=pool)
            TT(t, t, st_rstd[:], ALU.mult, AR.b(pg) + [b_rstd], AR.b(pg))
            ACT(out_fn(c), t, AF.Identity, AR.b(pg) + extra_reads, out_bufs(c), scale=scale_fn(c), bias=bias_fn(c))

    def ln_mod(l, s, kind):
        ln_stats(eps_ln1)
        ln_apply(lambda c: h_sb[:, c, :], lambda c: [b_h[c]],
                 lambda c: mcol(l, 3 * kind + 1, c, s), lambda c: mcol(l, 3 * kind, c, s), [b_mod])

    def post_ln(l, goff, boff):
        ln_stats(eps_ln)
        ln_apply(lambda c: XC[0][:, c, :], lambda c: [XC[1][c]],
                 lambda c: vcol(l, goff, c), lambda c: vcol(l, boff, c), [b_vecs])

    def proj8(w_slot, col0, rhs_fn, rhs_bufs, ncols=512):
        b = PS.get()
        wv = ring_view(w_slot, 8, ncols)
        for k in range(8):
            MM(psb(b), wv[:, k, col0:col0 + 128], rhs_fn(k), k == 0, k == 7,
               RING.b(w_slot) + rhs_bufs(k), [PS.bufs[b]])
        return b

    def proj8_kouter(specs, rhs_fn, rhs_bufs):
        banks = [PS.get() for _ in specs]
        for k in range(8):
            for (w_slot, col0, ncols), b in zip(specs, banks):
                wv = ring_view(w_slot, 8, ncols)
                MM(psb(b), wv[:, k, col0:col0 + 128], rhs_fn(k), k == 0, k == 7,
                   RING.b(w_slot) + rhs_bufs(k), [PS.bufs[b]])
        return banks

    def w_cols(w_l, c0, n):
        return w_l.rearrange("(kc p) n -> p kc n", p=128)[:, :, c0:c0 + n]

    def branch_out(l, bi, first):
        for half in range(2):
            ws = wload(scols(("w_br", l, bi), half * 512, 512))
            gs = wload(scols(("w_in", l), bi * D + half * 512, 512))
            for dd in range(4):
                d = half * 4 + dd
                by = proj8(ws, dd * 128, lambda k: branch(k), lambda k: b_branch(k))
                bg = proj8(gs, dd * 128, lambda k: h_sb[:, k, :], lambda k: [b_h[k]])
                pg = 12 + (d % 2)
                sig = AR.f32(pg, T)
                ACT(sig, psb(bg), AF.Sigmoid, [PS.bufs[bg]], AR.b(pg))
                PS.put(bg)
                if first:
                    TT(merged(d), psb(by), sig, ALU.mult, [PS.bufs[by]] + AR.b(pg), b_merged(d))
                else:
                    TT(sig, psb(by), sig, ALU.mult, [PS.bufs[by]] + AR.b(pg), AR.b(pg))
                    TT(merged(d), merged(d), sig, ALU.add, b_merged(d) + AR.b(pg), b_merged(d))
                PS.put(by)

    TWO_PI = 2.0 * np.pi
    MAGIC = 12582912.0
    CW1 = 6.28125
    CW2 = float(np.float32(TWO_PI - CW1))
    CW3 = float(TWO_PI - CW1 - np.float64(np.float32(TWO_PI - CW1)))
    PI_LO = 3.1415925

    def rope_tables(s, g):
        t0 = g * T
        pi_ = AR.t[0:64, 0:T].bitcast(I32)
        fw.dma(pool, pi_, pos[s:s + 1, t0:t0 + T].partition_broadcast(64), writes=AR.b(0))
        ang = AR.f32(1, T, parts=64)
        CP(ang, pi_, AR.b(0), AR.b(1))
        TS(ang, ang, const_sb[0:64, C_FREQ:C_FREQ + 1], None, ALU.mult, None, AR.b(1) + [b_const], AR.b(1))
        for which, outt, ob in ((0, sinpm, b_sin), (1, cos2, b_cos)):
            kk = AR.f32(2, T, parts=64)
            r = AR.f32(3, T, parts=64)
            if which == 0:
                TS(kk, ang, float(1.0 / TWO_PI), MAGIC, ALU.mult, ALU.add, AR.b(1), AR.b(2))
            else:
                TS(kk, ang, float(1.0 / TWO_PI), 0.25, ALU.mult, ALU.add, AR.b(1), AR.b(2))
                TS(kk, kk, MAGIC, None, ALU.add, None, AR.b(2), AR.b(2))
            TS(kk, kk, -MAGIC, None, ALU.add, None, AR.b(2), AR.b(2))
            STT(r, kk, -CW1, ang, ALU.mult, ALU.add, AR.b(1) + AR.b(2), AR.b(3))
            STT(r, kk, -CW2, r, ALU.mult, ALU.add, AR.b(2) + AR.b(3), AR.b(3))
            STT(r, kk, -CW3, r, ALU.mult, ALU.add, AR.b(2) + AR.b(3), AR.b(3))
            if which == 1:
                TS(r, r, float(np.pi / 2), None, ALU.add, None, AR.b(3), AR.b(3))
            TS(r, r, PI_LO, -PI_LO, ALU.min, ALU.max, AR.b(3), AR.b(3))
            ACT(outt[:], r, AF.Sin, AR.b(3), [ob])
        TS(sinpm[:], sinpm[:], const_sb[0:64, C_SGN:C_SGN + 1], None, ALU.mult, None, [b_sin, b_const], [b_sin])

    def mixer(l, s, g, nxt_l):
        t0 = g * T
        fw.phase = 'ln_mod_t'
        ln_mod(l, s, 0)
        fw.phase = 'pool'

        for half in range(2):
            wa = wload(scols(("w_in", l), OFF_POOL + half * 512, 512))
            pre = None
            if half == 0:
                pre = proj8_kouter([(wa, cc_ * 128, 512) for cc_ in range(4)], lambda k: h_sb[:, k, :], lambda k: [b_h[k]])
            for cc in range(4):
                c = half * 4 + cc
                w = POOL_W[c]
                ba = pre[cc] if pre else proj8(wa, cc * 128, lambda k: h_sb[:, k, :], lambda k: [b_h[k]])
                P0, P1, P2 = 0 + 3 * (c % 2), 1 + 3 * (c % 2), 2 + 3 * (c % 2)
                a = AR.f32(P0, 528)
                CP(a[:, 0:16], phalo[l][:, c, :], [b_phalo[l]], AR.b(P0))
                ACT(a[:, 16:528], psb(ba), AF.Copy, [PS.bufs[ba]], AR.b(P0))
                PS.put(ba)
                CP(phalo[l][:, c, :], a[:, 512:528], AR.b(P0), [b_phalo[l]])
                cur, curp = a, P0
                sh = 1
                lo = 1
                others = [P1, P2]
                oi = 0
                while sh < w:
                    np_ = others[oi % 2]
                    oi += 1
                    nt = AR.f32(np_, 528)
                    TT(nt[:, lo:528], cur[:, lo:528], cur[:, lo - sh:528 - sh], ALU.add, AR.b(curp), AR.b(np_))
                    cur, curp = nt, np_
                    sh *= 2
                    lo = lo + sh
                if g == 0:
                    wi = {2: 0, 4: 1, 8: 2, 16: 3}[w]
                    TT(cur[:, 16:32], cur[:, 16:32], const_sb[:, C_CORR + wi * 16:C_CORR + wi * 16 + 16], ALU.mult,
                       AR.b(curp) + [b_const], AR.b(curp))
                STT(branch(c), cur[:, 16:528], 1.0 / w, a[:, 16:528], ALU.mult, ALU.subtract,
                    AR.b(curp) + AR.b(P0), b_branch(c))
        for gq in range(4):
            bm = []
            for j in range(2):
                b = PS.get()
                for i in range(2):
                    MM(psb(b), wpool_sb[:, i, gq, j * 128:(j + 1) * 128], branch(2 * gq + i), i == 0, i == 1,
                       [b_wpool] + b_branch(2 * gq + i), [PS.bufs[b]])
                bm.append(b)
            for j in range(2):
                ACT(branch(2 * gq + j), psb(bm[j]), AF.Identity, [PS.bufs[bm[j]], b_vecs], b_branch(2 * gq + j),
                    scale=vcol(l, V_SPOOL, 2 * gq + j))
                PS.put(bm[j])
        if nxt_l is not None:
            load_wpool(nxt_l)
        fw.phase = 'bout_a'
        branch_out(l, 0, True)
        fw.phase = 'mla_proj'
        if l == 0:
            rope_tables(s, g)

        wq = wload(scols(("w_in", l), OFF_CQ, 448))
        wqv = RING.bf16(wq, 8 * 448).rearrange("p (k n) -> p k n", k=8)
        wk = wload((w_krs[l].rearrange("(kc p) n -> p kc n", p=128), []), eng=pool)
        wkv = RING.bf16(wk, 8 * 64).rearrange("p (k n) -> p k n", k=8)
        hb = lambda k: [b_h[k]]
        bq = []
        pms = PS.get()
        for i in range(2):
            b = PS.get()
            for k in range(8):
                MM(psb(b), wqv[:, k, i * 128:(i + 1) * 128], h_sb[:, k, :], k == 0, k == 7, RING.b(wq) + hb(k), [PS.bufs[b]])
            sq = AR.bf16(i, T)
            ACT(sq, psb(b), AF.Square, [PS.bufs[b]], AR.b(i))
            MM(psb(pms), ones_bf[:], sq, i == 0, i == 1, AR.b(i) + [b_cst2], [PS.bufs[pms]])
            bq.append(b)
        rq = AR.f32(2, T)
        ACT(rq, psb(pms), AF.Ln, [PS.bufs[pms], b_cst2], AR.b(2), scale=1.0 / 256, bias=eps_rms[:, 0:1])
        PS.put(pms)
        ACT(rq, rq, AF.Exp, AR.b(2), AR.b(2), scale=-0.5)
        for i in range(2):
            STT(cqn[:, i, :], psb(bq[i]), vcol(l, V_GQ, i), rq, ALU.mult, ALU.mult,
                [PS.bufs[bq[i]], b_vecs] + AR.b(2), [b_cqn])
            PS.put(bq[i])
        b = PS.get()
        pms = PS.get()
        for k in range(8):
            MM(psb(b), wqv[:, k, 256:384], h_sb[:, k, :], k == 0, k == 7, RING.b(wq) + hb(k), [PS.bufs[b]])
        sq = AR.bf16(3, T)
        ACT(sq, psb(b), AF.Square, [PS.bufs[b]], AR.b(3))
        MM(psb(pms), ones_bf[:], sq, True, True, AR.b(3) + [b_cst2], [PS.bufs[pms]])
        rk = AR.f32(4, T)
        ACT(rk, psb(pms), AF.Ln, [PS.bufs[pms], b_cst2], AR.b(4), scale=1.0 / 128, bias=eps_rms[:, 0:1])
        PS.put(pms)
        ACT(rk, rk, AF.Exp, AR.b(4), AR.b(4), scale=-0.5)
        STT(ckvnT[l][:, t0:t0 + T], psb(b), vcol(l, V_GKV, 0), rk, ALU.mult, ALU.mult,
            [PS.bufs[b], b_vecs] + AR.b(4), [b_kc[l][g]])
        PS.put(b)
        for tt in range(4):
            b = PS.get()
            tp = PS.t[:, b, 0:64].bitcast(BF16)
            fw.op(pe, lambda h, tp=tp, tt=tt: h.transpose(tp, ckvnT[l][:, t0 + tt * 128:t0 + (tt + 1) * 128], ident_bf[:]),
                  [b_kc[l][g], b_cst2], [PS.bufs[b]])
            CP(Vc[l][:, 4 * g + tt, :], tp, [PS.bufs[b]], [b_vc[l][g]])
            PS.put(b)
        b1 = PS.get()
        b2 = PS.get()
        for k in range(8):
            MM(psb(b1, parts=64), wqv[:, k, 384:448], h_sb[:, k, :], k == 0, k == 7, RING.b(wq) + hb(k), [PS.bufs[b1]])
        for k in range(8):
            MM(psb(b2, parts=64), wkv[:, k, :], h_sb[:, k, :], k == 0, k == 7, RING.b(wk) + hb(k), [PS.bufs[b2]])
        t1 = AR.f32(5, T, parts=64)
        t2 = AR.f32(6, T, parts=64)
        TT(t1, psb(b1, parts=64), cos2[:], ALU.mult, [PS.bufs[b1], b_cos], AR.b(5))
        TT(t2, psb(b2, parts=64), sinpm[:], ALU.mult, [PS.bufs[b2], b_sin], AR.b(6))
        PS.put(b1)
        PS.put(b2)
        TT(krotT[l][:, t0:t0 + T], t1, t2, ALU.add, AR.b(5) + AR.b(6), [b_kr[l][g]])

        ntile = 4 * g + 4
        fw.phase = 'mla_heads'
        for hh in range(8):
            par = hh % 2
            b = PS.get()
            for i in range(2):
                MM(psb(b), wuqn_sb[:, i, hh * 128:(hh + 1) * 128], cqn[:, i, :], i == 0, i == 1, [b_wuqn, b_cqn], [PS.bufs[b]])
            qn = AR.bf16(0, T, par * T)
            ACT(qn, psb(b), AF.Copy, [PS.bufs[b]], AR.b(0))
            PS.put(b)
            b = PS.get()
            MM(psb(b), wukT_sb[:, hh * 128:(hh + 1) * 128], qn, True, True, [b_wukT] + AR.b(0), [PS.bufs[b]])
            qp = AR.bf16(1, T, par * T)
            ACT(qp, psb(b), AF.Copy, [PS.bufs[b]], AR.b(1))
            PS.put(b)
            b1 = PS.get()
            b2 = PS.get()
            for i in range(2):
                MM(psb(b1, parts=64), wuqr_sb[:, i, hh * 64:(hh + 1) * 64], cqn[:, i, :], i == 0, i == 1, [b_wuqr, b_cqn], [PS.bufs[b1]])
            for i in range(2):
                MM(psb(b2, parts=64), wuqrs_sb[:, i, hh * 64:(hh + 1) * 64], cqn[:, i, :], i == 0, i == 1, [b_wuqrs, b_cqn], [PS.bufs[b2]])
            t1 = AR.f32(5, T, parts=64)
            t2 = AR.f32(6, T, parts=64)
            TT(t1, psb(b1, parts=64), cos2[:], ALU.mult, [PS.bufs[b1], b_cos], AR.b(5))
            TT(t2, psb(b2, parts=64), sinpm[:], ALU.mult, [PS.bufs[b2], b_sin], AR.b(6))
            PS.put(b1)
            PS.put(b2)
            qr = AR.bf16(2, T, par * T, parts=64)
            TT(qr, t1, t2, ALU.add, AR.b(5) + AR.b(6), AR.b(2))
            po = PS.get()
            pd = PS.get()
            j = 0
            it = 0
            while j < ntile:
                npair = 2 if (j + 1 < 4 * g) else 1
                ppg = 8 + 2 * (it % 2)
                it += 1
                if npair == 2:
                    bs = PS.get_pair()
                else:
                    bs = PS.get()
                for jj in range(npair):
                    jt = j + jj
                    qlo = max(0, jt - 4 * g) * 128
                    kg = jt // 4
                    MM(PS.t[:, bs + jj, qlo:T], ckvnT[l][:, jt * 128:(jt + 1) * 128], qp[:, qlo:T], True, False,
                       [b_kc[l][kg]] + AR.b(1), [PS.bufs[bs + jj]])
                    MM(PS.t[:, bs + jj, qlo:T], krotT[l][:, jt * 128:(jt + 1) * 128], qr[:, qlo:T], False, True,
                       [b_kr[l][kg]] + AR.b(2), [PS.bufs[bs + jj]])
                qlo = max(0, j - 4 * g) * 128 if npair == 1 else 0
                if npair == 2:
                    pT = AR.t[:, ppg * 544:ppg * 544 + 512].bitcast(BF16).rearrange("p (a n) -> p a n", a=2)
                    ACT(pT, PS.t[:, bs:bs + 2, :], AF.Exp, [PS.bufs[bs], PS.bufs[bs + 1]], AR.b(ppg), scale=ATT_SCALE)
                    pTs = [pT[:, 0, :], pT[:, 1, :]]
                else:
                    pT1 = AR.bf16(ppg, T)
                    ACT(pT1[:, qlo:T], PS.t[:, bs, qlo:T], AF.Exp, [PS.bufs[bs]], AR.b(ppg), scale=ATT_SCALE)
                    if j >= 4 * g:
                        fw.op(dve, lambda h, pT1=pT1, qlo=qlo: h.memset(pT1[64:128, qlo:qlo + 64], 0.0), [], AR.b(ppg))
                    pTs = [pT1]
                for jj in range(npair):
                    PS.put(bs + jj)
                for jj in range(npair):
                    jt = j + jj
                    ql = max(0, jt - 4 * g) * 128
                    kg = jt // 4
                    MM(PS.t[:, po, ql:T], Vc[l][:, jt, :], pTs[jj][:, ql:T], jt == 0, jt == ntile - 1,
                       [b_vc[l][kg]] + AR.b(ppg), [PS.bufs[po]], signal=False)
                    MM(PS.t[:, pd, ql:T], ones_bf[:], pTs[jj][:, ql:T], jt == 0, jt == ntile - 1,
                       [b_cst2] + AR.b(ppg), [PS.bufs[pd]], signal=True)
                j += npair
            rden = AR.f32(7, T)
            ACT(rden, psb(pd), AF.Ln, [PS.bufs[pd]], AR.b(7))
            ACT(rden, rden, AF.Exp, AR.b(7), AR.b(7), scale=-1.0)
            PS.put(pd)
            op_ = AR.bf16(3, T, par * T)
            TT(op_, psb(po), rden, ALU.mult, [PS.bufs[po]] + AR.b(7), AR.b(3))
            PS.put(po)
            b = PS.get()
            MM(psb(b), wuv_sb[:, hh * 128:(hh + 1) * 128], op_, True, True, [b_wuv] + AR.b(3), [PS.bufs[b]])
            ACT(branch(hh), psb(b), AF.Copy, [PS.bufs[b]], b_branch(hh))
            PS.put(b)
        if nxt_l is not None:
            load_attw(nxt_l)
        fw.phase = 'bout_b'
        branch_out(l, 1, False)
        fw.phase = 'sgu'

        wv_s = [wload(scols(("w_in", l), OFF_SG + D + nn * 512, 512)) for nn in range(2)]
        vn_all = lambda tb: AR.bf16(4 + tb, 1024)
        for tb in range(4):
            bp = PS.get_pair()
            for nn in range(2):
                wv = ring_view(wv_s[nn], 8, 512)
                for k in range(8):
                    MM(PS.t[:, bp + nn, :], h_sb[:, k, tb * 128:(tb + 1) * 128], wv[:, k, :], k == 0, k == 7,
                       RING.b(wv_s[nn]) + [b_h[k]], [PS.bufs[bp + nn]])
            gpg = 0 + 2 * (tb % 2)
            gv = AR.t[:, gpg * 544:gpg * 544 + 1024]
            ACT(gv.rearrange("p (a n) -> p a n", a=2), PS.t[:, bp:bp + 2, :], AF.Gelu,
                [PS.bufs[bp], PS.bufs[bp + 1]], AR.b(gpg, 2))
            PS.put(bp)
            PS.put(bp + 1)
            stt_ = AR.f32(12, 16)
            for i in range(2):
                fw.op(dve, lambda h, i=i, gv=gv, stt_=stt_: h.bn_stats(out=stt_[:, i * 6:(i + 1) * 6], in_=gv[:, i * 512:(i + 1) * 512]),
                      AR.b(gpg, 2), AR.b(12))
            mvv = AR.f32(12, 2, off=16)
            fw.op(dve, lambda h, mvv=mvv, stt_=stt_: h.bn_aggr(out=mvv, in_=stt_[:, 0:12].rearrange("p (a n) -> p a n", a=2)),
                  AR.b(12), AR.b(12))
            rs = AR.f32(12, 1, off=20)
            ACT(rs, mvv[:, 1:2], AF.Sqrt, AR.b(12) + [b_cst2], AR.b(12), bias=eps_ln1[:, 0:1])
            fw.op(dve, lambda h, rs=rs: h.reciprocal(out=rs, in_=rs), AR.b(12), AR.b(12))
            TS(gv, gv, mvv[:, 0:1], rs, ALU.subtract, ALU.mult, AR.b(gpg, 2) + AR.b(12), AR.b(gpg, 2))
            TT(gv, gv, gsg_sb[:], ALU.mult, AR.b(gpg, 2) + [b_gsg], AR.b(gpg, 2))
            TT(vn_all(tb), gv, bsg_sb[:], ALU.add, AR.b(gpg, 2) + [b_bsg], AR.b(4 + tb))
        for half in range(2):
            wu = wload(scols(("w_in", l), OFF_SG + half * 512, 512))
            for gg in range(4):
                gq = half * 4 + gg
                bz = PS.get()
                for tb in range(4):
                    MM(PS.t[:, bz, tb * 128:(tb + 1) * 128], vn_all(tb)[:, gq * 128:(gq + 1) * 128],
                       wsT_sb[:, gq * 128:(gq + 1) * 128], True, False, AR.b(4 + tb) + [b_wsT], [PS.bufs[bz]])
                    MM(PS.t[:, bz, tb * 128:(tb + 1) * 128], ones_bf[0:1, :], bsT_sb[0:1, gq * 128:(gq + 1) * 128],
                       False, True, [b_cst2, b_bsT], [PS.bufs[bz]], signal=(tb == 3))
                bu = proj8(wu, gg * 128, lambda k: h_sb[:, k, :], lambda k: [b_h[k]])
                pg = 0 + (gq % 2)
                ug = AR.f32(pg, T)
                ACT(ug, psb(bu), AF.Gelu, [PS.bufs[bu]], AR.b(pg))
                PS.put(bu)
                TT(branch(gq), ug, psb(bz), ALU.mult, AR.b(pg) + [PS.bufs[bz]], b_branch(gq))
                PS.put(bz)
        if nxt_l is not None:
            load_sguw(nxt_l)
        fw.phase = 'bout_c'
        branch_out(l, 2, False)
        fw.phase = 'wo'

        for d in range(8):
            CP(branch(d), merged(d), b_merged(d), b_branch(d))
        for half in range(2):
            wo = wload(scols(("w_o", l), half * 512, 512))
            for dd in range(4):
                d = half * 4 + dd
                by = proj8(wo, dd * 128, lambda k: branch(k), lambda k: b_branch(k))
                STT(XC[0][:, d, :], psb(by), mcol(l, 2, d, s), XC[0][:, d, :], ALU.mult, ALU.add,
                    [PS.bufs[by], b_mod, XC[1][d]], [XC[1][d]])
                PS.put(by)
        fw.phase = 'post_ln_t'
        post_ln(l, V_LNTG, V_LNTB)

    def ffn(l, s, g):
        fw.phase = 'ln_mod_f'
        ln_mod(l, s, 1)
        fw.phase = 'ffn_up'
        nblk = (NF + 3) // 4

        def ffn_stage2(f):
            av = AR.f32(2 * (f % 3), T)
            ag = AR.f32(2 * (f % 3) + 1, T)
            pav, pag = 2 * (f % 3), 2 * (f % 3) + 1
            ACT(ag, ag, AF.Silu, AR.b(pag), AR.b(pag))
            TT(actT(f), ag, av, ALU.mult, AR.b(pag) + AR.b(pav), b_actT(f), eng=pool)

        for fb in range(nblk):
            nfc = min(4, NF - fb * 4)
            wvs = wload(scols(("w_up", l), fb * 512, nfc * 128))
            wgs = wload(scols(("w_up", l), D_FF + fb * 512, nfc * 128))
            pre = {}
            if fb == 0:
                bks = proj8_kouter([(wsl_, ff_ * 128, nfc * 128) for ff_ in range(2) for wsl_ in (wvs, wgs)],
                                   lambda k: h_sb[:, k, :], lambda k: [b_h[k]])
                pre = {(0, 0): bks[0], (0, 1): bks[1], (1, 0): bks[2], (1, 1): bks[3]}
            for ff in range(nfc):
                f = fb * 4 + ff
                for which, wsl in ((0, wvs), (1, wgs)):
                    ch = which * NF + f
                    wv = ring_view(wsl, 8, nfc * 128)
                    if (ff, which) in pre:
                        b = pre[(ff, which)]
                    else:
                        b = PS.get()
                        for k in range(8):
                            MM(psb(b), wv[:, k, ff * 128:(ff + 1) * 128], h_sb[:, k, :], k == 0, k == 7,
                               RING.b(wsl) + [b_h[k]], [PS.bufs[b]])
                    pa = 2 * (f % 3) + which
                    acc = AR.f32(pa, T)
                    zh = zhalo[l][:, ch, :]
                    bzh = [b_zhalo[l][ch]]
                    w0, w1, w2 = vcol(l, V_CW, ch), vcol(l, V_CW + 2 * NF, ch), vcol(l, V_CW + 4 * NF, ch)
                    ACT(acc, psb(b), AF.Identity, [PS.bufs[b], b_vecs], AR.b(pa), scale=w2, bias=vcol(l, V_CB, ch))
                    STT(acc[:, 1:T], psb(b, 0, T - 1), w1, acc[:, 1:T], ALU.mult, ALU.add,
                        [PS.bufs[b], b_vecs] + AR.b(pa), AR.b(pa))
                    STT(acc[:, 2:T], psb(b, 0, T - 2), w0, acc[:, 2:T], ALU.mult, ALU.add,
                        [PS.bufs[b], b_vecs] + AR.b(pa), AR.b(pa))
                    STT(acc[:, 0:1], zh[:, 1:2], w1, acc[:, 0:1], ALU.mult, ALU.add, bzh + [b_vecs] + AR.b(pa), AR.b(pa))
                    STT(acc[:, 0:2], zh, w0, acc[:, 0:2], ALU.mult, ALU.add, bzh + [b_vecs] + AR.b(pa), AR.b(pa))
                    ACT(zh, psb(b, T - 2, T), AF.Copy, [PS.bufs[b]], bzh)
                    PS.put(b)
                if f > 0:
                    ffn_stage2(f - 1)
        ffn_stage2(NF - 1)
        fw.phase = 'ffn_down'
        for d in range(8):
            wd = wload(scols(("w_down", l), d * 128, 128))
            wv = RING.bf16(wd, NF * 128).rearrange("p (k n) -> p k n", k=NF)
            b = PS.get()
            for f in range(NF):
                MM(psb(b), wv[:, f, :], actT(f), f == 0, f == NF - 1, RING.b(wd) + b_actT(f), [PS.bufs[b]])
            STT(XC[0][:, d, :], psb(b), mcol(l, 5, d, s), XC[0][:, d, :], ALU.mult, ALU.add,
                [PS.bufs[b], b_mod, XC[1][d]], [XC[1][d]])
            PS.put(b)
        fw.phase = 'post_ln_f'
        post_ln(l, V_LNFG, V_LNFB)

    steps = [(s, g, l) for s in range(nseq) for g in range(ngroups) for l in range(nlayers)]
    groups = [(s, g) for s in range(nseq) for g in range(ngroups)]
    load_wpool(0)
    load_attw(0)
    load_sguw(0)

    def load_x(gi):
        s_, g_ = groups[gi]
        fw.dma(pool, xs_t[gi % 2][:], xT[s_].rearrange("(kc p) n -> p kc n", p=128)[:, :, g_ * T:(g_ + 1) * T],
               writes=xs_b[gi % 2])

    load_x(0)
    for idx, (s, g, l) in enumerate(steps):
        nxt_l = steps[idx + 1][2] if idx + 1 < len(steps) else None
        gi = s * ngroups + g
        XC[0], XC[1] = xs_t[gi % 2], xs_b[gi % 2]
        t0 = g * T
        if l == 0 and g == 0:
            for ll in range(nlayers):
                fw.op(dve, lambda h, ll=ll: h.memset(phalo[ll][:], 0.0), [], [b_phalo[ll]])
                fw.op(dve, lambda h, ll=ll: h.memset(zhalo[ll][:], 0.0), [], b_zhalo[ll])
        mixer(l, s, g, nxt_l)
        if dbg == (s, g, l, "mix"):
            fw.dma(pool, dbg_out, XC[0][:], reads=XC[1])
        if l == nlayers - 1 and gi + 1 < len(groups):
            load_x(gi + 1)
        ffn(l, s, g)
        if dbg == (s, g, l, "ffn"):
            fw.dma(pool, dbg_out, XC[0][:], reads=XC[1])
        if l == nlayers - 1:
            fw.dma(pool, outT[s].rearrange("(kc p) n -> p kc n", p=128)[:, :, t0:t0 + T], XC[0][:], reads=XC[1])
    fw.final_fence(sp)
    fw.emit()
    return nc, fw


def _host_prep(inp):
    f = lambda a: np.ascontiguousarray(np.asarray(a, dtype=np.float32))
    L = DEPTH
    w_in = np.asarray(inp["w_in"], np.float32)
    w_uq = np.asarray(inp["w_uq"], np.float32)
    w_ukv = np.asarray(inp["w_ukv"], np.float32)
    perm = np.concatenate([np.arange(32, 64), np.arange(0, 32)])
    shared = {
        "w_ada": f(inp["w_ada"]),
        "w_in": f(w_in),
        "w_krs": f(w_in[:, :, OFF_KR:OFF_KR + 64][:, :, perm]),
        "w_pool": f(inp["w_pool"]),
        "w_uqn": f(w_uq[:, :, :, 0:128].reshape(L, 256, 1024)),
        "w_uqr": f(w_uq[:, :, :, 128:192].reshape(L, 256, 512)),
        "w_uqrs": f(w_uq[:, :, :, 128:192][:, :, :, perm].reshape(L, 256, 512)),
        "w_ukT": f(np.transpose(w_ukv[:, :, :, 0:128], (0, 3, 2, 1)).reshape(L, 128, 1024)),
        "w_uv": f(w_ukv[:, :, :, 128:256].reshape(L, 128, 1024)),
        "wsT": f(np.transpose(np.asarray(inp["w_s"], np.float32), (0, 3, 1, 2)).reshape(L, 128, 1024)),
        "bsT": f(np.transpose(np.asarray(inp["b_s"], np.float32), (0, 2, 1)).reshape(L, 1, 1024)),
        "gsg": f(np.asarray(inp["g_sg"], np.float32).reshape(L, 1, 1024)),
        "bsg": f(np.asarray(inp["b_sg"], np.float32).reshape(L, 1, 1024)),
        "w_branch": f(inp["w_branch"]),
        "w_o": f(inp["w_o"]),
        "w_up": f(inp["w_up"]),
        "w_down": f(inp["w_down"]),
    }
    pp = lambda v, n: np.transpose(np.asarray(v, np.float32).reshape(L, n, 128), (0, 2, 1))
    vecs = np.zeros((L, 128, NV), np.float32)
    vecs[:, :, V_SPOOL:V_SPOOL + 8] = pp(inp["s_pool"], 8)
    vecs[:, :, V_GQ:V_GQ + 2] = pp(inp["g_q"], 2)
    vecs[:, :, V_GKV:V_GKV + 1] = pp(inp["g_kv"], 1)
    vecs[:, :, V_LNTG:V_LNTG + 8] = pp(inp["ln_t_g"], 8)
    vecs[:, :, V_LNTB:V_LNTB + 8] = pp(inp["ln_t_b"], 8)
    vecs[:, :, V_LNFG:V_LNFG + 8] = pp(inp["ln_f_g"], 8)
    vecs[:, :, V_LNFB:V_LNFB + 8] = pp(inp["ln_f_b"], 8)
    cw = np.asarray(inp["conv_w"], np.float32)
    for k in range(3):
        vecs[:, :, V_CW + k * 44:V_CW + (k + 1) * 44] = pp(cw[:, k], 44)
    vecs[:, :, V_CB:V_CB + 44] = pp(inp["conv_b"], 44)
    vecs[:, :, V_BADA:V_BADA + 48] = pp(inp["b_ada"], 48)
    shared["vecs"] = vecs
    consts = np.zeros((128, NCONST), np.float32)
    consts[:, C_IDENT:C_IDENT + 128] = np.eye(128, dtype=np.float32)
    consts[:, C_MASK:C_MASK + 128] = np.triu(np.ones((128, 128), np.float32))
    freqs = (10000.0 ** (-np.arange(0, 64, 2, dtype=np.float32) / 64)).astype(np.float32)
    consts[0:64, C_FREQ] = np.concatenate([freqs, freqs])
    consts[0:32, C_SGN] = -1.0
    consts[32:64, C_SGN] = 1.0
    for wi, w in enumerate((2, 4, 8, 16)):
        tt = np.arange(16)
        consts[:, C_CORR + wi * 16:C_CORR + (wi + 1) * 16] = (w / np.minimum(tt + 1, w)).astype(np.float32)[None, :]
    shared["consts"] = consts
    return shared


_CACHE = {}


def kernel(**inputs):
    x = np.asarray(inputs["x"], np.float32)
    c = np.asarray(inputs["c"], np.float32)
    pos = np.asarray(inputs["pos"], np.int32)
    shared = _host_prep(inputs)
    key = "full"
    if key not in _CACHE:
        _CACHE[key] = build_program()
    nc, _ = _CACHE[key]
    in_maps = []
    for core in range(NCORES):
        b0 = core * SPC
        m = dict(shared)
        m["xT"] = np.ascontiguousarray(np.transpose(x[b0:b0 + SPC], (0, 2, 1)))
        m["cT"] = np.ascontiguousarray(np.transpose(c[b0:b0 + SPC].reshape(SPC, 8, 128), (2, 1, 0)))
        m["pos"] = np.ascontiguousarray(pos[b0:b0 + SPC])
        in_maps.append(m)
    res = run_bass_kernel_spmd(nc, in_maps, core_ids=list(range(NCORES)))
    out = np.empty((BATCH, SEQ, D), np.float32)
    for core in range(NCORES):
        o = res.results[core]["outT"]
        out[core * SPC:(core + 1) * SPC] = np.transpose(o, (0, 2, 1))
    return out
```

```python
import numpy as np
import concourse.bass as bass
import concourse.mybir as mybir
from concourse.bass_utils import run_bass_kernel_spmd

F32 = mybir.dt.float32
BF16 = mybir.dt.bfloat16
I32 = mybir.dt.int32
AF = mybir.ActivationFunctionType
ALU = mybir.AluOpType

D = 1024
SEQ = 2048
BATCH = 32
DEPTH = 2
NCORES = 8
SPC = BATCH // NCORES
T = 512
NG = SEQ // T
D_FF = 2816
NF = D_FF // 128
OFF_POOL = 3 * D
OFF_CQ = OFF_POOL + D
OFF_CKV = OFF_CQ + 256
OFF_KR = OFF_CKV + 128
OFF_SG = OFF_KR + 64
IN_WIDTH = OFF_SG + 2 * D
ALPHA = (2 * DEPTH) ** 0.25
LN_EPS = 1e-5
RMS_EPS = 1e-6
ATT_SCALE = 192 ** -0.5
POOL_W = (2, 2, 4, 4, 8, 8, 16, 16)

V_SPOOL = 0
V_GQ = 8
V_GKV = 10
V_LNTG = 11
V_LNTB = 19
V_LNFG = 27
V_LNFB = 35
V_CW = 43
V_CB = V_CW + 132
V_BADA = V_CB + 44
NV = V_BADA + 48
C_IDENT = 0
C_MASK = 128
C_FREQ = 256
C_SGN = 257
C_CORR = 258
NCONST = C_CORR + 64

SEM_EPOCH = 8000


class Buf:
    __slots__ = ("w", "r", "name")

    def __init__(self, name=""):
        self.w = {}
        self.r = {}
        self.name = name


class Eng:
    def __init__(self, fw, name, is_pe=False):
        self.fw = fw
        self.name = name
        self.is_pe = is_pe
        self.prog = []
        self.waited = {}
        self.sems = []
        self.count = 0
        self._new_sem()

    def _new_sem(self):
        s = self.fw.new_sem(f"{self.name}_p{len(self.sems)}")
        self.sems.append(s)
        self.sem = s
        self.count = 0


class DmaSem:
    def __init__(self, fw, name):
        self.sem = fw.new_sem(name)
        self.value = 0


class FW:
    def __init__(self, nc, same_engine_sync=True):
        self.nc = nc
        self.same_sync = same_engine_sync
        self.pe = Eng(self, "pe", is_pe=True)
        self.act = Eng(self, "act")
        self.dve = Eng(self, "dve")
        self.pool = Eng(self, "pool")
        self.sp = Eng(self, "sp")
        self.dma_pool = {}
        self.retired = []
        self.n_inst = 0
        self.phase = 'setup'
        self.phases = {}

    def new_sem(self, name):
        return self.nc.alloc_semaphore(name=name)

    def _collect(self, eng, reads, writes):
        need = {}

        def add(d, raw):
            for s, v in d.items():
                if s in eng.sems:
                    if not (raw and self.same_sync and not eng.is_pe):
                        continue
                if need.get(s, 0) < v:
                    need[s] = v

        for b in reads:
            add(b.w, True)
        for b in writes:
            add(b.w, True)
            add(b.r, False)
        waits = []
        for s, v in need.items():
            if eng.waited.get(s, 0) < v:
                eng.waited[s] = v
                waits.append((s, v))
        return waits

    @staticmethod
    def _update(stamp, reads, writes):
        s, v = stamp
        for b in writes:
            b.w = {s: v}
            b.r = {}
        for b in reads:
            if b in writes:
                continue
            b.r[s] = v

    def op(self, eng, fn, reads=(), writes=(), signal=True):
        reads = list(reads)
        writes = list(writes)
        waits = self._collect(eng, reads, writes)
        sem = eng.sem
        stamp = (sem, eng.count + 1)
        self.n_inst += 1

        def emit(h, waits=waits, fn=fn, signal=signal, sem=sem):
            for s, v in waits:
                h.wait_ge(s, v)
            ins = fn(h)
            if signal:
                ins.then_inc(sem, 1)

        eng.prog.append(emit)
        self.phases.setdefault(eng.name, []).append(self.phase)
        self._update(stamp, reads, writes)
        if signal:
            eng.count += 1
            if eng.count >= SEM_EPOCH:
                eng._new_sem()
        return stamp

    def dma(self, eng, out_ap, in_ap, reads=(), writes=(), dsem=None, **kw):
        reads = list(reads)
        writes = list(writes)
        if dsem is None:
            dsem = self.get_dma_sem(eng)
        waits = self._collect(eng, reads, writes)
        if dsem.value > 0 and eng.waited.get(dsem.sem, 0) < dsem.value:
            eng.waited[dsem.sem] = dsem.value
            waits.append((dsem.sem, dsem.value))
        dsem.value += 16
        stamp = (dsem.sem, dsem.value)
        self.n_inst += 1

        def emit(h, waits=waits, out_ap=out_ap, in_ap=in_ap, kw=kw, sem=dsem.sem):
            for s, v in waits:
                h.wait_ge(s, v)
            h.dma_start(out=out_ap, in_=in_ap, **kw).then_inc(sem, 16)

        eng.prog.append(emit)
        self._update(stamp, reads, writes)
        return stamp

    def get_dma_sem(self, eng, n=8):
        key = eng.name
        if key not in self.dma_pool:
            self.dma_pool[key] = [[DmaSem(self, f"dma_{key}_{i}") for i in range(n)], 0]
        lst = self.dma_pool[key]
        i = lst[1] % n
        d = lst[0][i]
        if d.value >= 2048:
            self.retired.append(d)
            d = DmaSem(self, f"dma_{key}_{i}_{len(self.retired)}")
            lst[0][i] = d
        lst[1] += 1
        return d

    def final_fence(self, eng):
        sems = [d for lst, _ in self.dma_pool.values() for d in lst if d.value] + list(self.retired)

        def emit(h, sems=sems):
            for d in sems:
                h.wait_ge(d.sem, d.value)

        eng.prog.append(emit)

    def emit(self):
        with self.nc.Block() as block:
            @block.tensor
            def _(h):
                for f in self.pe.prog:
                    f(h)

            @block.scalar
            def _(h):
                for f in self.act.prog:
                    f(h)

            @block.vector
            def _(h):
                for f in self.dve.prog:
                    f(h)

            @block.gpsimd
            def _(h):
                for f in self.pool.prog:
                    f(h)

            @block.sync
            def _(h):
                for f in self.sp.prog:
                    f(h)


class Region:
    def __init__(self, nc, name, npages, page_words):
        self.t = nc.alloc_sbuf_tensor(name, [128, npages * page_words], F32)
        self.pw = page_words
        self.np = npages
        self.bufs = [Buf(f"{name}{i}") for i in range(npages)]

    def f32(self, page, n, off=0, parts=128):
        a = page * self.pw + off
        return self.t[0:parts, a:a + n]

    def bf16(self, page, n, off=0, parts=128):
        a = page * self.pw
        words = (off + n + 1) // 2
        v = self.t[0:parts, a:a + words].bitcast(BF16)
        return v[:, off:off + n]

    def b(self, page, npg=1):
        return self.bufs[page:page + npg]


class PsumPool:
    def __init__(self, nc):
        self.t = nc.alloc_psum_tensor("ps", [128, 8, 512], F32)
        self.bufs = [Buf(f"ps{i}") for i in range(8)]
        self.free = {i: i for i in range(8)}
        self.clock = 8

    def get(self):
        b = min(self.free, key=lambda k: self.free[k])
        del self.free[b]
        return b

    def get_pair(self):
        cands = [i for i in (0, 2, 4, 6) if i in self.free and i + 1 in self.free]
        assert cands, "no free psum pair"
        b = min(cands, key=lambda k: max(self.free[k], self.free[k + 1]))
        del self.free[b]
        del self.free[b + 1]
        return b

    def put(self, b):
        self.clock += 1
        self.free[b] = self.clock


def build_program(nseq=SPC, ngroups=NG, nlayers=DEPTH, dbg=None):
    nc = bass.Bass("TRN2", target_bir_lowering=False)
    dt_in = lambda name, shape, dt=F32: nc.dram_tensor(name, list(shape), dt, kind="ExternalInput").ap()
    xT = dt_in("xT", [nseq, D, SEQ])
    cT = dt_in("cT", [128, 8, nseq])
    pos = dt_in("pos", [nseq, SEQ], I32)
    consts = dt_in("consts", [128, NCONST])
    vecs = dt_in("vecs", [DEPTH, 128, NV])
    w_ada = dt_in("w_ada", [DEPTH, D, 6 * D])
    w_in = dt_in("w_in", [DEPTH, D, IN_WIDTH])
    w_krs = dt_in("w_krs", [DEPTH, D, 64])
    w_pool = dt_in("w_pool", [DEPTH, 4, 256, 256])
    w_uqn = dt_in("w_uqn", [DEPTH, 256, 1024])
    w_uqr = dt_in("w_uqr", [DEPTH, 256, 512])
    w_uqrs = dt_in("w_uqrs", [DEPTH, 256, 512])
    w_ukT = dt_in("w_ukT", [DEPTH, 128, 1024])
    w_uv = dt_in("w_uv", [DEPTH, 128, 1024])
    wsT = dt_in("wsT", [DEPTH, 128, 1024])
    bsT = dt_in("bsT", [DEPTH, 1, 1024])
    gsg = dt_in("gsg", [DEPTH, 1, 1024])
    bsg = dt_in("bsg", [DEPTH, 1, 1024])
    w_branch = dt_in("w_branch", [DEPTH, 3, D, D])
    w_o = dt_in("w_o", [DEPTH, D, D])
    w_up = dt_in("w_up", [DEPTH, D, 2 * D_FF])
    w_down = dt_in("w_down", [DEPTH, D_FF, D])
    outT = nc.dram_tensor("outT", [nseq, D, SEQ], F32, kind="ExternalOutput").ap()
    dbg_out = None
    if dbg is not None:
        dbg_out = nc.dram_tensor("dbg", [128, 8, T], F32, kind="ExternalOutput").ap()

    fw = FW(nc)
    pe, act, dve, pool, sp = fw.pe, fw.act, fw.dve, fw.pool, fw.sp
    A = nc.alloc_sbuf_tensor

    xs_t = [A(f"x_sb{i}", [128, 8, T], F32) for i in range(2)]
    xs_b = [[Buf(f"x{i}_{c}") for c in range(8)] for i in range(2)]
    XC = [xs_t[0], xs_b[0]]
    h_sb = A("h_sb", [128, 8, T], BF16)
    b_h = [Buf(f"h{c}") for c in range(8)]
    MB = Region(nc, "mb", 12, 512)
    merged = lambda d: MB.f32(d, 512)
    b_merged = lambda d: MB.b(d)
    branch = lambda c, lo=0, hi=T: MB.bf16(8 + c // 2, hi - lo, (c % 2) * 512 + lo)
    b_branch = lambda c: MB.b(8 + c // 2)
    actT = lambda f: MB.bf16(f // 2, 512, (f % 2) * 512)
    b_actT = lambda f: MB.b(f // 2)
    AR = Region(nc, "ar", 14, 544)
    st_mean = A("st_mean", [128, T], F32); b_mean = Buf("mean")
    st_var = A("st_var", [128, T], F32); b_var = Buf("var")
    st_rstd = A("st_rstd", [128, T], F32); b_rstd = Buf("rstd")
    NSLOT = 5
    RING = Region(nc, "ring", NSLOT, 2048)
    ring_i = [0]
    wpool_sb = A("wpool_sb", [128, 2, 4, 256], BF16); b_wpool = Buf("wpool")
    wuqn_sb = A("wuqn_sb", [128, 2, 1024], BF16); b_wuqn = Buf("wuqn")
    wuqr_sb = A("wuqr_sb", [128, 2, 512], BF16); b_wuqr = Buf("wuqr")
    wuqrs_sb = A("wuqrs_sb", [128, 2, 512], BF16); b_wuqrs = Buf("wuqrs")
    wukT_sb = A("wukT_sb", [128, 1024], BF16); b_wukT = Buf("wukT")
    wuv_sb = A("wuv_sb", [128, 1024], BF16); b_wuv = Buf("wuv")
    wsT_sb = A("wsT_sb", [128, 1024], BF16); b_wsT = Buf("wsT")
    bsT_sb = A("bsT_sb", [1, 1024], BF16); b_bsT = Buf("bsT")
    gsg_sb = A("gsg_sb", [128, 1024], F32); b_gsg = Buf("gsg")
    bsg_sb = A("bsg_sb", [128, 1024], F32); b_bsg = Buf("bsg")
    vecs_sb = A("vecs_sb", [128, DEPTH, NV], F32); b_vecs = Buf("vecs")
    mod_sb = A("mod_sb", [128, DEPTH * 48 * nseq], F32); b_mod = Buf("mod")
    c_sb = A("c_sb", [128, 8, nseq], F32); b_c = Buf("c")
    const_sb = A("const_sb", [128, NCONST], F32); b_const = Buf("const")
    ident_bf = A("ident_bf", [128, 128], BF16)
    mask_bf = A("mask_bf", [128, 128], BF16)
    ones_bf = A("ones_bf", [128, 128], BF16)
    onesm_bf = A("onesm_bf", [128, 128], BF16)
    ones_f = A("ones_f", [128, 128], F32)
    eps_ln = A("eps_ln", [128, 1], F32)
    eps_rms = A("eps_rms", [128, 1], F32)
    b_cst2 = Buf("cst2")
    ckvnT = [A(f"ckvnT{l}", [128, SEQ], BF16) for l in range(DEPTH)]
    krotT = [A(f"krotT{l}", [64, SEQ], BF16) for l in range(DEPTH)]
    Vc = [A(f"Vc{l}", [128, SEQ // 128, 128], BF16) for l in range(DEPTH)]
    b_kc = [[Buf(f"kc{l}_{j}") for j in range(NG)] for l in range(DEPTH)]
    b_kr = [[Buf(f"kr{l}_{j}") for j in range(NG)] for l in range(DEPTH)]
    b_vc = [[Buf(f"vc{l}_{j}") for j in range(NG)] for l in range(DEPTH)]
    cos2 = A("cos2", [64, T], F32); b_cos = Buf("cos")
    sinpm = A("sinpm", [64, T], F32); b_sin = Buf("sin")
    cqn = A("cqn", [128, 2, T], BF16); b_cqn = Buf("cqn")
    phalo = [A(f"phalo{l}", [128, 8, 16], F32) for l in range(DEPTH)]
    b_phalo = [Buf(f"phalo{l}") for l in range(DEPTH)]
    zhalo = [A(f"zhalo{l}", [128, 2 * NF, 2], F32) for l in range(DEPTH)]
    b_zhalo = [[Buf(f"zhalo{l}_{c}") for c in range(2 * NF)] for l in range(DEPTH)]
    PS = PsumPool(nc)

    def ACT(out, in_, func, reads, writes, **kw):
        fw.op(act, lambda h: h.activation(out=out, in_=in_, func=func, **kw), reads, writes)

    def TT(out, in0, in1, op, reads, writes, eng=dve):
        fw.op(eng, lambda h: h.tensor_tensor(out=out, in0=in0, in1=in1, op=op), reads, writes)

    def STT(out, in0, scalar, in1, op0, op1, reads, writes, eng=dve):
        fw.op(eng, lambda h: h.scalar_tensor_tensor(out=out, in0=in0, scalar=scalar, in1=in1, op0=op0, op1=op1),
              reads, writes)

    def TS(out, in0, s1, s2, op0, op1, reads, writes, eng=dve):
        if s2 is None:
            fw.op(eng, lambda h: h.tensor_scalar(out=out, in0=in0, scalar1=s1, scalar2=None, op0=op0), reads, writes)
        else:
            fw.op(eng, lambda h: h.tensor_scalar(out=out, in0=in0, scalar1=s1, scalar2=s2, op0=op0, op1=op1),
                  reads, writes)

    def CP(out, in_, reads, writes, eng=dve):
        fw.op(eng, lambda h: h.tensor_copy(out=out, in_=in_), reads, writes)

    def MM(out, lhsT, rhs, start, stop, reads, writes, signal=None):
        signal = True
        fw.op(pe, lambda h: h.matmul(out, lhsT=lhsT, rhs=rhs, start=start, stop=stop), reads, writes, signal=signal)

    def psb(b, lo=0, hi=T, parts=128):
        return PS.t[0:parts, b, lo:hi]

    SCR = {}

    def mk_scratch(key, src2d):
        rows, cols = src2d.shape
        t = nc.dram_tensor("scr_" + "_".join(str(k) for k in key), [rows, cols], BF16, kind="Internal").ap()
        bufs = []
        for r0 in range(0, rows, 128):
            b = Buf("scr")
            fw.dma(pool, t[r0:r0 + 128, :], src2d[r0:r0 + 128, :], writes=[b])
            bufs.append(b)
        SCR[key] = (t, bufs)

    def scols(key, c0, n):
        ap, bufs = SCR[key]
        return ap.rearrange("(kc p) n -> p kc n", p=128)[:, :, c0:c0 + n], bufs

    def wload(src, eng=None):
        src_ap, sbufs = src
        slot = ring_i[0] % NSLOT
        ring_i[0] += 1
        nel = 1
        for s_ in src_ap.shape[1:]:
            nel *= s_
        dst = RING.bf16(slot, nel).rearrange("p (k n) -> p k n", k=src_ap.shape[1])
        fw.dma(eng or sp, dst, src_ap, reads=sbufs, writes=RING.b(slot))
        return slot

    def ring_view(slot, k, n, tot=None):
        tot = tot or n
        v = RING.bf16(slot, k * tot).rearrange("p (k n) -> p k n", k=k)
        return v

    def mcol(l, kind, c, s):
        i = ((l * 6 + kind) * 8 + c) * nseq + s
        return mod_sb[:, i:i + 1]

    def vcol(l, off, c=0):
        return vecs_sb[:, l, off + c:off + c + 1]

    fw.dma(sp, const_sb[:], consts, writes=[b_const])
    fw.dma(sp, vecs_sb[:], vecs.rearrange("l p n -> p l n"), writes=[b_vecs])
    fw.dma(sp, c_sb[:], cT, writes=[b_c])
    CP(ident_bf[:], const_sb[:, C_IDENT:C_IDENT + 128], [b_const], [b_cst2])
    CP(mask_bf[:], const_sb[:, C_MASK:C_MASK + 128], [b_const], [b_cst2])
    fw.op(dve, lambda h: h.memset(ones_bf[:], 1.0), writes=[b_cst2])
    fw.op(dve, lambda h: h.memset(onesm_bf[:], 1.0 / D), writes=[b_cst2])
    fw.op(dve, lambda h: h.memset(ones_f[:], 1.0), writes=[b_cst2])
    fw.op(dve, lambda h: h.memset(eps_ln[:], LN_EPS / (ALPHA * ALPHA)), writes=[b_cst2])
    fw.op(dve, lambda h: h.memset(eps_rms[:], RMS_EPS), writes=[b_cst2])
    eps_ln1 = A("eps_ln1", [128, 1], F32)
    fw.op(dve, lambda h: h.memset(eps_ln1[:], LN_EPS), writes=[b_cst2])

    for l_ in range(nlayers):
        mk_scratch(("w_in", l_), w_in[l_])
        for bi_ in range(3):
            mk_scratch(("w_br", l_, bi_), w_branch[l_, bi_])
        mk_scratch(("w_o", l_), w_o[l_])
        mk_scratch(("w_up", l_), w_up[l_])
        mk_scratch(("w_down", l_), w_down[l_])

    cact = A("cact", [128, 8, nseq], F32); b_cact = Buf("cact")
    ACT(cact[:], c_sb[:], AF.Silu, [b_c], [b_cact])
    pm = PS.get()
    for l in range(DEPTH):
        for blk in range(24):
            slot = ring_i[0] % NSLOT
            ring_i[0] += 1
            stage = RING.f32(slot, 2048).rearrange("p (k n) -> p k n", k=8)
            fw.dma(sp, stage, w_ada[l].rearrange("(kc p) n -> p kc n", p=128)[:, :, blk * 256:(blk + 1) * 256],
                   writes=RING.b(slot))
            for jj in range(2):
                j = blk * 2 + jj
                col = (l * 48 + j) * nseq
                for kc in range(8):
                    MM(PS.t[:, pm, col:col + nseq], stage[:, kc, jj * 128:(jj + 1) * 128], cact[:, kc, :],
                       kc == 0, kc == 7, RING.b(slot) + [b_cact], [PS.bufs[pm]])
    for l in range(DEPTH):
        n48 = 48 * nseq
        TT(mod_sb[:, l * n48:(l + 1) * n48].rearrange("p (j s) -> p j s", s=nseq),
           PS.t[:, pm, l * n48:(l + 1) * n48].rearrange("p (j s) -> p j s", s=nseq),
           vecs_sb[:, l, V_BADA:V_BADA + 48].unsqueeze(2).to_broadcast([128, 48, nseq]),
           ALU.add, [PS.bufs[pm], b_vecs], [b_mod])
        for kind in (1, 4):
            a0 = (l * 6 + kind) * 8 * nseq
            TS(mod_sb[:, a0:a0 + 8 * nseq], mod_sb[:, a0:a0 + 8 * nseq], 1.0, None, ALU.add, None, [b_mod], [b_mod])
        for kind in (2, 5):
            a0 = (l * 6 + kind) * 8 * nseq
            TS(mod_sb[:, a0:a0 + 8 * nseq], mod_sb[:, a0:a0 + 8 * nseq], 1.0 / ALPHA, None, ALU.mult, None,
               [b_mod], [b_mod])
    PS.put(pm)

    def load_wpool(l):
        for gq in range(4):
            fw.dma(pool, wpool_sb[:, :, gq, :], w_pool[l, gq].rearrange("(i p) d -> p i d", p=128), writes=[b_wpool])

    def load_attw(l):
        fw.dma(pool, wuqn_sb[:], w_uqn[l].rearrange("(i p) n -> p i n", p=128), writes=[b_wuqn])
        fw.dma(pool, wuqr_sb[:], w_uqr[l].rearrange("(i p) n -> p i n", p=128), writes=[b_wuqr])
        fw.dma(pool, wuqrs_sb[:], w_uqrs[l].rearrange("(i p) n -> p i n", p=128), writes=[b_wuqrs])
        fw.dma(pool, wukT_sb[:], w_ukT[l], writes=[b_wukT])
        fw.dma(pool, wuv_sb[:], w_uv[l], writes=[b_wuv])

    def load_sguw(l):
        fw.dma(pool, wsT_sb[:], wsT[l], writes=[b_wsT])
        fw.dma(pool, bsT_sb[:], bsT[l], writes=[b_bsT])
        fw.dma(pool, gsg_sb[:], gsg[l].partition_broadcast(128), writes=[b_gsg])
        fw.dma(pool, bsg_sb[:], bsg[l].partition_broadcast(128), writes=[b_bsg])
        TT(wsT_sb[:].rearrange("p (g t) -> p g t", g=8), wsT_sb[:].rearrange("p (g t) -> p g t", g=8),
           mask_bf[:].unsqueeze(1).to_broadcast([128, 8, 128]), ALU.mult, [b_wsT, b_cst2], [b_wsT])

    def ln_stats(eps_tile):
        xb = lambda c: AR.bf16(c // 2, T, (c % 2) * T)
        xq = lambda c: AR.bf16(4 + c // 2, T, (c % 2) * T)
        pmn = PS.get()
        psq = PS.get()
        for pp in range(4):
            xb2 = AR.bf16(pp, 2 * T).rearrange("p (a n) -> p a n", a=2)
            xq2 = AR.bf16(4 + pp, 2 * T).rearrange("p (a n) -> p a n", a=2)
            xin = XC[0][:, 2 * pp:2 * pp + 2, :]
            rb = [XC[1][2 * pp], XC[1][2 * pp + 1]]
            CP(xb2, xin, rb, AR.b(pp))
            ACT(xq2, xin, AF.Square, rb, AR.b(4 + pp))
            for c in (2 * pp, 2 * pp + 1):
                MM(psb(pmn), onesm_bf[:], xb(c), c == 0, c == 7, AR.b(c // 2) + [b_cst2], [PS.bufs[pmn]])
                MM(psb(psq), onesm_bf[:], xq(c), c == 0, c == 7, AR.b(4 + c // 2) + [b_cst2], [PS.bufs[psq]])
        CP(st_mean[:], psb(pmn), [PS.bufs[pmn]], [b_mean])
        TT(st_var[:], st_mean[:], st_mean[:], ALU.mult, [b_mean], [b_var])
        TT(st_var[:], psb(psq), st_var[:], ALU.subtract, [PS.bufs[psq], b_var], [b_var])
        PS.put(pmn)
        PS.put(psq)
        ACT(st_var[:], st_var[:], AF.Ln, [b_var, b_cst2], [b_var], bias=eps_tile[:, 0:1])
        ACT(st_rstd[:], st_var[:], AF.Exp, [b_var], [b_rstd], scale=-0.5)

    def ln_apply(out_fn, out_bufs, scale_fn, bias_fn, extra_reads):
        for c in range(8):
            pg = 8 + (c % 4)
            t = AR.f32(pg, T)
            TT(t, XC[0][:, c, :], st_mean[:], ALU.subtract, [XC[1][c], b_mean], AR.b(pg), eng=pool)
            TT(t, t, st_rstd[:], ALU.mult, AR.b(pg) + [b_rstd], AR.b(pg))
            ACT(out_fn(c), t, AF.Identity, AR.b(pg) + extra_reads, out_bufs(c), scale=scale_fn(c), bias=bias_fn(c))

    def ln_mod(l, s, kind):
        ln_stats(eps_ln1)
        ln_apply(lambda c: h_sb[:, c, :], lambda c: [b_h[c]],
                 lambda c: mcol(l, 3 * kind + 1, c, s), lambda c: mcol(l, 3 * kind, c, s), [b_mod])

    def post_ln(l, goff, boff):
        ln_stats(eps_ln)
        ln_apply(lambda c: XC[0][:, c, :], lambda c: [XC[1][c]],
                 lambda c: vcol(l, goff, c), lambda c: vcol(l, boff, c), [b_vecs])

    def proj8(w_slot, col0, rhs_fn, rhs_bufs, ncols=512):
        b = PS.get()
        wv = ring_view(w_slot, 8, ncols)
        for k in range(8):
            MM(psb(b), wv[:, k, col0:col0 + 128], rhs_fn(k), k == 0, k == 7,
               RING.b(w_slot) + rhs_bufs(k), [PS.bufs[b]])
        return b

    def proj8_kouter(specs, rhs_fn, rhs_bufs):
        banks = [PS.get() for _ in specs]
        for k in range(8):
            for (w_slot, col0, ncols), b in zip(specs, banks):
                wv = ring_view(w_slot, 8, ncols)
                MM(psb(b), wv[:, k, col0:col0 + 128], rhs_fn(k), k == 0, k == 7,
                   RING.b(w_slot) + rhs_bufs(k), [PS.bufs[b]])
        return banks

    def w_cols(w_l, c0, n):
        return w_l.rearrange("(kc p) n -> p kc n", p=128)[:, :, c0:c0 + n]

    def branch_out(l, bi, first):
        for half in range(2):
            ws = wload(scols(("w_br", l, bi), half * 512, 512))
            gs = wload(scols(("w_in", l), bi * D + half * 512, 512))
            for dd in range(4):
                d = half * 4 + dd
                by = proj8(ws, dd * 128, lambda k: branch(k), lambda k: b_branch(k))
                bg = proj8(gs, dd * 128, lambda k: h_sb[:, k, :], lambda k: [b_h[k]])
                pg = 12 + (d % 2)
                sig = AR.f32(pg, T)
                ACT(sig, psb(bg), AF.Sigmoid, [PS.bufs[bg]], AR.b(pg))
                PS.put(bg)
                if first:
                    TT(merged(d), psb(by), sig, ALU.mult, [PS.bufs[by]] + AR.b(pg), b_merged(d))
                else:
                    TT(sig, psb(by), sig, ALU.mult, [PS.bufs[by]] + AR.b(pg), AR.b(pg))
                    TT(merged(d), merged(d), sig, ALU.add, b_merged(d) + AR.b(pg), b_merged(d))
                PS.put(by)

    TWO_PI = 2.0 * np.pi
    MAGIC = 12582912.0
    CW1 = 6.28125
    CW2 = float(np.float32(TWO_PI - CW1))
    CW3 = float(TWO_PI - CW1 - np.float64(np.float32(TWO_PI - CW1)))
    PI_LO = 3.1415925

    def rope_tables(s, g):
        t0 = g * T
        pi_ = AR.t[0:64, 0:T].bitcast(I32)
        fw.dma(pool, pi_, pos[s:s + 1, t0:t0 + T].partition_broadcast(64), writes=AR.b(0))
        ang = AR.f32(1, T, parts=64)
        CP(ang, pi_, AR.b(0), AR.b(1))
        TS(ang, ang, const_sb[0:64, C_FREQ:C_FREQ + 1], None, ALU.mult, None, AR.b(1) + [b_const], AR.b(1))
        for which, outt, ob in ((0, sinpm, b_sin), (1, cos2, b_cos)):
            kk = AR.f32(2, T, parts=64)
            r = AR.f32(3, T, parts=64)
            if which == 0:
                TS(kk, ang, float(1.0 / TWO_PI), MAGIC, ALU.mult, ALU.add, AR.b(1), AR.b(2))
            else:
                TS(kk, ang, float(1.0 / TWO_PI), 0.25, ALU.mult, ALU.add, AR.b(1), AR.b(2))
                TS(kk, kk, MAGIC, None, ALU.add, None, AR.b(2), AR.b(2))
            TS(kk, kk, -MAGIC, None, ALU.add, None, AR.b(2), AR.b(2))
            STT(r, kk, -CW1, ang, ALU.mult, ALU.add, AR.b(1) + AR.b(2), AR.b(3))
            STT(r, kk, -CW2, r, ALU.mult, ALU.add, AR.b(2) + AR.b(3), AR.b(3))
            STT(r, kk, -CW3, r, ALU.mult, ALU.add, AR.b(2) + AR.b(3), AR.b(3))
            if which == 1:
                TS(r, r, float(np.pi / 2), None, ALU.add, None, AR.b(3), AR.b(3))
            TS(r, r, PI_LO, -PI_LO, ALU.min, ALU.max, AR.b(3), AR.b(3))
            ACT(outt[:], r, AF.Sin, AR.b(3), [ob])
        TS(sinpm[:], sinpm[:], const_sb[0:64, C_SGN:C_SGN + 1], None, ALU.mult, None, [b_sin, b_const], [b_sin])

    def mixer(l, s, g, nxt_l):
        t0 = g * T
        fw.phase = 'ln_mod_t'
        ln_mod(l, s, 0)
        fw.phase = 'pool'

        for half in range(2):
            wa = wload(scols(("w_in", l), OFF_POOL + half * 512, 512))
            pre = None
            if half == 0:
                pre = proj8_kouter([(wa, cc_ * 128, 512) for cc_ in range(4)], lambda k: h_sb[:, k, :], lambda k: [b_h[k]])
            for cc in range(4):
                c = half * 4 + cc
                w = POOL_W[c]
                ba = pre[cc] if pre else proj8(wa, cc * 128, lambda k: h_sb[:, k, :], lambda k: [b_h[k]])
                P0, P1, P2 = 0 + 3 * (c % 2), 1 + 3 * (c % 2), 2 + 3 * (c % 2)
                a = AR.f32(P0, 528)
                CP(a[:, 0:16], phalo[l][:, c, :], [b_phalo[l]], AR.b(P0))
                ACT(a[:, 16:528], psb(ba), AF.Copy, [PS.bufs[ba]], AR.b(P0))
                PS.put(ba)
                CP(phalo[l][:, c, :], a[:, 512:528], AR.b(P0), [b_phalo[l]])
                cur, curp = a, P0
                sh = 1
                lo = 1
                others = [P1, P2]
                oi = 0
                while sh < w:
                    np_ = others[oi % 2]
                    oi += 1
                    nt = AR.f32(np_, 528)
                    TT(nt[:, lo:528], cur[:, lo:528], cur[:, lo - sh:528 - sh], ALU.add, AR.b(curp), AR.b(np_))
                    cur, curp = nt, np_
                    sh *= 2
                    lo = lo + sh
                if g == 0:
                    wi = {2: 0, 4: 1, 8: 2, 16: 3}[w]
                    TT(cur[:, 16:32], cur[:, 16:32], const_sb[:, C_CORR + wi * 16:C_CORR + wi * 16 + 16], ALU.mult,
                       AR.b(curp) + [b_const], AR.b(curp))
                STT(branch(c), cur[:, 16:528], 1.0 / w, a[:, 16:528], ALU.mult, ALU.subtract,
                    AR.b(curp) + AR.b(P0), b_branch(c))
        for gq in range(4):
            bm = []
            for j in range(2):
                b = PS.get()
                for i in range(2):
                    MM(psb(b), wpool_sb[:, i, gq, j * 128:(j + 1) * 128], branch(2 * gq + i), i == 0, i == 1,
                       [b_wpool] + b_branch(2 * gq + i), [PS.bufs[b]])
                bm.append(b)
            for j in range(2):
                ACT(branch(2 * gq + j), psb(bm[j]), AF.Identity, [PS.bufs[bm[j]], b_vecs], b_branch(2 * gq + j),
                    scale=vcol(l, V_SPOOL, 2 * gq + j))
                PS.put(bm[j])
        if nxt_l is not None:
            load_wpool(nxt_l)
        fw.phase = 'bout_a'
        branch_out(l, 0, True)
        fw.phase = 'mla_proj'
        if l == 0:
            rope_tables(s, g)

        wq = wload(scols(("w_in", l), OFF_CQ, 448))
        wqv = RING.bf16(wq, 8 * 448).rearrange("p (k n) -> p k n", k=8)
        wk = wload((w_krs[l].rearrange("(kc p) n -> p kc n", p=128), []), eng=pool)
        wkv = RING.bf16(wk, 8 * 64).rearrange("p (k n) -> p k n", k=8)
        hb = lambda k: [b_h[k]]
        bq = []
        pms = PS.get()
        for i in range(2):
            b = PS.get()
            for k in range(8):
                MM(psb(b), wqv[:, k, i * 128:(i + 1) * 128], h_sb[:, k, :], k == 0, k == 7, RING.b(wq) + hb(k), [PS.bufs[b]])
            sq = AR.bf16(i, T)
            ACT(sq, psb(b), AF.Square, [PS.bufs[b]], AR.b(i))
            MM(psb(pms), ones_bf[:], sq, i == 0, i == 1, AR.b(i) + [b_cst2], [PS.bufs[pms]])
            bq.append(b)
        rq = AR.f32(2, T)
        ACT(rq, psb(pms), AF.Ln, [PS.bufs[pms], b_cst2], AR.b(2), scale=1.0 / 256, bias=eps_rms[:, 0:1])
        PS.put(pms)
        ACT(rq, rq, AF.Exp, AR.b(2), AR.b(2), scale=-0.5)
        for i in range(2):
            STT(cqn[:, i, :], psb(bq[i]), vcol(l, V_GQ, i), rq, ALU.mult, ALU.mult,
                [PS.bufs[bq[i]], b_vecs] + AR.b(2), [b_cqn])
            PS.put(bq[i])
        b = PS.get()
        pms = PS.get()
        for k in range(8):
            MM(psb(b), wqv[:, k, 256:384], h_sb[:, k, :], k == 0, k == 7, RING.b(wq) + hb(k), [PS.bufs[b]])
        sq = AR.bf16(3, T)
        ACT(sq, psb(b), AF.Square, [PS.bufs[b]], AR.b(3))
        MM(psb(pms), ones_bf[:], sq, True, True, AR.b(3) + [b_cst2], [PS.bufs[pms]])
        rk = AR.f32(4, T)
        ACT(rk, psb(pms), AF.Ln, [PS.bufs[pms], b_cst2], AR.b(4), scale=1.0 / 128, bias=eps_rms[:, 0:1])
        PS.put(pms)
        ACT(rk, rk, AF.Exp, AR.b(4), AR.b(4), scale=-0.5)
        STT(ckvnT[l][:, t0:t0 + T], psb(b), vcol(l, V_GKV, 0), rk, ALU.mult, ALU.mult,
            [PS.bufs[b], b_vecs] + AR.b(4), [b_kc[l][g]])
        PS.put(b)
        for tt in range(4):
            b = PS.get()
            tp = PS.t[:, b, 0:64].bitcast(BF16)
            fw.op(pe, lambda h, tp=tp, tt=tt: h.transpose(tp, ckvnT[l][:, t0 + tt * 128:t0 + (tt + 1) * 128], ident_bf[:]),
                  [b_kc[l][g], b_cst2], [PS.bufs[b]])
            CP(Vc[l][:, 4 * g + tt, :], tp, [PS.bufs[b]], [b_vc[l][g]])
            PS.put(b)
        b1 = PS.get()
        b2 = PS.get()
        for k in range(8):
            MM(psb(b1, parts=64), wqv[:, k, 384:448], h_sb[:, k, :], k == 0, k == 7, RING.b(wq) + hb(k), [PS.bufs[b1]])
        for k in range(8):
            MM(psb(b2, parts=64), wkv[:, k, :], h_sb[:, k, :], k == 0, k == 7, RING.b(wk) + hb(k), [PS.bufs[b2]])
        t1 = AR.f32(5, T, parts=64)
        t2 = AR.f32(6, T, parts=64)
        TT(t1, psb(b1, parts=64), cos2[:], ALU.mult, [PS.bufs[b1], b_cos], AR.b(5))
        TT(t2, psb(b2, parts=64), sinpm[:], ALU.mult, [PS.bufs[b2], b_sin], AR.b(6))
        PS.put(b1)
        PS.put(b2)
        TT(krotT[l][:, t0:t0 + T], t1, t2, ALU.add, AR.b(5) + AR.b(6), [b_kr[l][g]])

        ntile = 4 * g + 4
        fw.phase = 'mla_heads'
        qp_all = lambda hh: AR.bf16(hh // 2, T, (hh % 2) * T)
        qr_all = lambda hh: AR.bf16(4 + hh // 2, T, (hh % 2) * T, parts=64)
        for hh in range(8):
            par = hh % 2
            b = PS.get()
            for i in range(2):
                MM(psb(b), wuqn_sb[:, i, hh * 128:(hh + 1) * 128], cqn[:, i, :], i == 0, i == 1, [b_wuqn, b_cqn], [PS.bufs[b]])
            qn = AR.bf16(12 + par, T)
            ACT(qn, psb(b), AF.Copy, [PS.bufs[b]], AR.b(12 + par))
            PS.put(b)
            b = PS.get()
            MM(psb(b), wukT_sb[:, hh * 128:(hh + 1) * 128], qn, True, True, [b_wukT] + AR.b(12 + par), [PS.bufs[b]])
            ACT(qp_all(hh), psb(b), AF.Copy, [PS.bufs[b]], AR.b(hh // 2))
            PS.put(b)
            b1 = PS.get()
            b2 = PS.get()
            for i in range(2):
                MM(psb(b1, parts=64), wuqr_sb[:, i, hh * 64:(hh + 1) * 64], cqn[:, i, :], i == 0, i == 1, [b_wuqr, b_cqn], [PS.bufs[b1]])
            for i in range(2):
                MM(psb(b2, parts=64), wuqrs_sb[:, i, hh * 64:(hh + 1) * 64], cqn[:, i, :], i == 0, i == 1, [b_wuqrs, b_cqn], [PS.bufs[b2]])
            p1, p2 = 8 + 2 * par, 9 + 2 * par
            t1 = AR.f32(p1, T, parts=64)
            t2 = AR.f32(p2, T, parts=64)
            TT(t1, psb(b1, parts=64), cos2[:], ALU.mult, [PS.bufs[b1], b_cos], AR.b(p1))
            TT(t2, psb(b2, parts=64), sinpm[:], ALU.mult, [PS.bufs[b2], b_sin], AR.b(p2))
            PS.put(b1)
            PS.put(b2)
            TT(qr_all(hh), t1, t2, ALU.add, AR.b(p1) + AR.b(p2), AR.b(4 + hh // 2))

        units = []
        j = 0
        while j < ntile:
            npair = 2 if (j + 1 < 4 * g) else 1
            units.append((j, npair))
            j += npair
        uctr = [0]
        for hh in range(8):
            par = hh % 2
            qp = qp_all(hh)
            qr = qr_all(hh)
            bpo = PS.get_pair()
            po, pd = bpo, bpo + 1
            st = {}

            def S(ui):
                j0, npair = units[ui]
                bs = PS.get_pair() if npair == 2 else PS.get()
                for jj in range(npair):
                    jt = j0 + jj
                    qlo = max(0, jt - 4 * g) * 128
                    kg = jt // 4
                    MM(PS.t[:, bs + jj, qlo:T], ckvnT[l][:, jt * 128:(jt + 1) * 128], qp[:, qlo:T], True, False,
                       [b_kc[l][kg]] + AR.b(hh // 2), [PS.bufs[bs + jj]])
                    MM(PS.t[:, bs + jj, qlo:T], krotT[l][:, jt * 128:(jt + 1) * 128], qr[:, qlo:T], False, True,
                       [b_kr[l][kg]] + AR.b(4 + hh // 2), [PS.bufs[bs + jj]])
                st[ui] = bs

            def E(ui):
                j0, npair = units[ui]
                bs = st[ui]
                ppg = 8 + (uctr[0] % 3)
                uctr[0] += 1
                if npair == 2:
                    pT = AR.t[:, ppg * 544:ppg * 544 + 512].bitcast(BF16).rearrange("p (a n) -> p a n", a=2)
                    ACT(pT, PS.t[:, bs:bs + 2, :], AF.Exp, [PS.bufs[bs], PS.bufs[bs + 1]], AR.b(ppg), scale=ATT_SCALE)
                    pTs = [pT[:, 0, :], pT[:, 1, :]]
                else:
                    qlo = max(0, j0 - 4 * g) * 128
                    pT1 = AR.bf16(ppg, T)
                    ACT(pT1[:, qlo:T], PS.t[:, bs, qlo:T], AF.Exp, [PS.bufs[bs]], AR.b(ppg), scale=ATT_SCALE)
                    fw.op(dve, lambda h, pT1=pT1, qlo=qlo: h.memset(pT1[64:128, qlo:qlo + 64], 0.0), [], AR.b(ppg))
                    pTs = [pT1]
                for jj in range(npair):
                    PS.put(bs + jj)
                st[ui] = (pTs, ppg)

            def P(ui):
                j0, npair = units[ui]
                pTs, ppg = st[ui]
                for jj in range(npair):
                    jt = j0 + jj
                    ql = max(0, jt - 4 * g) * 128
                    kg = jt // 4
                    MM(PS.t[:, po, ql:T], Vc[l][:, jt, :], pTs[jj][:, ql:T], jt == 0, jt == ntile - 1,
                       [b_vc[l][kg]] + AR.b(ppg), [PS.bufs[po]])
                    MM(PS.t[:, pd, ql:T], ones_bf[:], pTs[jj][:, ql:T], jt == 0, jt == ntile - 1,
                       [b_cst2] + AR.b(ppg), [PS.bufs[pd]])

            nu = len(units)
            S(0)
            for ui in range(nu):
                if ui + 1 < nu:
                    S(ui + 1)
                E(ui)
                P(ui)
            rden = AR.f32(11, T)
            ACT(rden, psb(pd), AF.Ln, [PS.bufs[pd]], AR.b(11))
            ACT(rden, rden, AF.Exp, AR.b(11), AR.b(11), scale=-1.0)
            PS.put(pd)
            op_ = AR.bf16(12 + par, T)
            TT(op_, psb(po), rden, ALU.mult, [PS.bufs[po]] + AR.b(11), AR.b(12 + par))
            PS.put(po)
            b = PS.get()
            MM(psb(b), wuv_sb[:, hh * 128:(hh + 1) * 128], op_, True, True, [b_wuv] + AR.b(12 + par), [PS.bufs[b]])
            ACT(branch(hh), psb(b), AF.Copy, [PS.bufs[b]], b_branch(hh))
            PS.put(b)
        if nxt_l is not None:
            load_attw(nxt_l)
        fw.phase = 'bout_b'
        branch_out(l, 1, False)
        fw.phase = 'sgu'

        wv_s = [wload(scols(("w_in", l), OFF_SG + D + nn * 512, 512)) for nn in range(2)]
        vn_all = lambda tb: AR.bf16(4 + tb, 1024)
        for tb in range(4):
            bp = PS.get_pair()
            for nn in range(2):
                wv = ring_view(wv_s[nn], 8, 512)
                for k in range(8):
                    MM(PS.t[:, bp + nn, :], h_sb[:, k, tb * 128:(tb + 1) * 128], wv[:, k, :], k == 0, k == 7,
                       RING.b(wv_s[nn]) + [b_h[k]], [PS.bufs[bp + nn]])
            gpg = 0 + 2 * (tb % 2)
            gv = AR.t[:, gpg * 544:gpg * 544 + 1024]
            ACT(gv.rearrange("p (a n) -> p a n", a=2), PS.t[:, bp:bp + 2, :], AF.Gelu,
                [PS.bufs[bp], PS.bufs[bp + 1]], AR.b(gpg, 2))
            PS.put(bp)
            PS.put(bp + 1)
            stt_ = AR.f32(12, 16)
            for i in range(2):
                fw.op(dve, lambda h, i=i, gv=gv, stt_=stt_: h.bn_stats(out=stt_[:, i * 6:(i + 1) * 6], in_=gv[:, i * 512:(i + 1) * 512]),
                      AR.b(gpg, 2), AR.b(12))
            mvv = AR.f32(12, 2, off=16)
            fw.op(dve, lambda h, mvv=mvv, stt_=stt_: h.bn_aggr(out=mvv, in_=stt_[:, 0:12].rearrange("p (a n) -> p a n", a=2)),
                  AR.b(12), AR.b(12))
            rs = AR.f32(12, 1, off=20)
            ACT(rs, mvv[:, 1:2], AF.Sqrt, AR.b(12) + [b_cst2], AR.b(12), bias=eps_ln1[:, 0:1])
            fw.op(dve, lambda h, rs=rs: h.reciprocal(out=rs, in_=rs), AR.b(12), AR.b(12))
            TS(gv, gv, mvv[:, 0:1], rs, ALU.subtract, ALU.mult, AR.b(gpg, 2) + AR.b(12), AR.b(gpg, 2))
            TT(gv, gv, gsg_sb[:], ALU.mult, AR.b(gpg, 2) + [b_gsg], AR.b(gpg, 2))
            TT(vn_all(tb), gv, bsg_sb[:], ALU.add, AR.b(gpg, 2) + [b_bsg], AR.b(4 + tb))
        for half in range(2):
            wu = wload(scols(("w_in", l), OFF_SG + half * 512, 512))
            for gg in range(4):
                gq = half * 4 + gg
                bz = PS.get()
                for tb in range(4):
                    MM(PS.t[:, bz, tb * 128:(tb + 1) * 128], vn_all(tb)[:, gq * 128:(gq + 1) * 128],
                       wsT_sb[:, gq * 128:(gq + 1) * 128], True, False, AR.b(4 + tb) + [b_wsT], [PS.bufs[bz]])
                    MM(PS.t[:, bz, tb * 128:(tb + 1) * 128], ones_bf[0:1, :], bsT_sb[0:1, gq * 128:(gq + 1) * 128],
                       False, True, [b_cst2, b_bsT], [PS.bufs[bz]], signal=(tb == 3))
                bu = proj8(wu, gg * 128, lambda k: h_sb[:, k, :], lambda k: [b_h[k]])
                pg = 0 + (gq % 2)
                ug = AR.f32(pg, T)
                ACT(ug, psb(bu), AF.Gelu, [PS.bufs[bu]], AR.b(pg))
                PS.put(bu)
                TT(branch(gq), ug, psb(bz), ALU.mult, AR.b(pg) + [PS.bufs[bz]], b_branch(gq))
                PS.put(bz)
        if nxt_l is not None:
            load_sguw(nxt_l)
        fw.phase = 'bout_c'
        branch_out(l, 2, False)
        fw.phase = 'wo'

        for d in range(8):
            CP(branch(d), merged(d), b_merged(d), b_branch(d))
        for half in range(2):
            wo = wload(scols(("w_o", l), half * 512, 512))
            for dd in range(4):
                d = half * 4 + dd
                by = proj8(wo, dd * 128, lambda k: branch(k), lambda k: b_branch(k))
                STT(XC[0][:, d, :], psb(by), mcol(l, 2, d, s), XC[0][:, d, :], ALU.mult, ALU.add,
                    [PS.bufs[by], b_mod, XC[1][d]], [XC[1][d]])
                PS.put(by)
        fw.phase = 'post_ln_t'
        post_ln(l, V_LNTG, V_LNTB)

    def ffn(l, s, g):
        fw.phase = 'ln_mod_f'
        ln_mod(l, s, 1)
        fw.phase = 'ffn_up'
        nblk = (NF + 3) // 4

        def ffn_stage2(f):
            av = AR.f32(2 * (f % 3), T)
            ag = AR.f32(2 * (f % 3) + 1, T)
            pav, pag = 2 * (f % 3), 2 * (f % 3) + 1
            ACT(ag, ag, AF.Silu, AR.b(pag), AR.b(pag))
            TT(actT(f), ag, av, ALU.mult, AR.b(pag) + AR.b(pav), b_actT(f), eng=pool)

        for fb in range(nblk):
            nfc = min(4, NF - fb * 4)
            wvs = wload(scols(("w_up", l), fb * 512, nfc * 128))
            wgs = wload(scols(("w_up", l), D_FF + fb * 512, nfc * 128))
            pre = {}
            if fb == 0:
                bks = proj8_kouter([(wsl_, ff_ * 128, nfc * 128) for ff_ in range(2) for wsl_ in (wvs, wgs)],
                                   lambda k: h_sb[:, k, :], lambda k: [b_h[k]])
                pre = {(0, 0): bks[0], (0, 1): bks[1], (1, 0): bks[2], (1, 1): bks[3]}
            for ff in range(nfc):
                f = fb * 4 + ff
                for which, wsl in ((0, wvs), (1, wgs)):
                    ch = which * NF + f
                    wv = ring_view(wsl, 8, nfc * 128)
                    if (ff, which) in pre:
                        b = pre[(ff, which)]
                    else:
                        b = PS.get()
                        for k in range(8):
                            MM(psb(b), wv[:, k, ff * 128:(ff + 1) * 128], h_sb[:, k, :], k == 0, k == 7,
                               RING.b(wsl) + [b_h[k]], [PS.bufs[b]])
                    pa = 2 * (f % 3) + which
                    acc = AR.f32(pa, T)
                    zh = zhalo[l][:, ch, :]
                    bzh = [b_zhalo[l][ch]]
                    w0, w1, w2 = vcol(l, V_CW, ch), vcol(l, V_CW + 2 * NF, ch), vcol(l, V_CW + 4 * NF, ch)
                    ACT(acc, psb(b), AF.Identity, [PS.bufs[b], b_vecs], AR.b(pa), scale=w2, bias=vcol(l, V_CB, ch))
                    STT(acc[:, 1:T], psb(b, 0, T - 1), w1, acc[:, 1:T], ALU.mult, ALU.add,
                        [PS.bufs[b], b_vecs] + AR.b(pa), AR.b(pa))
                    STT(acc[:, 2:T], psb(b, 0, T - 2), w0, acc[:, 2:T], ALU.mult, ALU.add,
                        [PS.bufs[b], b_vecs] + AR.b(pa), AR.b(pa))
                    STT(acc[:, 0:1], zh[:, 1:2], w1, acc[:, 0:1], ALU.mult, ALU.add, bzh + [b_vecs] + AR.b(pa), AR.b(pa))
                    STT(acc[:, 0:2], zh, w0, acc[:, 0:2], ALU.mult, ALU.add, bzh + [b_vecs] + AR.b(pa), AR.b(pa))
                    ACT(zh, psb(b, T - 2, T), AF.Copy, [PS.bufs[b]], bzh)
                    PS.put(b)
                if f > 0:
                    ffn_stage2(f - 1)
        ffn_stage2(NF - 1)
        fw.phase = 'ffn_down'
        for d in range(8):
            wd = wload(scols(("w_down", l), d * 128, 128))
            wv = RING.bf16(wd, NF * 128).rearrange("p (k n) -> p k n", k=NF)
            b = PS.get()
            for f in range(NF):
                MM(psb(b), wv[:, f, :], actT(f), f == 0, f == NF - 1, RING.b(wd) + b_actT(f), [PS.bufs[b]])
            STT(XC[0][:, d, :], psb(b), mcol(l, 5, d, s), XC[0][:, d, :], ALU.mult, ALU.add,
                [PS.bufs[b], b_mod, XC[1][d]], [XC[1][d]])
            PS.put(b)
        fw.phase = 'post_ln_f'
        post_ln(l, V_LNFG, V_LNFB)

    steps = [(s, g, l) for s in range(nseq) for g in range(ngroups) for l in range(nlayers)]
    groups = [(s, g) for s in range(nseq) for g in range(ngroups)]
    load_wpool(0)
    load_attw(0)
    load_sguw(0)

    def load_x(gi):
        s_, g_ = groups[gi]
        fw.dma(pool, xs_t[gi % 2][:], xT[s_].rearrange("(kc p) n -> p kc n", p=128)[:, :, g_ * T:(g_ + 1) * T],
               writes=xs_b[gi % 2])

    load_x(0)
    for idx, (s, g, l) in enumerate(steps):
        nxt_l = steps[idx + 1][2] if idx + 1 < len(steps) else None
        gi = s * ngroups + g
        XC[0], XC[1] = xs_t[gi % 2], xs_b[gi % 2]
        t0 = g * T
        if l == 0 and g == 0:
            for ll in range(nlayers):
                fw.op(dve, lambda h, ll=ll: h.memset(phalo[ll][:], 0.0), [], [b_phalo[ll]])
                fw.op(dve, lambda h, ll=ll: h.memset(zhalo[ll][:], 0.0), [], b_zhalo[ll])
        mixer(l, s, g, nxt_l)
        if dbg == (s, g, l, "mix"):
            fw.dma(pool, dbg_out, XC[0][:], reads=XC[1])
        if l == nlayers - 1 and gi + 1 < len(groups):
            load_x(gi + 1)
        ffn(l, s, g)
        if dbg == (s, g, l, "ffn"):
            fw.dma(pool, dbg_out, XC[0][:], reads=XC[1])
        if l == nlayers - 1:
            fw.dma(pool, outT[s].rearrange("(kc p) n -> p kc n", p=128)[:, :, t0:t0 + T], XC[0][:], reads=XC[1])
    fw.final_fence(sp)
    fw.emit()
    return nc, fw


def _host_prep(inp):
    f = lambda a: np.ascontiguousarray(np.asarray(a, dtype=np.float32))
    L = DEPTH
    w_in = np.asarray(inp["w_in"], np.float32)
    w_uq = np.asarray(inp["w_uq"], np.float32)
    w_ukv = np.asarray(inp["w_ukv"], np.float32)
    perm = np.concatenate([np.arange(32, 64), np.arange(0, 32)])
    shared = {
        "w_ada": f(inp["w_ada"]),
        "w_in": f(w_in),
        "w_krs": f(w_in[:, :, OFF_KR:OFF_KR + 64][:, :, perm]),
        "w_pool": f(inp["w_pool"]),
        "w_uqn": f(w_uq[:, :, :, 0:128].reshape(L, 256, 1024)),
        "w_uqr": f(w_uq[:, :, :, 128:192].reshape(L, 256, 512)),
        "w_uqrs": f(w_uq[:, :, :, 128:192][:, :, :, perm].reshape(L, 256, 512)),
        "w_ukT": f(np.transpose(w_ukv[:, :, :, 0:128], (0, 3, 2, 1)).reshape(L, 128, 1024)),
        "w_uv": f(w_ukv[:, :, :, 128:256].reshape(L, 128, 1024)),
        "wsT": f(np.transpose(np.asarray(inp["w_s"], np.float32), (0, 3, 1, 2)).reshape(L, 128, 1024)),
        "bsT": f(np.transpose(np.asarray(inp["b_s"], np.float32), (0, 2, 1)).reshape(L, 1, 1024)),
        "gsg": f(np.asarray(inp["g_sg"], np.float32).reshape(L, 1, 1024)),
        "bsg": f(np.asarray(inp["b_sg"], np.float32).reshape(L, 1, 1024)),
        "w_branch": f(inp["w_branch"]),
        "w_o": f(inp["w_o"]),
        "w_up": f(inp["w_up"]),
        "w_down": f(inp["w_down"]),
    }
    pp = lambda v, n: np.transpose(np.asarray(v, np.float32).reshape(L, n, 128), (0, 2, 1))
    vecs = np.zeros((L, 128, NV), np.float32)
    vecs[:, :, V_SPOOL:V_SPOOL + 8] = pp(inp["s_pool"], 8)
    vecs[:, :, V_GQ:V_GQ + 2] = pp(inp["g_q"], 2)
    vecs[:, :, V_GKV:V_GKV + 1] = pp(inp["g_kv"], 1)
    vecs[:, :, V_LNTG:V_LNTG + 8] = pp(inp["ln_t_g"], 8)
    vecs[:, :, V_LNTB:V_LNTB + 8] = pp(inp["ln_t_b"], 8)
    vecs[:, :, V_LNFG:V_LNFG + 8] = pp(inp["ln_f_g"], 8)
    vecs[:, :, V_LNFB:V_LNFB + 8] = pp(inp["ln_f_b"], 8)
    cw = np.asarray(inp["conv_w"], np.float32)
    for k in range(3):
        vecs[:, :, V_CW + k * 44:V_CW + (k + 1) * 44] = pp(cw[:, k], 44)
    vecs[:, :, V_CB:V_CB + 44] = pp(inp["conv_b"], 44)
    vecs[:, :, V_BADA:V_BADA + 48] = pp(inp["b_ada"], 48)
    shared["vecs"] = vecs
    consts = np.zeros((128, NCONST), np.float32)
    consts[:, C_IDENT:C_IDENT + 128] = np.eye(128, dtype=np.float32)
    consts[:, C_MASK:C_MASK + 128] = np.triu(np.ones((128, 128), np.float32))
    freqs = (10000.0 ** (-np.arange(0, 64, 2, dtype=np.float32) / 64)).astype(np.float32)
    consts[0:64, C_FREQ] = np.concatenate([freqs, freqs])
    consts[0:32, C_SGN] = -1.0
    consts[32:64, C_SGN] = 1.0
    for wi, w in enumerate((2, 4, 8, 16)):
        tt = np.arange(16)
        consts[:, C_CORR + wi * 16:C_CORR + (wi + 1) * 16] = (w / np.minimum(tt + 1, w)).astype(np.float32)[None, :]
    shared["consts"] = consts
    return shared


_CACHE = {}


def kernel(**inputs):
    x = np.asarray(inputs["x"], np.float32)
    c = np.asarray(inputs["c"], np.float32)
    pos = np.asarray(inputs["pos"], np.int32)
    shared = _host_prep(inputs)
    key = "full"
    if key not in _CACHE:
        _CACHE[key] = build_program()
    nc, _ = _CACHE[key]
    in_maps = []
    for core in range(NCORES):
        b0 = core * SPC
        m = dict(shared)
        m["xT"] = np.ascontiguousarray(np.transpose(x[b0:b0 + SPC], (0, 2, 1)))
        m["cT"] = np.ascontiguousarray(np.transpose(c[b0:b0 + SPC].reshape(SPC, 8, 128), (2, 1, 0)))
        m["pos"] = np.ascontiguousarray(pos[b0:b0 + SPC])
        in_maps.append(m)
    res = run_bass_kernel_spmd(nc, in_maps, core_ids=list(range(NCORES)))
    out = np.empty((BATCH, SEQ, D), np.float32)
    for core in range(NCORES):
        o = res.results[core]["outT"]
        out[core * SPC:(core + 1) * SPC] = np.transpose(o, (0, 2, 1))
    return out
```

```python
import numpy as np
import concourse.bass as bass
import concourse.mybir as mybir
from concourse.bass_utils import run_bass_kernel_spmd

F32 = mybir.dt.float32
BF16 = mybir.dt.bfloat16
I32 = mybir.dt.int32
AF = mybir.ActivationFunctionType
ALU = mybir.AluOpType

D = 1024
SEQ = 2048
BATCH = 32
DEPTH = 2
NCORES = 8
SPC = BATCH // NCORES
T = 512
NG = SEQ // T
D_FF = 2816
NF = D_FF // 128
OFF_POOL = 3 * D
OFF_CQ = OFF_POOL + D
OFF_CKV = OFF_CQ + 256
OFF_KR = OFF_CKV + 128
OFF_SG = OFF_KR + 64
IN_WIDTH = OFF_SG + 2 * D
ALPHA = (2 * DEPTH) ** 0.25
LN_EPS = 1e-5
RMS_EPS = 1e-6
ATT_SCALE = 192 ** -0.5
POOL_W = (2, 2, 4, 4, 8, 8, 16, 16)

V_SPOOL = 0
V_GQ = 8
V_GKV = 10
V_LNTG = 11
V_LNTB = 19
V_LNFG = 27
V_LNFB = 35
V_CW = 43
V_CB = V_CW + 132
V_BADA = V_CB + 44
NV = V_BADA + 48
C_IDENT = 0
C_MASK = 128
C_FREQ = 256
C_SGN = 257
C_CORR = 258
NCONST = C_CORR + 64

SEM_EPOCH = 8000


class Buf:
    __slots__ = ("w", "r", "name")

    def __init__(self, name=""):
        self.w = {}
        self.r = {}
        self.name = name


class Eng:
    def __init__(self, fw, name, is_pe=False):
        self.fw = fw
        self.name = name
        self.is_pe = is_pe
        self.prog = []
        self.waited = {}
        self.sems = []
        self.count = 0
        self._new_sem()

    def _new_sem(self):
        s = self.fw.new_sem(f"{self.name}_p{len(self.sems)}")
        self.sems.append(s)
        self.sem = s
        self.count = 0


class DmaSem:
    def __init__(self, fw, name):
        self.sem = fw.new_sem(name)
        self.value = 0


class FW:
    def __init__(self, nc, same_engine_sync=True):
        self.nc = nc
        self.same_sync = same_engine_sync
        self.pe = Eng(self, "pe", is_pe=True)
        self.act = Eng(self, "act")
        self.dve = Eng(self, "dve")
        self.pool = Eng(self, "pool")
        self.sp = Eng(self, "sp")
        self.dma_pool = {}
        self.retired = []
        self.n_inst = 0
        self.phase = 'setup'
        self.phases = {}

    def new_sem(self, name):
        return self.nc.alloc_semaphore(name=name)

    def _collect(self, eng, reads, writes):
        need = {}

        def add(d, raw):
            for s, v in d.items():
                if s in eng.sems:
                    if not (raw and self.same_sync and not eng.is_pe):
                        continue
                if need.get(s, 0) < v:
                    need[s] = v

        for b in reads:
            add(b.w, True)
        for b in writes:
            add(b.w, True)
            add(b.r, False)
        waits = []
        for s, v in need.items():
            if eng.waited.get(s, 0) < v:
                eng.waited[s] = v
                waits.append((s, v))
        return waits

    @staticmethod
    def _update(stamp, reads, writes):
        s, v = stamp
        for b in writes:
            b.w = {s: v}
            b.r = {}
        for b in reads:
            if b in writes:
                continue
            b.r[s] = v

    def op(self, eng, fn, reads=(), writes=(), signal=True):
        reads = list(reads)
        writes = list(writes)
        waits = self._collect(eng, reads, writes)
        sem = eng.sem
        stamp = (sem, eng.count + 1)
        self.n_inst += 1

        def emit(h, waits=waits, fn=fn, signal=signal, sem=sem):
            for s, v in waits:
                h.wait_ge(s, v)
            ins = fn(h)
            if signal:
                ins.then_inc(sem, 1)

        eng.prog.append(emit)
        self.phases.setdefault(eng.name, []).append(self.phase)
        self._update(stamp, reads, writes)
        if signal:
            eng.count += 1
            if eng.count >= SEM_EPOCH:
                eng._new_sem()
        return stamp

    def dma(self, eng, out_ap, in_ap, reads=(), writes=(), dsem=None, **kw):
        reads = list(reads)
        writes = list(writes)
        if dsem is None:
            dsem = self.get_dma_sem(eng)
        waits = self._collect(eng, reads, writes)
        if dsem.value > 0 and eng.waited.get(dsem.sem, 0) < dsem.value:
            eng.waited[dsem.sem] = dsem.value
            waits.append((dsem.sem, dsem.value))
        dsem.value += 16
        stamp = (dsem.sem, dsem.value)
        self.n_inst += 1

        def emit(h, waits=waits, out_ap=out_ap, in_ap=in_ap, kw=kw, sem=dsem.sem):
            for s, v in waits:
                h.wait_ge(s, v)
            h.dma_start(out=out_ap, in_=in_ap, **kw).then_inc(sem, 16)

        eng.prog.append(emit)
        self._update(stamp, reads, writes)
        return stamp

    def get_dma_sem(self, eng, n=8):
        key = eng.name
        if key not in self.dma_pool:
            self.dma_pool[key] = [[DmaSem(self, f"dma_{key}_{i}") for i in range(n)], 0]
        lst = self.dma_pool[key]
        i = lst[1] % n
        d = lst[0][i]
        if d.value >= 2048:
            self.retired.append(d)
            d = DmaSem(self, f"dma_{key}_{i}_{len(self.retired)}")
            lst[0][i] = d
        lst[1] += 1
        return d

    def final_fence(self, eng):
        sems = [d for lst, _ in self.dma_pool.values() for d in lst if d.value] + list(self.retired)

        def emit(h, sems=sems):
            for d in sems:
                h.wait_ge(d.sem, d.value)

        eng.prog.append(emit)

    def emit(self):
        with self.nc.Block() as block:
            @block.tensor
            def _(h):
                for f in self.pe.prog:
                    f(h)

            @block.scalar
            def _(h):
                for f in self.act.prog:
                    f(h)

            @block.vector
            def _(h):
                for f in self.dve.prog:
                    f(h)

            @block.gpsimd
            def _(h):
                for f in self.pool.prog:
                    f(h)

            @block.sync
            def _(h):
                for f in self.sp.prog:
                    f(h)


class Region:
    def __init__(self, nc, name, npages, page_words):
        self.t = nc.alloc_sbuf_tensor(name, [128, npages * page_words], F32)
        self.pw = page_words
        self.np = npages
        self.bufs = [Buf(f"{name}{i}") for i in range(npages)]

    def f32(self, page, n, off=0, parts=128):
        a = page * self.pw + off
        return self.t[0:parts, a:a + n]

    def bf16(self, page, n, off=0, parts=128):
        a = page * self.pw
        words = (off + n + 1) // 2
        v = self.t[0:parts, a:a + words].bitcast(BF16)
        return v[:, off:off + n]

    def b(self, page, npg=1):
        return self.bufs[page:page + npg]


class PsumPool:
    def __init__(self, nc):
        self.t = nc.alloc_psum_tensor("ps", [128, 8, 512], F32)
        self.bufs = [Buf(f"ps{i}") for i in range(8)]
        self.free = {i: i for i in range(8)}
        self.clock = 8

    def get(self):
        b = min(self.free, key=lambda k: self.free[k])
        del self.free[b]
        return b

    def get_pair(self):
        cands = [i for i in (0, 2, 4, 6) if i in self.free and i + 1 in self.free]
        assert cands, "no free psum pair"
        b = min(cands, key=lambda k: max(self.free[k], self.free[k + 1]))
        del self.free[b]
        del self.free[b + 1]
        return b

    def put(self, b):
        self.clock += 1
        self.free[b] = self.clock


def build_program(nseq=SPC, ngroups=NG, nlayers=DEPTH, dbg=None):
    nc = bass.Bass("TRN2", target_bir_lowering=False)
    dt_in = lambda name, shape, dt=F32: nc.dram_tensor(name, list(shape), dt, kind="ExternalInput").ap()
    xT = dt_in("xT", [nseq, D, SEQ])
    cT = dt_in("cT", [128, 8, nseq])
    pos = dt_in("pos", [nseq, SEQ], I32)
    consts = dt_in("consts", [128, NCONST])
    vecs = dt_in("vecs", [DEPTH, 128, NV])
    w_ada = dt_in("w_ada", [DEPTH, D, 6 * D])
    w_in = dt_in("w_in", [DEPTH, D, IN_WIDTH])
    w_krs = dt_in("w_krs", [DEPTH, D, 64])
    w_pool = dt_in("w_pool", [DEPTH, 4, 256, 256])
    w_uqn = dt_in("w_uqn", [DEPTH, 256, 1024])
    w_uqr = dt_in("w_uqr", [DEPTH, 256, 512])
    w_uqrs = dt_in("w_uqrs", [DEPTH, 256, 512])
    w_ukT = dt_in("w_ukT", [DEPTH, 128, 1024])
    w_uv = dt_in("w_uv", [DEPTH, 128, 1024])
    wsT = dt_in("wsT", [DEPTH, 128, 1024])
    bsT = dt_in("bsT", [DEPTH, 1, 1024])
    gsg = dt_in("gsg", [DEPTH, 1, 1024])
    bsg = dt_in("bsg", [DEPTH, 1, 1024])
    w_branch = dt_in("w_branch", [DEPTH, 3, D, D])
    w_o = dt_in("w_o", [DEPTH, D, D])
    w_up = dt_in("w_up", [DEPTH, D, 2 * D_FF])
    w_down = dt_in("w_down", [DEPTH, D_FF, D])
    outT = nc.dram_tensor("outT", [nseq, D, SEQ], F32, kind="ExternalOutput").ap()
    dbg_out = None
    if dbg is not None:
        dbg_out = nc.dram_tensor("dbg", [128, 8, T], F32, kind="ExternalOutput").ap()

    fw = FW(nc)
    pe, act, dve, pool, sp = fw.pe, fw.act, fw.dve, fw.pool, fw.sp
    A = nc.alloc_sbuf_tensor

    xs_t = [A(f"x_sb{i}", [128, 8, T], F32) for i in range(2)]
    xs_b = [[Buf(f"x{i}_{c}") for c in range(8)] for i in range(2)]
    XC = [xs_t[0], xs_b[0]]
    h_sb = A("h_sb", [128, 8, T], BF16)
    b_h = [Buf(f"h{c}") for c in range(8)]
    MB = Region(nc, "mb", 12, 512)
    merged = lambda d: MB.f32(d, 512)
    b_merged = lambda d: MB.b(d)
    branch = lambda c, lo=0, hi=T: MB.bf16(8 + c // 2, hi - lo, (c % 2) * 512 + lo)
    b_branch = lambda c: MB.b(8 + c // 2)
    actT = lambda f: MB.bf16(f // 2, 512, (f % 2) * 512)
    b_actT = lambda f: MB.b(f // 2)
    AR = Region(nc, "ar", 14, 544)
    st_mean = A("st_mean", [128, T], F32); b_mean = Buf("mean")
    st_var = A("st_var", [128, T], F32); b_var = Buf("var")
    st_rstd = A("st_rstd", [128, T], F32); b_rstd = Buf("rstd")
    NSLOT = 5
    RING = Region(nc, "ring", NSLOT, 2048)
    ring_i = [0]
    wpool_sb = A("wpool_sb", [128, 2, 4, 256], BF16); b_wpool = Buf("wpool")
    wuqn_sb = A("wuqn_sb", [128, 2, 1024], BF16); b_wuqn = Buf("wuqn")
    wuqr_sb = A("wuqr_sb", [128, 2, 512], BF16); b_wuqr = Buf("wuqr")
    wuqrs_sb = A("wuqrs_sb", [128, 2, 512], BF16); b_wuqrs = Buf("wuqrs")
    wukT_sb = A("wukT_sb", [128, 1024], BF16); b_wukT = Buf("wukT")
    wuv_sb = A("wuv_sb", [128, 1024], BF16); b_wuv = Buf("wuv")
    wsT_sb = A("wsT_sb", [128, 1024], BF16); b_wsT = Buf("wsT")
    bsT_sb = A("bsT_sb", [1, 1024], BF16); b_bsT = Buf("bsT")
    gsg_sb = A("gsg_sb", [128, 1024], F32); b_gsg = Buf("gsg")
    bsg_sb = A("bsg_sb", [128, 1024], F32); b_bsg = Buf("bsg")
    vecs_sb = A("vecs_sb", [128, DEPTH, NV], F32); b_vecs = Buf("vecs")
    mod_sb = A("mod_sb", [128, DEPTH * 48 * nseq], F32); b_mod = Buf("mod")
    c_sb = A("c_sb", [128, 8, nseq], F32); b_c = Buf("c")
    const_sb = A("const_sb", [128, NCONST], F32); b_const = Buf("const")
    ident_bf = A("ident_bf", [128, 128], BF16)
    mask_bf = A("mask_bf", [128, 128], BF16)
    ones_bf = A("ones_bf", [128, 128], BF16)
    onesm_bf = A("onesm_bf", [128, 128], BF16)
    ones_f = A("ones_f", [128, 128], F32)
    eps_ln = A("eps_ln", [128, 1], F32)
    eps_rms = A("eps_rms", [128, 1], F32)
    b_cst2 = Buf("cst2")
    ckvnT = [A(f"ckvnT{l}", [128, SEQ], BF16) for l in range(DEPTH)]
    krotT = [A(f"krotT{l}", [64, SEQ], BF16) for l in range(DEPTH)]
    Vc = [A(f"Vc{l}", [128, SEQ // 128, 128], BF16) for l in range(DEPTH)]
    b_kc = [[Buf(f"kc{l}_{j}") for j in range(NG)] for l in range(DEPTH)]
    b_kr = [[Buf(f"kr{l}_{j}") for j in range(NG)] for l in range(DEPTH)]
    b_vc = [[Buf(f"vc{l}_{j}") for j in range(NG)] for l in range(DEPTH)]
    cos2 = A("cos2", [64, T], F32); b_cos = Buf("cos")
    sinpm = A("sinpm", [64, T], F32); b_sin = Buf("sin")
    cqn = A("cqn", [128, 2, T], BF16); b_cqn = Buf("cqn")
    phalo = [A(f"phalo{l}", [128, 8, 16], F32) for l in range(DEPTH)]
    b_phalo = [Buf(f"phalo{l}") for l in range(DEPTH)]
    zhalo = [A(f"zhalo{l}", [128, 2 * NF, 2], F32) for l in range(DEPTH)]
    b_zhalo = [[Buf(f"zhalo{l}_{c}") for c in range(2 * NF)] for l in range(DEPTH)]
    PS = PsumPool(nc)

    def ACT(out, in_, func, reads, writes, **kw):
        fw.op(act, lambda h: h.activation(out=out, in_=in_, func=func, **kw), reads, writes)

    def TT(out, in0, in1, op, reads, writes, eng=dve):
        fw.op(eng, lambda h: h.tensor_tensor(out=out, in0=in0, in1=in1, op=op), reads, writes)

    def STT(out, in0, scalar, in1, op0, op1, reads, writes, eng=dve):
        fw.op(eng, lambda h: h.scalar_tensor_tensor(out=out, in0=in0, scalar=scalar, in1=in1, op0=op0, op1=op1),
              reads, writes)

    def TS(out, in0, s1, s2, op0, op1, reads, writes, eng=dve):
        if s2 is None:
            fw.op(eng, lambda h: h.tensor_scalar(out=out, in0=in0, scalar1=s1, scalar2=None, op0=op0), reads, writes)
        else:
            fw.op(eng, lambda h: h.tensor_scalar(out=out, in0=in0, scalar1=s1, scalar2=s2, op0=op0, op1=op1),
                  reads, writes)

    def CP(out, in_, reads, writes, eng=dve):
        fw.op(eng, lambda h: h.tensor_copy(out=out, in_=in_), reads, writes)

    def MM(out, lhsT, rhs, start, stop, reads, writes, signal=None):
        signal = True
        fw.op(pe, lambda h: h.matmul(out, lhsT=lhsT, rhs=rhs, start=start, stop=stop), reads, writes, signal=signal)

    def psb(b, lo=0, hi=T, parts=128):
        return PS.t[0:parts, b, lo:hi]

    SCR = {}

    def mk_scratch(key, src2d):
        rows, cols = src2d.shape
        t = nc.dram_tensor("scr_" + "_".join(str(k) for k in key), [rows, cols], BF16, kind="Internal").ap()
        bufs = []
        for r0 in range(0, rows, 128):
            b = Buf("scr")
            fw.dma(pool, t[r0:r0 + 128, :], src2d[r0:r0 + 128, :], writes=[b])
            bufs.append(b)
        SCR[key] = (t, bufs)

    def scols(key, c0, n):
        ap, bufs = SCR[key]
        return ap.rearrange("(kc p) n -> p kc n", p=128)[:, :, c0:c0 + n], bufs

    def wload(src, eng=None):
        src_ap, sbufs = src
        slot = ring_i[0] % NSLOT
        ring_i[0] += 1
        nel = 1
        for s_ in src_ap.shape[1:]:
            nel *= s_
        dst = RING.bf16(slot, nel).rearrange("p (k n) -> p k n", k=src_ap.shape[1])
        fw.dma(eng or sp, dst, src_ap, reads=sbufs, writes=RING.b(slot))
        return slot

    def ring_view(slot, k, n, tot=None):
        tot = tot or n
        v = RING.bf16(slot, k * tot).rearrange("p (k n) -> p k n", k=k)
        return v

    def mcol(l, kind, c, s):
        i = ((l * 6 + kind) * 8 + c) * nseq + s
        return mod_sb[:, i:i + 1]

    def vcol(l, off, c=0):
        return vecs_sb[:, l, off + c:off + c + 1]

    fw.dma(sp, const_sb[:], consts, writes=[b_const])
    fw.dma(sp, vecs_sb[:], vecs.rearrange("l p n -> p l n"), writes=[b_vecs])
    fw.dma(sp, c_sb[:], cT, writes=[b_c])
    CP(ident_bf[:], const_sb[:, C_IDENT:C_IDENT + 128], [b_const], [b_cst2])
    CP(mask_bf[:], const_sb[:, C_MASK:C_MASK + 128], [b_const], [b_cst2])
    fw.op(dve, lambda h: h.memset(ones_bf[:], 1.0), writes=[b_cst2])
    fw.op(dve, lambda h: h.memset(onesm_bf[:], 1.0 / D), writes=[b_cst2])
    fw.op(dve, lambda h: h.memset(ones_f[:], 1.0), writes=[b_cst2])
    fw.op(dve, lambda h: h.memset(eps_ln[:], LN_EPS / (ALPHA * ALPHA)), writes=[b_cst2])
    fw.op(dve, lambda h: h.memset(eps_rms[:], RMS_EPS), writes=[b_cst2])
    eps_ln1 = A("eps_ln1", [128, 1], F32)
    fw.op(dve, lambda h: h.memset(eps_ln1[:], LN_EPS), writes=[b_cst2])

    for l_ in range(nlayers):
        mk_scratch(("w_in", l_), w_in[l_])
        for bi_ in range(3):
            mk_scratch(("w_br", l_, bi_), w_branch[l_, bi_])
        mk_scratch(("w_o", l_), w_o[l_])
        mk_scratch(("w_up", l_), w_up[l_])
        mk_scratch(("w_down", l_), w_down[l_])

    cact = A("cact", [128, 8, nseq], F32); b_cact = Buf("cact")
    ACT(cact[:], c_sb[:], AF.Silu, [b_c], [b_cact])
    pm = PS.get()
    for l in range(DEPTH):
        for blk in range(24):
            slot = ring_i[0] % NSLOT
            ring_i[0] += 1
            stage = RING.f32(slot, 2048).rearrange("p (k n) -> p k n", k=8)
            fw.dma(sp, stage, w_ada[l].rearrange("(kc p) n -> p kc n", p=128)[:, :, blk * 256:(blk + 1) * 256],
                   writes=RING.b(slot))
            for jj in range(2):
                j = blk * 2 + jj
                col = (l * 48 + j) * nseq
                for kc in range(8):
                    MM(PS.t[:, pm, col:col + nseq], stage[:, kc, jj * 128:(jj + 1) * 128], cact[:, kc, :],
                       kc == 0, kc == 7, RING.b(slot) + [b_cact], [PS.bufs[pm]])
    for l in range(DEPTH):
        n48 = 48 * nseq
        TT(mod_sb[:, l * n48:(l + 1) * n48].rearrange("p (j s) -> p j s", s=nseq),
           PS.t[:, pm, l * n48:(l + 1) * n48].rearrange("p (j s) -> p j s", s=nseq),
           vecs_sb[:, l, V_BADA:V_BADA + 48].unsqueeze(2).to_broadcast([128, 48, nseq]),
           ALU.add, [PS.bufs[pm], b_vecs], [b_mod])
        for kind in (1, 4):
            a0 = (l * 6 + kind) * 8 * nseq
            TS(mod_sb[:, a0:a0 + 8 * nseq], mod_sb[:, a0:a0 + 8 * nseq], 1.0, None, ALU.add, None, [b_mod], [b_mod])
        for kind in (2, 5):
            a0 = (l * 6 + kind) * 8 * nseq
            TS(mod_sb[:, a0:a0 + 8 * nseq], mod_sb[:, a0:a0 + 8 * nseq], 1.0 / ALPHA, None, ALU.mult, None,
               [b_mod], [b_mod])
    PS.put(pm)

    def load_wpool(l):
        for gq in range(4):
            fw.dma(pool, wpool_sb[:, :, gq, :], w_pool[l, gq].rearrange("(i p) d -> p i d", p=128), writes=[b_wpool])

    def load_attw(l):
        fw.dma(pool, wuqn_sb[:], w_uqn[l].rearrange("(i p) n -> p i n", p=128), writes=[b_wuqn])
        fw.dma(pool, wuqr_sb[:], w_uqr[l].rearrange("(i p) n -> p i n", p=128), writes=[b_wuqr])
        fw.dma(pool, wuqrs_sb[:], w_uqrs[l].rearrange("(i p) n -> p i n", p=128), writes=[b_wuqrs])
        fw.dma(pool, wukT_sb[:], w_ukT[l], writes=[b_wukT])
        fw.dma(pool, wuv_sb[:], w_uv[l], writes=[b_wuv])

    def load_sguw(l):
        fw.dma(pool, wsT_sb[:], wsT[l], writes=[b_wsT])
        fw.dma(pool, bsT_sb[:], bsT[l], writes=[b_bsT])
        fw.dma(pool, gsg_sb[:], gsg[l].partition_broadcast(128), writes=[b_gsg])
        fw.dma(pool, bsg_sb[:], bsg[l].partition_broadcast(128), writes=[b_bsg])
        TT(wsT_sb[:].rearrange("p (g t) -> p g t", g=8), wsT_sb[:].rearrange("p (g t) -> p g t", g=8),
           mask_bf[:].unsqueeze(1).to_broadcast([128, 8, 128]), ALU.mult, [b_wsT, b_cst2], [b_wsT])

    def ln_stats(eps_tile):
        xb = lambda c: AR.bf16(c // 2, T, (c % 2) * T)
        xq = lambda c: AR.bf16(4 + c // 2, T, (c % 2) * T)
        pmn = PS.get()
        psq = PS.get()
        for pp in range(4):
            xb2 = AR.bf16(pp, 2 * T).rearrange("p (a n) -> p a n", a=2)
            xq2 = AR.bf16(4 + pp, 2 * T).rearrange("p (a n) -> p a n", a=2)
            xin = XC[0][:, 2 * pp:2 * pp + 2, :]
            rb = [XC[1][2 * pp], XC[1][2 * pp + 1]]
            CP(xb2, xin, rb, AR.b(pp))
            ACT(xq2, xin, AF.Square, rb, AR.b(4 + pp))
            for c in (2 * pp, 2 * pp + 1):
                MM(psb(pmn), onesm_bf[:], xb(c), c == 0, c == 7, AR.b(c // 2) + [b_cst2], [PS.bufs[pmn]])
                MM(psb(psq), onesm_bf[:], xq(c), c == 0, c == 7, AR.b(4 + c // 2) + [b_cst2], [PS.bufs[psq]])
        CP(st_mean[:], psb(pmn), [PS.bufs[pmn]], [b_mean])
        TT(st_var[:], st_mean[:], st_mean[:], ALU.mult, [b_mean], [b_var])
        TT(st_var[:], psb(psq), st_var[:], ALU.subtract, [PS.bufs[psq], b_var], [b_var])
        PS.put(pmn)
        PS.put(psq)
        ACT(st_var[:], st_var[:], AF.Ln, [b_var, b_cst2], [b_var], bias=eps_tile[:, 0:1])
        ACT(st_rstd[:], st_var[:], AF.Exp, [b_var], [b_rstd], scale=-0.5)

    def ln_apply(out_fn, out_bufs, scale_fn, bias_fn, extra_reads):
        for c in range(8):
            pg = 8 + (c % 4)
            t = AR.f32(pg, T)
            TT(t, XC[0][:, c, :], st_mean[:], ALU.subtract, [XC[1][c], b_mean], AR.b(pg), eng=pool)
            TT(t, t, st_rstd[:], ALU.mult, AR.b(pg) + [b_rstd], AR.b(pg))
            ACT(out_fn(c), t, AF.Identity, AR.b(pg) + extra_reads, out_bufs(c), scale=scale_fn(c), bias=bias_fn(c))

    def ln_mod(l, s, kind):
        ln_stats(eps_ln1)
        ln_apply(lambda c: h_sb[:, c, :], lambda c: [b_h[c]],
                 lambda c: mcol(l, 3 * kind + 1, c, s), lambda c: mcol(l, 3 * kind, c, s), [b_mod])

    def post_ln(l, goff, boff):
        ln_stats(eps_ln)
        ln_apply(lambda c: XC[0][:, c, :], lambda c: [XC[1][c]],
                 lambda c: vcol(l, goff, c), lambda c: vcol(l, boff, c), [b_vecs])

    def proj8(w_slot, col0, rhs_fn, rhs_bufs, ncols=512):
        b = PS.get()
        wv = ring_view(w_slot, 8, ncols)
        for k in range(8):
            MM(psb(b), wv[:, k, col0:col0 + 128], rhs_fn(k), k == 0, k == 7,
               RING.b(w_slot) + rhs_bufs(k), [PS.bufs[b]])
        return b

    def proj8_kouter(specs, rhs_fn, rhs_bufs):
        banks = [PS.get() for _ in specs]
        for k in range(8):
            for (w_slot, col0, ncols), b in zip(specs, banks):
                wv = ring_view(w_slot, 8, ncols)
                MM(psb(b), wv[:, k, col0:col0 + 128], rhs_fn(k), k == 0, k == 7,
                   RING.b(w_slot) + rhs_bufs(k), [PS.bufs[b]])
        return banks

    def w_cols(w_l, c0, n):
        return w_l.rearrange("(kc p) n -> p kc n", p=128)[:, :, c0:c0 + n]

    def branch_out(l, bi, first):
        for half in range(2):
            ws = wload(scols(("w_br", l, bi), half * 512, 512))
            gs = wload(scols(("w_in", l), bi * D + half * 512, 512))
            for dd in range(4):
                d = half * 4 + dd
                by = proj8(ws, dd * 128, lambda k: branch(k), lambda k: b_branch(k))
                bg = proj8(gs, dd * 128, lambda k: h_sb[:, k, :], lambda k: [b_h[k]])
                pg = 12 + (d % 2)
                sig = AR.f32(pg, T)
                ACT(sig, psb(bg), AF.Sigmoid, [PS.bufs[bg]], AR.b(pg))
                PS.put(bg)
                if first:
                    TT(merged(d), psb(by), sig, ALU.mult, [PS.bufs[by]] + AR.b(pg), b_merged(d))
                else:
                    TT(sig, psb(by), sig, ALU.mult, [PS.bufs[by]] + AR.b(pg), AR.b(pg))
                    TT(merged(d), merged(d), sig, ALU.add, b_merged(d) + AR.b(pg), b_merged(d))
                PS.put(by)

    TWO_PI = 2.0 * np.pi
    MAGIC = 12582912.0
    CW1 = 6.28125
    CW2 = float(np.float32(TWO_PI - CW1))
    CW3 = float(TWO_PI - CW1 - np.float64(np.float32(TWO_PI - CW1)))
    PI_LO = 3.1415925

    def rope_tables(s, g):
        t0 = g * T
        pi_ = AR.t[0:64, 0:T].bitcast(I32)
        fw.dma(pool, pi_, pos[s:s + 1, t0:t0 + T].partition_broadcast(64), writes=AR.b(0))
        ang = AR.f32(1, T, parts=64)
        CP(ang, pi_, AR.b(0), AR.b(1))
        TS(ang, ang, const_sb[0:64, C_FREQ:C_FREQ + 1], None, ALU.mult, None, AR.b(1) + [b_const], AR.b(1))
        for which, outt, ob in ((0, sinpm, b_sin), (1, cos2, b_cos)):
            kk = AR.f32(2, T, parts=64)
            r = AR.f32(3, T, parts=64)
            if which == 0:
                TS(kk, ang, float(1.0 / TWO_PI), MAGIC, ALU.mult, ALU.add, AR.b(1), AR.b(2))
            else:
                TS(kk, ang, float(1.0 / TWO_PI), 0.25, ALU.mult, ALU.add, AR.b(1), AR.b(2))
                TS(kk, kk, MAGIC, None, ALU.add, None, AR.b(2), AR.b(2))
            TS(kk, kk, -MAGIC, None, ALU.add, None, AR.b(2), AR.b(2))
            STT(r, kk, -CW1, ang, ALU.mult, ALU.add, AR.b(1) + AR.b(2), AR.b(3))
            STT(r, kk, -CW2, r, ALU.mult, ALU.add, AR.b(2) + AR.b(3), AR.b(3))
            STT(r, kk, -CW3, r, ALU.mult, ALU.add, AR.b(2) + AR.b(3), AR.b(3))
            if which == 1:
                TS(r, r, float(np.pi / 2), None, ALU.add, None, AR.b(3), AR.b(3))
            TS(r, r, PI_LO, -PI_LO, ALU.min, ALU.max, AR.b(3), AR.b(3))
            ACT(outt[:], r, AF.Sin, AR.b(3), [ob])
        TS(sinpm[:], sinpm[:], const_sb[0:64, C_SGN:C_SGN + 1], None, ALU.mult, None, [b_sin, b_const], [b_sin])

    def mixer(l, s, g, nxt_l):
        t0 = g * T
        fw.phase = 'ln_mod_t'
        ln_mod(l, s, 0)
        fw.phase = 'pool'

        for half in range(2):
            wa = wload(scols(("w_in", l), OFF_POOL + half * 512, 512))
            pre = None
            if half == 0:
                pre = proj8_kouter([(wa, cc_ * 128, 512) for cc_ in range(4)], lambda k: h_sb[:, k, :], lambda k: [b_h[k]])
            for cc in range(4):
                c = half * 4 + cc
                w = POOL_W[c]
                ba = pre[cc] if pre else proj8(wa, cc * 128, lambda k: h_sb[:, k, :], lambda k: [b_h[k]])
                P0, P1, P2 = 0 + 3 * (c % 2), 1 + 3 * (c % 2), 2 + 3 * (c % 2)
                a = AR.f32(P0, 528)
                ce = pool if c in (1, 3, 5) else dve
                CP(a[:, 0:16], phalo[l][:, c, :], [b_phalo[l]], AR.b(P0))
                ACT(a[:, 16:528], psb(ba), AF.Copy, [PS.bufs[ba]], AR.b(P0))
                PS.put(ba)
                CP(phalo[l][:, c, :], a[:, 512:528], AR.b(P0), [b_phalo[l]])
                cur, curp = a, P0
                sh = 1
                lo = 1
                others = [P1, P2]
                oi = 0
                while sh < w:
                    np_ = others[oi % 2]
                    oi += 1
                    nt = AR.f32(np_, 528)
                    TT(nt[:, lo:528], cur[:, lo:528], cur[:, lo - sh:528 - sh], ALU.add, AR.b(curp), AR.b(np_), eng=ce)
                    cur, curp = nt, np_
                    sh *= 2
                    lo = lo + sh
                if g == 0:
                    wi = {2: 0, 4: 1, 8: 2, 16: 3}[w]
                    TT(cur[:, 16:32], cur[:, 16:32], const_sb[:, C_CORR + wi * 16:C_CORR + wi * 16 + 16], ALU.mult,
                       AR.b(curp) + [b_const], AR.b(curp))
                STT(branch(c), cur[:, 16:528], 1.0 / w, a[:, 16:528], ALU.mult, ALU.subtract,
                    AR.b(curp) + AR.b(P0), b_branch(c))
        for gq in range(4):
            bm = []
            for j in range(2):
                b = PS.get()
                for i in range(2):
                    MM(psb(b), wpool_sb[:, i, gq, j * 128:(j + 1) * 128], branch(2 * gq + i), i == 0, i == 1,
                       [b_wpool] + b_branch(2 * gq + i), [PS.bufs[b]])
                bm.append(b)
            for j in range(2):
                ACT(branch(2 * gq + j), psb(bm[j]), AF.Identity, [PS.bufs[bm[j]], b_vecs], b_branch(2 * gq + j),
                    scale=vcol(l, V_SPOOL, 2 * gq + j))
                PS.put(bm[j])
        if nxt_l is not None:
            load_wpool(nxt_l)
        fw.phase = 'bout_a'
        branch_out(l, 0, True)
        fw.phase = 'mla_proj'
        if l == 0:
            rope_tables(s, g)

        wq = wload(scols(("w_in", l), OFF_CQ, 448))
        wqv = RING.bf16(wq, 8 * 448).rearrange("p (k n) -> p k n", k=8)
        wk = wload((w_krs[l].rearrange("(kc p) n -> p kc n", p=128), []), eng=pool)
        wkv = RING.bf16(wk, 8 * 64).rearrange("p (k n) -> p k n", k=8)
        hb = lambda k: [b_h[k]]
        bq = []
        pms = PS.get()
        for i in range(2):
            b = PS.get()
            for k in range(8):
                MM(psb(b), wqv[:, k, i * 128:(i + 1) * 128], h_sb[:, k, :], k == 0, k == 7, RING.b(wq) + hb(k), [PS.bufs[b]])
            sq = AR.bf16(i, T)
            ACT(sq, psb(b), AF.Square, [PS.bufs[b]], AR.b(i))
            MM(psb(pms), ones_bf[:], sq, i == 0, i == 1, AR.b(i) + [b_cst2], [PS.bufs[pms]])
            bq.append(b)
        rq = AR.f32(2, T)
        ACT(rq, psb(pms), AF.Ln, [PS.bufs[pms], b_cst2], AR.b(2), scale=1.0 / 256, bias=eps_rms[:, 0:1])
        PS.put(pms)
        ACT(rq, rq, AF.Exp, AR.b(2), AR.b(2), scale=-0.5)
        for i in range(2):
            STT(cqn[:, i, :], psb(bq[i]), vcol(l, V_GQ, i), rq, ALU.mult, ALU.mult,
                [PS.bufs[bq[i]], b_vecs] + AR.b(2), [b_cqn])
            PS.put(bq[i])
        b = PS.get()
        pms = PS.get()
        for k in range(8):
            MM(psb(b), wqv[:, k, 256:384], h_sb[:, k, :], k == 0, k == 7, RING.b(wq) + hb(k), [PS.bufs[b]])
        sq = AR.bf16(3, T)
        ACT(sq, psb(b), AF.Square, [PS.bufs[b]], AR.b(3))
        MM(psb(pms), ones_bf[:], sq, True, True, AR.b(3) + [b_cst2], [PS.bufs[pms]])
        rk = AR.f32(4, T)
        ACT(rk, psb(pms), AF.Ln, [PS.bufs[pms], b_cst2], AR.b(4), scale=1.0 / 128, bias=eps_rms[:, 0:1])
        PS.put(pms)
        ACT(rk, rk, AF.Exp, AR.b(4), AR.b(4), scale=-0.5)
        STT(ckvnT[l][:, t0:t0 + T], psb(b), vcol(l, V_GKV, 0), rk, ALU.mult, ALU.mult,
            [PS.bufs[b], b_vecs] + AR.b(4), [b_kc[l][g]])
        PS.put(b)
        for tt in range(4):
            b = PS.get()
            tp = PS.t[:, b, 0:64].bitcast(BF16)
            fw.op(pe, lambda h, tp=tp, tt=tt: h.transpose(tp, ckvnT[l][:, t0 + tt * 128:t0 + (tt + 1) * 128], ident_bf[:]),
                  [b_kc[l][g], b_cst2], [PS.bufs[b]])
            CP(Vc[l][:, 4 * g + tt, :], tp, [PS.bufs[b]], [b_vc[l][g]])
            PS.put(b)
        b1 = PS.get()
        b2 = PS.get()
        for k in range(8):
            MM(psb(b1, parts=64), wqv[:, k, 384:448], h_sb[:, k, :], k == 0, k == 7, RING.b(wq) + hb(k), [PS.bufs[b1]])
        for k in range(8):
            MM(psb(b2, parts=64), wkv[:, k, :], h_sb[:, k, :], k == 0, k == 7, RING.b(wk) + hb(k), [PS.bufs[b2]])
        t1 = AR.f32(5, T, parts=64)
        t2 = AR.f32(6, T, parts=64)
        TT(t1, psb(b1, parts=64), cos2[:], ALU.mult, [PS.bufs[b1], b_cos], AR.b(5))
        TT(t2, psb(b2, parts=64), sinpm[:], ALU.mult, [PS.bufs[b2], b_sin], AR.b(6))
        PS.put(b1)
        PS.put(b2)
        TT(krotT[l][:, t0:t0 + T], t1, t2, ALU.add, AR.b(5) + AR.b(6), [b_kr[l][g]])

        ntile = 4 * g + 4
        fw.phase = 'mla_heads'
        qp_all = lambda hh: AR.bf16(hh // 2, T, (hh % 2) * T)
        qr_all = lambda hh: AR.bf16(4 + hh // 2, T, (hh % 2) * T, parts=64)
        for hh in range(8):
            par = hh % 2
            b = PS.get()
            for i in range(2):
                MM(psb(b), wuqn_sb[:, i, hh * 128:(hh + 1) * 128], cqn[:, i, :], i == 0, i == 1, [b_wuqn, b_cqn], [PS.bufs[b]])
            qn = AR.bf16(12 + par, T)
            ACT(qn, psb(b), AF.Copy, [PS.bufs[b]], AR.b(12 + par))
            PS.put(b)
            b = PS.get()
            MM(psb(b), wukT_sb[:, hh * 128:(hh + 1) * 128], qn, True, True, [b_wukT] + AR.b(12 + par), [PS.bufs[b]])
            ACT(qp_all(hh), psb(b), AF.Copy, [PS.bufs[b]], AR.b(hh // 2))
            PS.put(b)
            b1 = PS.get()
            b2 = PS.get()
            for i in range(2):
                MM(psb(b1, parts=64), wuqr_sb[:, i, hh * 64:(hh + 1) * 64], cqn[:, i, :], i == 0, i == 1, [b_wuqr, b_cqn], [PS.bufs[b1]])
            for i in range(2):
                MM(psb(b2, parts=64), wuqrs_sb[:, i, hh * 64:(hh + 1) * 64], cqn[:, i, :], i == 0, i == 1, [b_wuqrs, b_cqn], [PS.bufs[b2]])
            p1, p2 = 8 + 2 * par, 9 + 2 * par
            t1 = AR.f32(p1, T, parts=64)
            t2 = AR.f32(p2, T, parts=64)
            TT(t1, psb(b1, parts=64), cos2[:], ALU.mult, [PS.bufs[b1], b_cos], AR.b(p1))
            TT(t2, psb(b2, parts=64), sinpm[:], ALU.mult, [PS.bufs[b2], b_sin], AR.b(p2))
            PS.put(b1)
            PS.put(b2)
            TT(qr_all(hh), t1, t2, ALU.add, AR.b(p1) + AR.b(p2), AR.b(4 + hh // 2))

        units = []
        j = 0
        while j < ntile:
            npair = 2 if (j + 1 < 4 * g) else 1
            units.append((j, npair))
            j += npair
        uctr = [0]
        for hh in range(8):
            par = hh % 2
            qp = qp_all(hh)
            qr = qr_all(hh)
            bpo = PS.get_pair()
            po, pd = bpo, bpo + 1
            st = {}

            def S(ui):
                j0, npair = units[ui]
                bs = PS.get_pair() if npair == 2 else PS.get()
                for jj in range(npair):
                    jt = j0 + jj
                    qlo = max(0, jt - 4 * g) * 128
                    kg = jt // 4
                    MM(PS.t[:, bs + jj, qlo:T], ckvnT[l][:, jt * 128:(jt + 1) * 128], qp[:, qlo:T], True, False,
                       [b_kc[l][kg]] + AR.b(hh // 2), [PS.bufs[bs + jj]])
                    MM(PS.t[:, bs + jj, qlo:T], krotT[l][:, jt * 128:(jt + 1) * 128], qr[:, qlo:T], False, True,
                       [b_kr[l][kg]] + AR.b(4 + hh // 2), [PS.bufs[bs + jj]])
                st[ui] = bs

            def E(ui):
                j0, npair = units[ui]
                bs = st[ui]
                ppg = 8 + (uctr[0] % 3)
                uctr[0] += 1
                if npair == 2:
                    pT = AR.t[:, ppg * 544:ppg * 544 + 512].bitcast(BF16).rearrange("p (a n) -> p a n", a=2)
                    ACT(pT, PS.t[:, bs:bs + 2, :], AF.Exp, [PS.bufs[bs], PS.bufs[bs + 1]], AR.b(ppg), scale=ATT_SCALE)
                    pTs = [pT[:, 0, :], pT[:, 1, :]]
                else:
                    qlo = max(0, j0 - 4 * g) * 128
                    pT1 = AR.bf16(ppg, T)
                    ACT(pT1[:, qlo:T], PS.t[:, bs, qlo:T], AF.Exp, [PS.bufs[bs]], AR.b(ppg), scale=ATT_SCALE)
                    fw.op(dve, lambda h, pT1=pT1, qlo=qlo: h.memset(pT1[64:128, qlo:qlo + 64], 0.0), [], AR.b(ppg))
                    pTs = [pT1]
                for jj in range(npair):
                    PS.put(bs + jj)
                st[ui] = (pTs, ppg)

            def P(ui):
                j0, npair = units[ui]
                pTs, ppg = st[ui]
                for jj in range(npair):
                    jt = j0 + jj
                    ql = max(0, jt - 4 * g) * 128
                    kg = jt // 4
                    MM(PS.t[:, po, ql:T], Vc[l][:, jt, :], pTs[jj][:, ql:T], jt == 0, jt == ntile - 1,
                       [b_vc[l][kg]] + AR.b(ppg), [PS.bufs[po]])
                    MM(PS.t[:, pd, ql:T], ones_bf[:], pTs[jj][:, ql:T], jt == 0, jt == ntile - 1,
                       [b_cst2] + AR.b(ppg), [PS.bufs[pd]])

            nu = len(units)
            S(0)
            for ui in range(nu):
                if ui + 1 < nu:
                    S(ui + 1)
                E(ui)
                P(ui)
            rden = AR.f32(11, T)
            ACT(rden, psb(pd), AF.Ln, [PS.bufs[pd]], AR.b(11))
            ACT(rden, rden, AF.Exp, AR.b(11), AR.b(11), scale=-1.0)
            PS.put(pd)
            op_ = AR.bf16(12 + par, T)
            TT(op_, psb(po), rden, ALU.mult, [PS.bufs[po]] + AR.b(11), AR.b(12 + par))
            PS.put(po)
            b = PS.get()
            MM(psb(b), wuv_sb[:, hh * 128:(hh + 1) * 128], op_, True, True, [b_wuv] + AR.b(12 + par), [PS.bufs[b]])
            ACT(branch(hh), psb(b), AF.Copy, [PS.bufs[b]], b_branch(hh))
            PS.put(b)
        if nxt_l is not None:
            load_attw(nxt_l)
        fw.phase = 'bout_b'
        branch_out(l, 1, False)
        fw.phase = 'sgu'

        wv_s = [wload(scols(("w_in", l), OFF_SG + D + nn * 512, 512)) for nn in range(2)]
        vn_all = lambda tb: AR.bf16(4 + tb, 1024)
        for tb in range(4):
            bp = PS.get_pair()
            for nn in range(2):
                wv = ring_view(wv_s[nn], 8, 512)
                for k in range(8):
                    MM(PS.t[:, bp + nn, :], h_sb[:, k, tb * 128:(tb + 1) * 128], wv[:, k, :], k == 0, k == 7,
                       RING.b(wv_s[nn]) + [b_h[k]], [PS.bufs[bp + nn]])
            gpg = 0 + 2 * (tb % 2)
            gv = AR.t[:, gpg * 544:gpg * 544 + 1024]
            ACT(gv.rearrange("p (a n) -> p a n", a=2), PS.t[:, bp:bp + 2, :], AF.Gelu,
                [PS.bufs[bp], PS.bufs[bp + 1]], AR.b(gpg, 2))
            PS.put(bp)
            PS.put(bp + 1)
            stt_ = AR.f32(12, 16)
            for i in range(2):
                fw.op(dve, lambda h, i=i, gv=gv, stt_=stt_: h.bn_stats(out=stt_[:, i * 6:(i + 1) * 6], in_=gv[:, i * 512:(i + 1) * 512]),
                      AR.b(gpg, 2), AR.b(12))
            mvv = AR.f32(12, 2, off=16)
            fw.op(dve, lambda h, mvv=mvv, stt_=stt_: h.bn_aggr(out=mvv, in_=stt_[:, 0:12].rearrange("p (a n) -> p a n", a=2)),
                  AR.b(12), AR.b(12))
            rs = AR.f32(12, 1, off=20)
            ACT(rs, mvv[:, 1:2], AF.Sqrt, AR.b(12) + [b_cst2], AR.b(12), bias=eps_ln1[:, 0:1])
            fw.op(dve, lambda h, rs=rs: h.reciprocal(out=rs, in_=rs), AR.b(12), AR.b(12))
            TS(gv, gv, mvv[:, 0:1], rs, ALU.subtract, ALU.mult, AR.b(gpg, 2) + AR.b(12), AR.b(gpg, 2))
            TT(gv, gv, gsg_sb[:], ALU.mult, AR.b(gpg, 2) + [b_gsg], AR.b(gpg, 2))
            TT(vn_all(tb), gv, bsg_sb[:], ALU.add, AR.b(gpg, 2) + [b_bsg], AR.b(4 + tb))
        for half in range(2):
            wu = wload(scols(("w_in", l), OFF_SG + half * 512, 512))
            for gg in range(4):
                gq = half * 4 + gg
                bu = proj8(wu, gg * 128, lambda k: h_sb[:, k, :], lambda k: [b_h[k]])
                ACT(branch(gq), psb(bu), AF.Gelu, [PS.bufs[bu]], b_branch(gq))
                PS.put(bu)
        for gq in range(8):
            bz = PS.get()
            for tb in range(4):
                MM(PS.t[:, bz, tb * 128:(tb + 1) * 128], vn_all(tb)[:, gq * 128:(gq + 1) * 128],
                   wsT_sb[:, gq * 128:(gq + 1) * 128], True, False, AR.b(4 + tb) + [b_wsT], [PS.bufs[bz]])
                MM(PS.t[:, bz, tb * 128:(tb + 1) * 128], ones_bf[0:1, :], bsT_sb[0:1, gq * 128:(gq + 1) * 128],
                   False, True, [b_cst2, b_bsT], [PS.bufs[bz]])
            TT(branch(gq), branch(gq), psb(bz), ALU.mult, b_branch(gq) + [PS.bufs[bz]], b_branch(gq))
            PS.put(bz)
        if nxt_l is not None:
            load_sguw(nxt_l)
        fw.phase = 'bout_c'
        branch_out(l, 2, False)
        fw.phase = 'wo'

        for d in range(8):
            CP(branch(d), merged(d), b_merged(d), b_branch(d))
        for half in range(2):
            wo = wload(scols(("w_o", l), half * 512, 512))
            for dd in range(4):
                d = half * 4 + dd
                by = proj8(wo, dd * 128, lambda k: branch(k), lambda k: b_branch(k))
                STT(XC[0][:, d, :], psb(by), mcol(l, 2, d, s), XC[0][:, d, :], ALU.mult, ALU.add,
                    [PS.bufs[by], b_mod, XC[1][d]], [XC[1][d]])
                PS.put(by)
        fw.phase = 'post_ln_t'
        post_ln(l, V_LNTG, V_LNTB)

    def ffn(l, s, g):
        fw.phase = 'ln_mod_f'
        ln_mod(l, s, 1)
        fw.phase = 'ffn_up'
        nblk = (NF + 3) // 4

        def ffn_stage2(f):
            av = AR.f32(2 * (f % 3), T)
            ag = AR.f32(2 * (f % 3) + 1, T)
            pav, pag = 2 * (f % 3), 2 * (f % 3) + 1
            ACT(ag, ag, AF.Silu, AR.b(pag), AR.b(pag))
            TT(actT(f), ag, av, ALU.mult, AR.b(pag) + AR.b(pav), b_actT(f), eng=pool)

        for fb in range(nblk):
            nfc = min(4, NF - fb * 4)
            wvs = wload(scols(("w_up", l), fb * 512, nfc * 128))
            wgs = wload(scols(("w_up", l), D_FF + fb * 512, nfc * 128))
            pre = {}
            if fb == 0:
                bks = proj8_kouter([(wsl_, ff_ * 128, nfc * 128) for ff_ in range(2) for wsl_ in (wvs, wgs)],
                                   lambda k: h_sb[:, k, :], lambda k: [b_h[k]])
                pre = {(0, 0): bks[0], (0, 1): bks[1], (1, 0): bks[2], (1, 1): bks[3]}
            for ff in range(nfc):
                f = fb * 4 + ff
                items = []
                for which, wsl in ((0, wvs), (1, wgs)):
                    ch = which * NF + f
                    wv = ring_view(wsl, 8, nfc * 128)
                    if (ff, which) in pre:
                        b = pre[(ff, which)]
                    else:
                        b = PS.get()
                        for k in range(8):
                            MM(psb(b), wv[:, k, ff * 128:(ff + 1) * 128], h_sb[:, k, :], k == 0, k == 7,
                               RING.b(wsl) + [b_h[k]], [PS.bufs[b]])
                    pa = 2 * (f % 3) + which
                    items.append((b, pa, AR.f32(pa, T), zhalo[l][:, ch, :], [b_zhalo[l][ch]],
                                  vcol(l, V_CW, ch), vcol(l, V_CW + 2 * NF, ch), vcol(l, V_CW + 4 * NF, ch), vcol(l, V_CB, ch)))
                for (b, pa, acc, zh, bzh, w0, w1, w2, cb) in items:
                    ACT(acc, psb(b), AF.Identity, [PS.bufs[b], b_vecs], AR.b(pa), scale=w2, bias=cb)
                for (b, pa, acc, zh, bzh, w0, w1, w2, cb) in items:
                    STT(acc[:, 1:T], psb(b, 0, T - 1), w1, acc[:, 1:T], ALU.mult, ALU.add,
                        [PS.bufs[b], b_vecs] + AR.b(pa), AR.b(pa))
                for (b, pa, acc, zh, bzh, w0, w1, w2, cb) in items:
                    STT(acc[:, 2:T], psb(b, 0, T - 2), w0, acc[:, 2:T], ALU.mult, ALU.add,
                        [PS.bufs[b], b_vecs] + AR.b(pa), AR.b(pa))
                for (b, pa, acc, zh, bzh, w0, w1, w2, cb) in items:
                    STT(acc[:, 0:1], zh[:, 1:2], w1, acc[:, 0:1], ALU.mult, ALU.add, bzh + [b_vecs] + AR.b(pa), AR.b(pa))
                for (b, pa, acc, zh, bzh, w0, w1, w2, cb) in items:
                    STT(acc[:, 0:2], zh, w0, acc[:, 0:2], ALU.mult, ALU.add, bzh + [b_vecs] + AR.b(pa), AR.b(pa))
                for (b, pa, acc, zh, bzh, w0, w1, w2, cb) in items:
                    ACT(zh, psb(b, T - 2, T), AF.Copy, [PS.bufs[b]], bzh)
                    PS.put(b)
                if f > 0:
                    ffn_stage2(f - 1)
        ffn_stage2(NF - 1)
        fw.phase = 'ffn_down'
        for d in range(8):
            wd = wload(scols(("w_down", l), d * 128, 128))
            wv = RING.bf16(wd, NF * 128).rearrange("p (k n) -> p k n", k=NF)
            b = PS.get()
            for f in range(NF):
                MM(psb(b), wv[:, f, :], actT(f), f == 0, f == NF - 1, RING.b(wd) + b_actT(f), [PS.bufs[b]])
            STT(XC[0][:, d, :], psb(b), mcol(l, 5, d, s), XC[0][:, d, :], ALU.mult, ALU.add,
                [PS.bufs[b], b_mod, XC[1][d]], [XC[1][d]])
            PS.put(b)
        fw.phase = 'post_ln_f'
        post_ln(l, V_LNFG, V_LNFB)

    steps = [(s, g, l) for s in range(nseq) for g in range(ngroups) for l in range(nlayers)]
    groups = [(s, g) for s in range(nseq) for g in range(ngroups)]
    load_wpool(0)
    load_attw(0)
    load_sguw(0)

    def load_x(gi):
        s_, g_ = groups[gi]
        fw.dma(pool, xs_t[gi % 2][:], xT[s_].rearrange("(kc p) n -> p kc n", p=128)[:, :, g_ * T:(g_ + 1) * T],
               writes=xs_b[gi % 2])

    load_x(0)
    for idx, (s, g, l) in enumerate(steps):
        nxt_l = steps[idx + 1][2] if idx + 1 < len(steps) else None
        gi = s * ngroups + g
        XC[0], XC[1] = xs_t[gi % 2], xs_b[gi % 2]
        t0 = g * T
        if l == 0 and g == 0:
            for ll in range(nlayers):
                fw.op(dve, lambda h, ll=ll: h.memset(phalo[ll][:], 0.0), [], [b_phalo[ll]])
                fw.op(dve, lambda h, ll=ll: h.memset(zhalo[ll][:], 0.0), [], b_zhalo[ll])
        mixer(l, s, g, nxt_l)
        if dbg == (s, g, l, "mix"):
            fw.dma(pool, dbg_out, XC[0][:], reads=XC[1])
        if l == nlayers - 1 and gi + 1 < len(groups):
            load_x(gi + 1)
        ffn(l, s, g)
        if dbg == (s, g, l, "ffn"):
            fw.dma(pool, dbg_out, XC[0][:], reads=XC[1])
        if l == nlayers - 1:
            fw.dma(pool, outT[s].rearrange("(kc p) n -> p kc n", p=128)[:, :, t0:t0 + T], XC[0][:], reads=XC[1])
    fw.final_fence(sp)
    fw.emit()
    return nc, fw


def _host_prep(inp):
    f = lambda a: np.ascontiguousarray(np.asarray(a, dtype=np.float32))
    L = DEPTH
    w_in = np.asarray(inp["w_in"], np.float32)
    w_uq = np.asarray(inp["w_uq"], np.float32)
    w_ukv = np.asarray(inp["w_ukv"], np.float32)
    perm = np.concatenate([np.arange(32, 64), np.arange(0, 32)])
    shared = {
        "w_ada": f(inp["w_ada"]),
        "w_in": f(w_in),
        "w_krs": f(w_in[:, :, OFF_KR:OFF_KR + 64][:, :, perm]),
        "w_pool": f(inp["w_pool"]),
        "w_uqn": f(w_uq[:, :, :, 0:128].reshape(L, 256, 1024)),
        "w_uqr": f(w_uq[:, :, :, 128:192].reshape(L, 256, 512)),
        "w_uqrs": f(w_uq[:, :, :, 128:192][:, :, :, perm].reshape(L, 256, 512)),
        "w_ukT": f(np.transpose(w_ukv[:, :, :, 0:128], (0, 3, 2, 1)).reshape(L, 128, 1024)),
        "w_uv": f(w_ukv[:, :, :, 128:256].reshape(L, 128, 1024)),
        "wsT": f(np.transpose(np.asarray(inp["w_s"], np.float32), (0, 3, 1, 2)).reshape(L, 128, 1024)),
        "bsT": f(np.transpose(np.asarray(inp["b_s"], np.float32), (0, 2, 1)).reshape(L, 1, 1024)),
        "gsg": f(np.asarray(inp["g_sg"], np.float32).reshape(L, 1, 1024)),
        "bsg": f(np.asarray(inp["b_sg"], np.float32).reshape(L, 1, 1024)),
        "w_branch": f(inp["w_branch"]),
        "w_o": f(inp["w_o"]),
        "w_up": f(inp["w_up"]),
        "w_down": f(inp["w_down"]),
    }
    pp = lambda v, n: np.transpose(np.asarray(v, np.float32).reshape(L, n, 128), (0, 2, 1))
    vecs = np.zeros((L, 128, NV), np.float32)
    vecs[:, :, V_SPOOL:V_SPOOL + 8] = pp(inp["s_pool"], 8)
    vecs[:, :, V_GQ:V_GQ + 2] = pp(inp["g_q"], 2)
    vecs[:, :, V_GKV:V_GKV + 1] = pp(inp["g_kv"], 1)
    vecs[:, :, V_LNTG:V_LNTG + 8] = pp(inp["ln_t_g"], 8)
    vecs[:, :, V_LNTB:V_LNTB + 8] = pp(inp["ln_t_b"], 8)
    vecs[:, :, V_LNFG:V_LNFG + 8] = pp(inp["ln_f_g"], 8)
    vecs[:, :, V_LNFB:V_LNFB + 8] = pp(inp["ln_f_b"], 8)
    cw = np.asarray(inp["conv_w"], np.float32)
    for k in range(3):
        vecs[:, :, V_CW + k * 44:V_CW + (k + 1) * 44] = pp(cw[:, k], 44)
    vecs[:, :, V_CB:V_CB + 44] = pp(inp["conv_b"], 44)
    vecs[:, :, V_BADA:V_BADA + 48] = pp(inp["b_ada"], 48)
    shared["vecs"] = vecs
    consts = np.zeros((128, NCONST), np.float32)
    consts[:, C_IDENT:C_IDENT + 128] = np.eye(128, dtype=np.float32)
    consts[:, C_MASK:C_MASK + 128] = np.triu(np.ones((128, 128), np.float32))
    freqs = (10000.0 ** (-np.arange(0, 64, 2, dtype=np.float32) / 64)).astype(np.float32)
    consts[0:64, C_FREQ] = np.concatenate([freqs, freqs])
    consts[0:32, C_SGN] = -1.0
    consts[32:64, C_SGN] = 1.0
    for wi, w in enumerate((2, 4, 8, 16)):
        tt = np.arange(16)
        consts[:, C_CORR + wi * 16:C_CORR + (wi + 1) * 16] = (w / np.minimum(tt + 1, w)).astype(np.float32)[None, :]
    shared["consts"] = consts
    return shared


_CACHE = {}


def kernel(**inputs):
    x = np.asarray(inputs["x"], np.float32)
    c = np.asarray(inputs["c"], np.float32)
    pos = np.asarray(inputs["pos"], np.int32)
    shared = _host_prep(inputs)
    key = "full"
    if key not in _CACHE:
        _CACHE[key] = build_program()
    nc, _ = _CACHE[key]
    in_maps = []
    for core in range(NCORES):
        b0 = core * SPC
        m = dict(shared)
        m["xT"] = np.ascontiguousarray(np.transpose(x[b0:b0 + SPC], (0, 2, 1)))
        m["cT"] = np.ascontiguousarray(np.transpose(c[b0:b0 + SPC].reshape(SPC, 8, 128), (2, 1, 0)))
        m["pos"] = np.ascontiguousarray(pos[b0:b0 + SPC])
        in_maps.append(m)
    res = run_bass_kernel_spmd(nc, in_maps, core_ids=list(range(NCORES)))
    out = np.empty((BATCH, SEQ, D), np.float32)
    for core in range(NCORES):
        o = res.results[core]["outT"]
        out[core * SPC:(core + 1) * SPC] = np.transpose(o, (0, 2, 1))
    return out
```

```python
import numpy as np
import concourse.bass as bass
import concourse.mybir as mybir
from concourse.bass_utils import run_bass_kernel_spmd

F32 = mybir.dt.float32
BF16 = mybir.dt.bfloat16
I32 = mybir.dt.int32
AF = mybir.ActivationFunctionType
ALU = mybir.AluOpType

D = 1024
SEQ = 2048
BATCH = 32
DEPTH = 2
NCORES = 8
SPC = BATCH // NCORES
T = 512
NG = SEQ // T
D_FF = 2816
NF = D_FF // 128
OFF_POOL = 3 * D
OFF_CQ = OFF_POOL + D
OFF_CKV = OFF_CQ + 256
OFF_KR = OFF_CKV + 128
OFF_SG = OFF_KR + 64
IN_WIDTH = OFF_SG + 2 * D
ALPHA = (2 * DEPTH) ** 0.25
LN_EPS = 1e-5
RMS_EPS = 1e-6
ATT_SCALE = 192 ** -0.5
POOL_W = (2, 2, 4, 4, 8, 8, 16, 16)

V_SPOOL = 0
V_GQ = 8
V_GKV = 10
V_LNTG = 11
V_LNTB = 19
V_LNFG = 27
V_LNFB = 35
V_CW = 43
V_CB = V_CW + 132
V_BADA = V_CB + 44
NV = V_BADA + 48
C_IDENT = 0
C_MASK = 128
C_FREQ = 256
C_SGN = 257
C_CORR = 258
NCONST = C_CORR + 64

SEM_EPOCH = 8000


class Buf:
    __slots__ = ("w", "r", "name")

    def __init__(self, name=""):
        self.w = {}
        self.r = {}
        self.name = name


class Eng:
    def __init__(self, fw, name, is_pe=False):
        self.fw = fw
        self.name = name
        self.is_pe = is_pe
        self.prog = []
        self.waited = {}
        self.sems = []
        self.count = 0
        self._new_sem()

    def _new_sem(self):
        s = self.fw.new_sem(f"{self.name}_p{len(self.sems)}")
        self.sems.append(s)
        self.sem = s
        self.count = 0


class DmaSem:
    def __init__(self, fw, name):
        self.sem = fw.new_sem(name)
        self.value = 0


class FW:
    def __init__(self, nc, same_engine_sync=True):
        self.nc = nc
        self.same_sync = same_engine_sync
        self.pe = Eng(self, "pe", is_pe=True)
        self.act = Eng(self, "act")
        self.dve = Eng(self, "dve")
        self.pool = Eng(self, "pool")
        self.sp = Eng(self, "sp")
        self.dma_pool = {}
        self.retired = []
        self.n_inst = 0
        self.phase = 'setup'
        self.phases = {}

    def new_sem(self, name):
        return self.nc.alloc_semaphore(name=name)

    def _collect(self, eng, reads, writes):
        need = {}

        def add(d, raw):
            for s, v in d.items():
                if s in eng.sems:
                    if not (raw and self.same_sync and not eng.is_pe):
                        continue
                if need.get(s, 0) < v:
                    need[s] = v

        for b in reads:
            add(b.w, True)
        for b in writes:
            add(b.w, True)
            add(b.r, False)
        waits = []
        for s, v in need.items():
            if eng.waited.get(s, 0) < v:
                eng.waited[s] = v
                waits.append((s, v))
        return waits

    @staticmethod
    def _update(stamp, reads, writes):
        s, v = stamp
        for b in writes:
            b.w = {s: v}
            b.r = {}
        for b in reads:
            if b in writes:
                continue
            b.r[s] = v

    def op(self, eng, fn, reads=(), writes=(), signal=True):
        reads = list(reads)
        writes = list(writes)
        waits = self._collect(eng, reads, writes)
        sem = eng.sem
        stamp = (sem, eng.count + 1)
        self.n_inst += 1

        def emit(h, waits=waits, fn=fn, signal=signal, sem=sem):
            for s, v in waits:
                h.wait_ge(s, v)
            ins = fn(h)
            if signal:
                ins.then_inc(sem, 1)

        eng.prog.append(emit)
        self.phases.setdefault(eng.name, []).append(self.phase)
        self._update(stamp, reads, writes)
        if signal:
            eng.count += 1
            if eng.count >= SEM_EPOCH:
                eng._new_sem()
        return stamp

    def dma(self, eng, out_ap, in_ap, reads=(), writes=(), dsem=None, **kw):
        reads = list(reads)
        writes = list(writes)
        if dsem is None:
            dsem = self.get_dma_sem(eng)
        waits = self._collect(eng, reads, writes)
        if dsem.value > 0 and eng.waited.get(dsem.sem, 0) < dsem.value:
            eng.waited[dsem.sem] = dsem.value
            waits.append((dsem.sem, dsem.value))
        dsem.value += 16
        stamp = (dsem.sem, dsem.value)
        self.n_inst += 1

        def emit(h, waits=waits, out_ap=out_ap, in_ap=in_ap, kw=kw, sem=dsem.sem):
            for s, v in waits:
                h.wait_ge(s, v)
            h.dma_start(out=out_ap, in_=in_ap, **kw).then_inc(sem, 16)

        eng.prog.append(emit)
        self._update(stamp, reads, writes)
        return stamp

    def get_dma_sem(self, eng, n=8):
        key = eng.name
        if key not in self.dma_pool:
            self.dma_pool[key] = [[DmaSem(self, f"dma_{key}_{i}") for i in range(n)], 0]
        lst = self.dma_pool[key]
        i = lst[1] % n
        d = lst[0][i]
        if d.value >= 2048:
            self.retired.append(d)
            d = DmaSem(self, f"dma_{key}_{i}_{len(self.retired)}")
            lst[0][i] = d
        lst[1] += 1
        return d

    def final_fence(self, eng):
        sems = [d for lst, _ in self.dma_pool.values() for d in lst if d.value] + list(self.retired)

        def emit(h, sems=sems):
            for d in sems:
                h.wait_ge(d.sem, d.value)

        eng.prog.append(emit)

    def emit(self):
        with self.nc.Block() as block:
            @block.tensor
            def _(h):
                for f in self.pe.prog:
                    f(h)

            @block.scalar
            def _(h):
                for f in self.act.prog:
                    f(h)

            @block.vector
            def _(h):
                for f in self.dve.prog:
                    f(h)

            @block.gpsimd
            def _(h):
                for f in self.pool.prog:
                    f(h)

            @block.sync
            def _(h):
                for f in self.sp.prog:
                    f(h)


class Region:
    def __init__(self, nc, name, npages, page_words):
        self.t = nc.alloc_sbuf_tensor(name, [128, npages * page_words], F32)
        self.pw = page_words
        self.np = npages
        self.bufs = [Buf(f"{name}{i}") for i in range(npages)]

    def f32(self, page, n, off=0, parts=128):
        a = page * self.pw + off
        return self.t[0:parts, a:a + n]

    def bf16(self, page, n, off=0, parts=128):
        a = page * self.pw
        words = (off + n + 1) // 2
        v = self.t[0:parts, a:a + words].bitcast(BF16)
        return v[:, off:off + n]

    def b(self, page, npg=1):
        return self.bufs[page:page + npg]


class PsumPool:
    def __init__(self, nc):
        self.t = nc.alloc_psum_tensor("ps", [128, 8, 512], F32)
        self.bufs = [Buf(f"ps{i}") for i in range(8)]
        self.free = {i: i for i in range(8)}
        self.clock = 8

    def get(self):
        b = min(self.free, key=lambda k: self.free[k])
        del self.free[b]
        return b

    def get_pair(self):
        cands = [i for i in (0, 2, 4, 6) if i in self.free and i + 1 in self.free]
        assert cands, "no free psum pair"
        b = min(cands, key=lambda k: max(self.free[k], self.free[k + 1]))
        del self.free[b]
        del self.free[b + 1]
        return b

    def put(self, b):
        self.clock += 1
        self.free[b] = self.clock


def build_program(nseq=SPC, ngroups=NG, nlayers=DEPTH, dbg=None):
    nc = bass.Bass("TRN2", target_bir_lowering=False)
    dt_in = lambda name, shape, dt=F32: nc.dram_tensor(name, list(shape), dt, kind="ExternalInput").ap()
    xT = dt_in("xT", [nseq, D, SEQ])
    cT = dt_in("cT", [128, 8, nseq])
    pos = dt_in("pos", [nseq, SEQ], I32)
    consts = dt_in("consts", [128, NCONST])
    vecs = dt_in("vecs", [DEPTH, 128, NV])
    w_ada = dt_in("w_ada", [DEPTH, D, 6 * D])
    w_in = dt_in("w_in", [DEPTH, D, IN_WIDTH])
    w_krs = dt_in("w_krs", [DEPTH, D, 64])
    w_pool = dt_in("w_pool", [DEPTH, 4, 256, 256])
    w_uqn = dt_in("w_uqn", [DEPTH, 256, 1024])
    w_uqr = dt_in("w_uqr", [DEPTH, 256, 512])
    w_uqrs = dt_in("w_uqrs", [DEPTH, 256, 512])
    w_ukT = dt_in("w_ukT", [DEPTH, 128, 1024])
    w_uv = dt_in("w_uv", [DEPTH, 128, 1024])
    wsT = dt_in("wsT", [DEPTH, 128, 1024])
    bsT = dt_in("bsT", [DEPTH, 1, 1024])
    gsg = dt_in("gsg", [DEPTH, 1, 1024])
    bsg = dt_in("bsg", [DEPTH, 1, 1024])
    w_branch = dt_in("w_branch", [DEPTH, 3, D, D])
    w_o = dt_in("w_o", [DEPTH, D, D])
    w_up = dt_in("w_up", [DEPTH, D, 2 * D_FF])
    w_down = dt_in("w_down", [DEPTH, D_FF, D])
    outT = nc.dram_tensor("outT", [nseq, D, SEQ], F32, kind="ExternalOutput").ap()
    dbg_out = None
    if dbg is not None:
        dbg_out = nc.dram_tensor("dbg", [128, 8, T], F32, kind="ExternalOutput").ap()

    fw = FW(nc)
    pe, act, dve, pool, sp = fw.pe, fw.act, fw.dve, fw.pool, fw.sp
    A = nc.alloc_sbuf_tensor

    xs_t = [A(f"x_sb{i}", [128, 8, T], F32) for i in range(2)]
    xs_b = [[Buf(f"x{i}_{c}") for c in range(8)] for i in range(2)]
    XC = [xs_t[0], xs_b[0]]
    h_sb = A("h_sb", [128, 8, T], BF16)
    b_h = [Buf(f"h{c}") for c in range(8)]
    MB = Region(nc, "mb", 12, 512)
    merged = lambda d: MB.f32(d, 512)
    b_merged = lambda d: MB.b(d)
    branch = lambda c, lo=0, hi=T: MB.bf16(8 + c // 2, hi - lo, (c % 2) * 512 + lo)
    b_branch = lambda c: MB.b(8 + c // 2)
    actT = lambda f: MB.bf16(f // 2, 512, (f % 2) * 512)
    b_actT = lambda f: MB.b(f // 2)
    AR = Region(nc, "ar", 14, 544)
    st_mean = A("st_mean", [128, T], F32); b_mean = Buf("mean")
    st_var = A("st_var", [128, T], F32); b_var = Buf("var")
    st_rstd = A("st_rstd", [128, T], F32); b_rstd = Buf("rstd")
    NSLOT = 5
    RING = Region(nc, "ring", NSLOT, 2048)
    ring_i = [0]
    wpool_sb = A("wpool_sb", [128, 2, 4, 256], BF16); b_wpool = Buf("wpool")
    wuqn_sb = A("wuqn_sb", [128, 2, 1024], BF16); b_wuqn = Buf("wuqn")
    wuqr_sb = A("wuqr_sb", [128, 2, 512], BF16); b_wuqr = Buf("wuqr")
    wuqrs_sb = A("wuqrs_sb", [128, 2, 512], BF16); b_wuqrs = Buf("wuqrs")
    wukT_sb = A("wukT_sb", [128, 1024], BF16); b_wukT = Buf("wukT")
    wuv_sb = A("wuv_sb", [128, 1024], BF16); b_wuv = Buf("wuv")
    wsT_sb = A("wsT_sb", [128, 1024], BF16); b_wsT = Buf("wsT")
    bsT_sb = A("bsT_sb", [1, 1024], BF16); b_bsT = Buf("bsT")
    gsg_sb = A("gsg_sb", [128, 1024], F32); b_gsg = Buf("gsg")
    bsg_sb = A("bsg_sb", [128, 1024], F32); b_bsg = Buf("bsg")
    vecs_sb = A("vecs_sb", [128, DEPTH, NV], F32); b_vecs = Buf("vecs")
    mod_sb = A("mod_sb", [128, DEPTH * 48 * nseq], F32); b_mod = Buf("mod")
    c_sb = A("c_sb", [128, 8, nseq], F32); b_c = Buf("c")
    const_sb = A("const_sb", [128, NCONST], F32); b_const = Buf("const")
    ident_bf = A("ident_bf", [128, 128], BF16)
    mask_bf = A("mask_bf", [128, 128], BF16)
    ones_bf = A("ones_bf", [128, 128], BF16)
    onesm_bf = A("onesm_bf", [128, 128], BF16)
    ones_f = A("ones_f", [128, 128], F32)
    eps_ln = A("eps_ln", [128, 1], F32)
    eps_rms = A("eps_rms", [128, 1], F32)
    b_cst2 = Buf("cst2")
    ckvnT = [A(f"ckvnT{l}", [128, SEQ], BF16) for l in range(DEPTH)]
    krotT = [A(f"krotT{l}", [64, SEQ], BF16) for l in range(DEPTH)]
    Vc = [A(f"Vc{l}", [128, SEQ // 128, 128], BF16) for l in range(DEPTH)]
    b_kc = [[Buf(f"kc{l}_{j}") for j in range(NG)] for l in range(DEPTH)]
    b_kr = [[Buf(f"kr{l}_{j}") for j in range(NG)] for l in range(DEPTH)]
    b_vc = [[Buf(f"vc{l}_{j}") for j in range(NG)] for l in range(DEPTH)]
    cos2 = A("cos2", [64, T], F32); b_cos = Buf("cos")
    sinpm = A("sinpm", [64, T], F32); b_sin = Buf("sin")
    cqn = A("cqn", [128, 2, T], BF16); b_cqn = Buf("cqn")
    phalo = [A(f"phalo{l}", [128, 8, 16], F32) for l in range(DEPTH)]
    b_phalo = [Buf(f"phalo{l}") for l in range(DEPTH)]
    zhalo = [A(f"zhalo{l}", [128, 2 * NF, 2], F32) for l in range(DEPTH)]
    b_zhalo = [[Buf(f"zhalo{l}_{c}") for c in range(2 * NF)] for l in range(DEPTH)]
    PS = PsumPool(nc)

    def ACT(out, in_, func, reads, writes, **kw):
        fw.op(act, lambda h: h.activation(out=out, in_=in_, func=func, **kw), reads, writes)

    def TT(out, in0, in1, op, reads, writes, eng=dve):
        fw.op(eng, lambda h: h.tensor_tensor(out=out, in0=in0, in1=in1, op=op), reads, writes)

    def STT(out, in0, scalar, in1, op0, op1, reads, writes, eng=dve):
        fw.op(eng, lambda h: h.scalar_tensor_tensor(out=out, in0=in0, scalar=scalar, in1=in1, op0=op0, op1=op1),
              reads, writes)

    def TS(out, in0, s1, s2, op0, op1, reads, writes, eng=dve):
        if s2 is None:
            fw.op(eng, lambda h: h.tensor_scalar(out=out, in0=in0, scalar1=s1, scalar2=None, op0=op0), reads, writes)
        else:
            fw.op(eng, lambda h: h.tensor_scalar(out=out, in0=in0, scalar1=s1, scalar2=s2, op0=op0, op1=op1),
                  reads, writes)

    def CP(out, in_, reads, writes, eng=dve):
        fw.op(eng, lambda h: h.tensor_copy(out=out, in_=in_), reads, writes)

    def MM(out, lhsT, rhs, start, stop, reads, writes, signal=None):
        signal = True if signal is None else signal
        fw.op(pe, lambda h: h.matmul(out, lhsT=lhsT, rhs=rhs, start=start, stop=stop), reads, writes, signal=signal)

    def psb(b, lo=0, hi=T, parts=128):
        return PS.t[0:parts, b, lo:hi]

    SCR = {}

    def mk_scratch(key, src2d):
        rows, cols = src2d.shape
        t = nc.dram_tensor("scr_" + "_".join(str(k) for k in key), [rows, cols], BF16, kind="Internal").ap()
        bufs = []
        for r0 in range(0, rows, 128):
            b = Buf("scr")
            fw.dma(pool, t[r0:r0 + 128, :], src2d[r0:r0 + 128, :], writes=[b])
            bufs.append(b)
        SCR[key] = (t, bufs)

    def scols(key, c0, n):
        ap, bufs = SCR[key]
        return ap.rearrange("(kc p) n -> p kc n", p=128)[:, :, c0:c0 + n], bufs

    def wload(src, eng=None):
        src_ap, sbufs = src
        slot = ring_i[0] % NSLOT
        ring_i[0] += 1
        nel = 1
        for s_ in src_ap.shape[1:]:
            nel *= s_
        dst = RING.bf16(slot, nel).rearrange("p (k n) -> p k n", k=src_ap.shape[1])
        fw.dma(eng or sp, dst, src_ap, reads=sbufs, writes=RING.b(slot))
        return slot

    def ring_view(slot, k, n, tot=None):
        tot = tot or n
        v = RING.bf16(slot, k * tot).rearrange("p (k n) -> p k n", k=k)
        return v

    def mcol(l, kind, c, s):
        i = ((l * 6 + kind) * 8 + c) * nseq + s
        return mod_sb[:, i:i + 1]

    def vcol(l, off, c=0):
        return vecs_sb[:, l, off + c:off + c + 1]

    fw.dma(sp, const_sb[:], consts, writes=[b_const])
    fw.dma(sp, vecs_sb[:], vecs.rearrange("l p n -> p l n"), writes=[b_vecs])
    fw.dma(sp, c_sb[:], cT, writes=[b_c])
    CP(ident_bf[:], const_sb[:, C_IDENT:C_IDENT + 128], [b_const], [b_cst2])
    CP(mask_bf[:], const_sb[:, C_MASK:C_MASK + 128], [b_const], [b_cst2])
    fw.op(dve, lambda h: h.memset(ones_bf[:], 1.0), writes=[b_cst2])
    fw.op(dve, lambda h: h.memset(onesm_bf[:], 1.0 / D), writes=[b_cst2])
    fw.op(dve, lambda h: h.memset(ones_f[:], 1.0), writes=[b_cst2])
    fw.op(dve, lambda h: h.memset(eps_ln[:], LN_EPS / (ALPHA * ALPHA)), writes=[b_cst2])
    fw.op(dve, lambda h: h.memset(eps_rms[:], RMS_EPS), writes=[b_cst2])
    eps_ln1 = A("eps_ln1", [128, 1], F32)
    fw.op(dve, lambda h: h.memset(eps_ln1[:], LN_EPS), writes=[b_cst2])

    for l_ in range(nlayers):
        mk_scratch(("w_in", l_), w_in[l_])
        for bi_ in range(3):
            mk_scratch(("w_br", l_, bi_), w_branch[l_, bi_])
        mk_scratch(("w_o", l_), w_o[l_])
        mk_scratch(("w_up", l_), w_up[l_])
        mk_scratch(("w_down", l_), w_down[l_])

    cact = A("cact", [128, 8, nseq], F32); b_cact = Buf("cact")
    ACT(cact[:], c_sb[:], AF.Silu, [b_c], [b_cact])
    pm = PS.get()
    for l in range(DEPTH):
        for blk in range(24):
            slot = ring_i[0] % NSLOT
            ring_i[0] += 1
            stage = RING.f32(slot, 2048).rearrange("p (k n) -> p k n", k=8)
            fw.dma(sp, stage, w_ada[l].rearrange("(kc p) n -> p kc n", p=128)[:, :, blk * 256:(blk + 1) * 256],
                   writes=RING.b(slot))
            for jj in range(2):
                j = blk * 2 + jj
                col = (l * 48 + j) * nseq
                for kc in range(8):
                    MM(PS.t[:, pm, col:col + nseq], stage[:, kc, jj * 128:(jj + 1) * 128], cact[:, kc, :],
                       kc == 0, kc == 7, RING.b(slot) + [b_cact], [PS.bufs[pm]])
    for l in range(DEPTH):
        n48 = 48 * nseq
        TT(mod_sb[:, l * n48:(l + 1) * n48].rearrange("p (j s) -> p j s", s=nseq),
           PS.t[:, pm, l * n48:(l + 1) * n48].rearrange("p (j s) -> p j s", s=nseq),
           vecs_sb[:, l, V_BADA:V_BADA + 48].unsqueeze(2).to_broadcast([128, 48, nseq]),
           ALU.add, [PS.bufs[pm], b_vecs], [b_mod])
        for kind in (1, 4):
            a0 = (l * 6 + kind) * 8 * nseq
            TS(mod_sb[:, a0:a0 + 8 * nseq], mod_sb[:, a0:a0 + 8 * nseq], 1.0, None, ALU.add, None, [b_mod], [b_mod])
        for kind in (2, 5):
            a0 = (l * 6 + kind) * 8 * nseq
            TS(mod_sb[:, a0:a0 + 8 * nseq], mod_sb[:, a0:a0 + 8 * nseq], 1.0 / ALPHA, None, ALU.mult, None,
               [b_mod], [b_mod])
    PS.put(pm)

    def load_wpool(l):
        for gq in range(4):
            fw.dma(pool, wpool_sb[:, :, gq, :], w_pool[l, gq].rearrange("(i p) d -> p i d", p=128), writes=[b_wpool])

    def load_attw(l):
        fw.dma(pool, wuqn_sb[:], w_uqn[l].rearrange("(i p) n -> p i n", p=128), writes=[b_wuqn])
        fw.dma(pool, wuqr_sb[:], w_uqr[l].rearrange("(i p) n -> p i n", p=128), writes=[b_wuqr])
        fw.dma(pool, wuqrs_sb[:], w_uqrs[l].rearrange("(i p) n -> p i n", p=128), writes=[b_wuqrs])
        fw.dma(pool, wukT_sb[:], w_ukT[l], writes=[b_wukT])
        fw.dma(pool, wuv_sb[:], w_uv[l], writes=[b_wuv])

    def load_sguw(l):
        fw.dma(pool, wsT_sb[:], wsT[l], writes=[b_wsT])
        fw.dma(pool, bsT_sb[:], bsT[l], writes=[b_bsT])
        fw.dma(pool, gsg_sb[:], gsg[l].partition_broadcast(128), writes=[b_gsg])
        fw.dma(pool, bsg_sb[:], bsg[l].partition_broadcast(128), writes=[b_bsg])
        TT(wsT_sb[:].rearrange("p (g t) -> p g t", g=8), wsT_sb[:].rearrange("p (g t) -> p g t", g=8),
           mask_bf[:].unsqueeze(1).to_broadcast([128, 8, 128]), ALU.mult, [b_wsT, b_cst2], [b_wsT])

    def ln_stats(eps_tile):
        xb = lambda c: AR.bf16(c // 2, T, (c % 2) * T)
        xq = lambda c: AR.bf16(4 + c // 2, T, (c % 2) * T)
        pmn = PS.get()
        psq = PS.get()
        for pp in range(4):
            xb2 = AR.bf16(pp, 2 * T).rearrange("p (a n) -> p a n", a=2)
            xq2 = AR.bf16(4 + pp, 2 * T).rearrange("p (a n) -> p a n", a=2)
            xin = XC[0][:, 2 * pp:2 * pp + 2, :]
            rb = [XC[1][2 * pp], XC[1][2 * pp + 1]]
            CP(xb2, xin, rb, AR.b(pp))
            ACT(xq2, xin, AF.Square, rb, AR.b(4 + pp))
            for c in (2 * pp, 2 * pp + 1):
                MM(psb(pmn), onesm_bf[:], xb(c), c == 0, c == 7, AR.b(c // 2) + [b_cst2], [PS.bufs[pmn]])
                MM(psb(psq), onesm_bf[:], xq(c), c == 0, c == 7, AR.b(4 + c // 2) + [b_cst2], [PS.bufs[psq]])
        CP(st_mean[:], psb(pmn), [PS.bufs[pmn]], [b_mean])
        TT(st_var[:], st_mean[:], st_mean[:], ALU.mult, [b_mean], [b_var])
        TT(st_var[:], psb(psq), st_var[:], ALU.subtract, [PS.bufs[psq], b_var], [b_var])
        PS.put(pmn)
        PS.put(psq)
        ACT(st_var[:], st_var[:], AF.Ln, [b_var, b_cst2], [b_var], bias=eps_tile[:, 0:1])
        ACT(st_rstd[:], st_var[:], AF.Exp, [b_var], [b_rstd], scale=-0.5)

    def ln_apply(out_fn, out_bufs, scale_fn, bias_fn, extra_reads):
        for c in range(8):
            pg = 8 + (c % 4)
            t = AR.f32(pg, T)
            TT(t, XC[0][:, c, :], st_mean[:], ALU.subtract, [XC[1][c], b_mean], AR.b(pg), eng=pool)
            TT(t, t, st_rstd[:], ALU.mult, AR.b(pg) + [b_rstd], AR.b(pg))
            ACT(out_fn(c), t, AF.Identity, AR.b(pg) + extra_reads, out_bufs(c), scale=scale_fn(c), bias=bias_fn(c))

    def ln_mod(l, s, kind):
        ln_stats(eps_ln1)
        ln_apply(lambda c: h_sb[:, c, :], lambda c: [b_h[c]],
                 lambda c: mcol(l, 3 * kind + 1, c, s), lambda c: mcol(l, 3 * kind, c, s), [b_mod])

    def post_ln(l, goff, boff):
        ln_stats(eps_ln)
        ln_apply(lambda c: XC[0][:, c, :], lambda c: [XC[1][c]],
                 lambda c: vcol(l, goff, c), lambda c: vcol(l, boff, c), [b_vecs])

    def proj8(w_slot, col0, rhs_fn, rhs_bufs, ncols=512):
        b = PS.get()
        wv = ring_view(w_slot, 8, ncols)
        for k in range(8):
            MM(psb(b), wv[:, k, col0:col0 + 128], rhs_fn(k), k == 0, k == 7,
               RING.b(w_slot) + rhs_bufs(k), [PS.bufs[b]], signal=(k == 7))
        return b

    def proj8_kouter(specs, rhs_fn, rhs_bufs):
        banks = [PS.get() for _ in specs]
        for k in range(8):
            for (w_slot, col0, ncols), b in zip(specs, banks):
                wv = ring_view(w_slot, 8, ncols)
                MM(psb(b), wv[:, k, col0:col0 + 128], rhs_fn(k), k == 0, k == 7,
                   RING.b(w_slot) + rhs_bufs(k), [PS.bufs[b]])
        return banks

    def w_cols(w_l, c0, n):
        return w_l.rearrange("(kc p) n -> p kc n", p=128)[:, :, c0:c0 + n]

    def branch_out(l, bi, first):
        for half in range(2):
            ws = wload(scols(("w_br", l, bi), half * 512, 512))
            gs = wload(scols(("w_in", l), bi * D + half * 512, 512))
            for dd in range(4):
                d = half * 4 + dd
                by = proj8(ws, dd * 128, lambda k: branch(k), lambda k: b_branch(k))
                bg = proj8(gs, dd * 128, lambda k: h_sb[:, k, :], lambda k: [b_h[k]])
                pg = 12 + (d % 2)
                sig = AR.f32(pg, T)
                ACT(sig, psb(bg), AF.Sigmoid, [PS.bufs[bg]], AR.b(pg))
                PS.put(bg)
                if first:
                    TT(merged(d), psb(by), sig, ALU.mult, [PS.bufs[by]] + AR.b(pg), b_merged(d))
                else:
                    TT(sig, psb(by), sig, ALU.mult, [PS.bufs[by]] + AR.b(pg), AR.b(pg))
                    TT(merged(d), merged(d), sig, ALU.add, b_merged(d) + AR.b(pg), b_merged(d))
                PS.put(by)

    TWO_PI = 2.0 * np.pi
    MAGIC = 12582912.0
    CW1 = 6.28125
    CW2 = float(np.float32(TWO_PI - CW1))
    CW3 = float(TWO_PI - CW1 - np.float64(np.float32(TWO_PI - CW1)))
    PI_LO = 3.1415925

    def rope_tables(s, g):
        t0 = g * T
        pi_ = AR.t[0:64, 0:T].bitcast(I32)
        fw.dma(pool, pi_, pos[s:s + 1, t0:t0 + T].partition_broadcast(64), writes=AR.b(0))
        ang = AR.f32(1, T, parts=64)
        CP(ang, pi_, AR.b(0), AR.b(1))
        TS(ang, ang, const_sb[0:64, C_FREQ:C_FREQ + 1], None, ALU.mult, None, AR.b(1) + [b_const], AR.b(1))
        for which, outt, ob in ((0, sinpm, b_sin), (1, cos2, b_cos)):
            kk = AR.f32(2, T, parts=64)
            r = AR.f32(3, T, parts=64)
            if which == 0:
                TS(kk, ang, float(1.0 / TWO_PI), MAGIC, ALU.mult, ALU.add, AR.b(1), AR.b(2))
            else:
                TS(kk, ang, float(1.0 / TWO_PI), 0.25, ALU.mult, ALU.add, AR.b(1), AR.b(2))
                TS(kk, kk, MAGIC, None, ALU.add, None, AR.b(2), AR.b(2))
            TS(kk, kk, -MAGIC, None, ALU.add, None, AR.b(2), AR.b(2))
            STT(r, kk, -CW1, ang, ALU.mult, ALU.add, AR.b(1) + AR.b(2), AR.b(3))
            STT(r, kk, -CW2, r, ALU.mult, ALU.add, AR.b(2) + AR.b(3), AR.b(3))
            STT(r, kk, -CW3, r, ALU.mult, ALU.add, AR.b(2) + AR.b(3), AR.b(3))
            if which == 1:
                TS(r, r, float(np.pi / 2), None, ALU.add, None, AR.b(3), AR.b(3))
            TS(r, r, PI_LO, -PI_LO, ALU.min, ALU.max, AR.b(3), AR.b(3))
            ACT(outt[:], r, AF.Sin, AR.b(3), [ob])
        TS(sinpm[:], sinpm[:], const_sb[0:64, C_SGN:C_SGN + 1], None, ALU.mult, None, [b_sin, b_const], [b_sin])

    def mixer(l, s, g, nxt_l):
        t0 = g * T
        fw.phase = 'ln_mod_t'
        ln_mod(l, s, 0)
        fw.phase = 'pool'

        for half in range(2):
            wa = wload(scols(("w_in", l), OFF_POOL + half * 512, 512))
            pre = None
            if half == 0:
                pre = proj8_kouter([(wa, cc_ * 128, 512) for cc_ in range(4)], lambda k: h_sb[:, k, :], lambda k: [b_h[k]])
            for cc in range(4):
                c = half * 4 + cc
                w = POOL_W[c]
                ba = pre[cc] if pre else proj8(wa, cc * 128, lambda k: h_sb[:, k, :], lambda k: [b_h[k]])
                P0, P1, P2 = 0 + 3 * (c % 2), 1 + 3 * (c % 2), 2 + 3 * (c % 2)
                a = AR.f32(P0, 528)
                ce = pool if c in (1, 3, 5) else dve
                CP(a[:, 0:16], phalo[l][:, c, :], [b_phalo[l]], AR.b(P0))
                ACT(a[:, 16:528], psb(ba), AF.Copy, [PS.bufs[ba]], AR.b(P0))
                PS.put(ba)
                CP(phalo[l][:, c, :], a[:, 512:528], AR.b(P0), [b_phalo[l]])
                cur, curp = a, P0
                sh = 1
                lo = 1
                others = [P1, P2]
                oi = 0
                while sh < w:
                    np_ = others[oi % 2]
                    oi += 1
                    nt = AR.f32(np_, 528)
                    TT(nt[:, lo:528], cur[:, lo:528], cur[:, lo - sh:528 - sh], ALU.add, AR.b(curp), AR.b(np_), eng=ce)
                    cur, curp = nt, np_
                    sh *= 2
                    lo = lo + sh
                if g == 0:
                    wi = {2: 0, 4: 1, 8: 2, 16: 3}[w]
                    TT(cur[:, 16:32], cur[:, 16:32], const_sb[:, C_CORR + wi * 16:C_CORR + wi * 16 + 16], ALU.mult,
                       AR.b(curp) + [b_const], AR.b(curp))
                STT(branch(c), cur[:, 16:528], 1.0 / w, a[:, 16:528], ALU.mult, ALU.subtract,
                    AR.b(curp) + AR.b(P0), b_branch(c))
        for gq in range(4):
            bm = []
            for j in range(2):
                b = PS.get()
                for i in range(2):
                    MM(psb(b), wpool_sb[:, i, gq, j * 128:(j + 1) * 128], branch(2 * gq + i), i == 0, i == 1,
                       [b_wpool] + b_branch(2 * gq + i), [PS.bufs[b]])
                bm.append(b)
            for j in range(2):
                ACT(branch(2 * gq + j), psb(bm[j]), AF.Identity, [PS.bufs[bm[j]], b_vecs], b_branch(2 * gq + j),
                    scale=vcol(l, V_SPOOL, 2 * gq + j))
                PS.put(bm[j])
        if nxt_l is not None:
            load_wpool(nxt_l)
        fw.phase = 'bout_a'
        branch_out(l, 0, True)
        fw.phase = 'mla_proj'
        if l == 0:
            rope_tables(s, g)

        wq = wload(scols(("w_in", l), OFF_CQ, 448))
        wqv = RING.bf16(wq, 8 * 448).rearrange("p (k n) -> p k n", k=8)
        wk = wload((w_krs[l].rearrange("(kc p) n -> p kc n", p=128), []), eng=pool)
        wkv = RING.bf16(wk, 8 * 64).rearrange("p (k n) -> p k n", k=8)
        hb = lambda k: [b_h[k]]
        bq = []
        pms = PS.get()
        for i in range(2):
            b = PS.get()
            for k in range(8):
                MM(psb(b), wqv[:, k, i * 128:(i + 1) * 128], h_sb[:, k, :], k == 0, k == 7, RING.b(wq) + hb(k), [PS.bufs[b]])
            sq = AR.bf16(i, T)
            ACT(sq, psb(b), AF.Square, [PS.bufs[b]], AR.b(i))
            MM(psb(pms), ones_bf[:], sq, i == 0, i == 1, AR.b(i) + [b_cst2], [PS.bufs[pms]])
            bq.append(b)
        rq = AR.f32(2, T)
        ACT(rq, psb(pms), AF.Ln, [PS.bufs[pms], b_cst2], AR.b(2), scale=1.0 / 256, bias=eps_rms[:, 0:1])
        PS.put(pms)
        ACT(rq, rq, AF.Exp, AR.b(2), AR.b(2), scale=-0.5)
        for i in range(2):
            STT(cqn[:, i, :], psb(bq[i]), vcol(l, V_GQ, i), rq, ALU.mult, ALU.mult,
                [PS.bufs[bq[i]], b_vecs] + AR.b(2), [b_cqn])
            PS.put(bq[i])
        b = PS.get()
        pms = PS.get()
        for k in range(8):
            MM(psb(b), wqv[:, k, 256:384], h_sb[:, k, :], k == 0, k == 7, RING.b(wq) + hb(k), [PS.bufs[b]])
        sq = AR.bf16(3, T)
        ACT(sq, psb(b), AF.Square, [PS.bufs[b]], AR.b(3))
        MM(psb(pms), ones_bf[:], sq, True, True, AR.b(3) + [b_cst2], [PS.bufs[pms]])
        rk = AR.f32(4, T)
        ACT(rk, psb(pms), AF.Ln, [PS.bufs[pms], b_cst2], AR.b(4), scale=1.0 / 128, bias=eps_rms[:, 0:1])
        PS.put(pms)
        ACT(rk, rk, AF.Exp, AR.b(4), AR.b(4), scale=-0.5)
        STT(ckvnT[l][:, t0:t0 + T], psb(b), vcol(l, V_GKV, 0), rk, ALU.mult, ALU.mult,
            [PS.bufs[b], b_vecs] + AR.b(4), [b_kc[l][g]])
        PS.put(b)
        for tt in range(4):
            b = PS.get()
            tp = PS.t[:, b, 0:64].bitcast(BF16)
            fw.op(pe, lambda h, tp=tp, tt=tt: h.transpose(tp, ckvnT[l][:, t0 + tt * 128:t0 + (tt + 1) * 128], ident_bf[:]),
                  [b_kc[l][g], b_cst2], [PS.bufs[b]])
            CP(Vc[l][:, 4 * g + tt, :], tp, [PS.bufs[b]], [b_vc[l][g]])
            PS.put(b)
        b1 = PS.get()
        b2 = PS.get()
        for k in range(8):
            MM(psb(b1, parts=64), wqv[:, k, 384:448], h_sb[:, k, :], k == 0, k == 7, RING.b(wq) + hb(k), [PS.bufs[b1]])
        for k in range(8):
            MM(psb(b2, parts=64), wkv[:, k, :], h_sb[:, k, :], k == 0, k == 7, RING.b(wk) + hb(k), [PS.bufs[b2]])
        t1 = AR.f32(5, T, parts=64)
        t2 = AR.f32(6, T, parts=64)
        TT(t1, psb(b1, parts=64), cos2[:], ALU.mult, [PS.bufs[b1], b_cos], AR.b(5))
        TT(t2, psb(b2, parts=64), sinpm[:], ALU.mult, [PS.bufs[b2], b_sin], AR.b(6))
        PS.put(b1)
        PS.put(b2)
        TT(krotT[l][:, t0:t0 + T], t1, t2, ALU.add, AR.b(5) + AR.b(6), [b_kr[l][g]])

        ntile = 4 * g + 4
        fw.phase = 'mla_heads'
        qp_all = lambda hh: AR.bf16(hh // 2, T, (hh % 2) * T)
        qr_all = lambda hh: AR.bf16(4 + hh // 2, T, (hh % 2) * T, parts=64)
        for hh in range(8):
            par = hh % 2
            b = PS.get()
            for i in range(2):
                MM(psb(b), wuqn_sb[:, i, hh * 128:(hh + 1) * 128], cqn[:, i, :], i == 0, i == 1, [b_wuqn, b_cqn], [PS.bufs[b]])
            qn = AR.bf16(12 + par, T)
            ACT(qn, psb(b), AF.Copy, [PS.bufs[b]], AR.b(12 + par))
            PS.put(b)
            b = PS.get()
            MM(psb(b), wukT_sb[:, hh * 128:(hh + 1) * 128], qn, True, True, [b_wukT] + AR.b(12 + par), [PS.bufs[b]])
            ACT(qp_all(hh), psb(b), AF.Copy, [PS.bufs[b]], AR.b(hh // 2))
            PS.put(b)
            b1 = PS.get()
            b2 = PS.get()
            for i in range(2):
                MM(psb(b1, parts=64), wuqr_sb[:, i, hh * 64:(hh + 1) * 64], cqn[:, i, :], i == 0, i == 1, [b_wuqr, b_cqn], [PS.bufs[b1]])
            for i in range(2):
                MM(psb(b2, parts=64), wuqrs_sb[:, i, hh * 64:(hh + 1) * 64], cqn[:, i, :], i == 0, i == 1, [b_wuqrs, b_cqn], [PS.bufs[b2]])
            p1, p2 = 8 + 2 * par, 9 + 2 * par
            t1 = AR.f32(p1, T, parts=64)
            t2 = AR.f32(p2, T, parts=64)
            TT(t1, psb(b1, parts=64), cos2[:], ALU.mult, [PS.bufs[b1], b_cos], AR.b(p1))
            TT(t2, psb(b2, parts=64), sinpm[:], ALU.mult, [PS.bufs[b2], b_sin], AR.b(p2))
            PS.put(b1)
            PS.put(b2)
            TT(qr_all(hh), t1, t2, ALU.add, AR.b(p1) + AR.b(p2), AR.b(4 + hh // 2))

        units = []
        j = 0
        while j < ntile:
            npair = 2 if (j + 1 < 4 * g) else 1
            units.append((j, npair))
            j += npair
        uctr = [0]
        for hh in range(8):
            par = hh % 2
            qp = qp_all(hh)
            qr = qr_all(hh)
            bpo = PS.get_pair()
            po, pd = bpo, bpo + 1
            st = {}

            def S(ui):
                j0, npair = units[ui]
                bs = PS.get_pair() if npair == 2 else PS.get()
                for jj in range(npair):
                    jt = j0 + jj
                    qlo = max(0, jt - 4 * g) * 128
                    kg = jt // 4
                    MM(PS.t[:, bs + jj, qlo:T], ckvnT[l][:, jt * 128:(jt + 1) * 128], qp[:, qlo:T], True, False,
                       [b_kc[l][kg]] + AR.b(hh // 2), [PS.bufs[bs + jj]])
                    MM(PS.t[:, bs + jj, qlo:T], krotT[l][:, jt * 128:(jt + 1) * 128], qr[:, qlo:T], False, True,
                       [b_kr[l][kg]] + AR.b(4 + hh // 2), [PS.bufs[bs + jj]])
                st[ui] = bs

            def E(ui):
                j0, npair = units[ui]
                bs = st[ui]
                ppg = 8 + (uctr[0] % 3)
                uctr[0] += 1
                if npair == 2:
                    pT = AR.t[:, ppg * 544:ppg * 544 + 512].bitcast(BF16).rearrange("p (a n) -> p a n", a=2)
                    ACT(pT, PS.t[:, bs:bs + 2, :], AF.Exp, [PS.bufs[bs], PS.bufs[bs + 1]], AR.b(ppg), scale=ATT_SCALE)
                    pTs = [pT[:, 0, :], pT[:, 1, :]]
                else:
                    qlo = max(0, j0 - 4 * g) * 128
                    pT1 = AR.bf16(ppg, T)
                    ACT(pT1[:, qlo:T], PS.t[:, bs, qlo:T], AF.Exp, [PS.bufs[bs]], AR.b(ppg), scale=ATT_SCALE)
                    fw.op(dve, lambda h, pT1=pT1, qlo=qlo: h.memset(pT1[64:128, qlo:qlo + 64], 0.0), [], AR.b(ppg))
                    pTs = [pT1]
                for jj in range(npair):
                    PS.put(bs + jj)
                st[ui] = (pTs, ppg)

            def P(ui):
                j0, npair = units[ui]
                pTs, ppg = st[ui]
                for jj in range(npair):
                    jt = j0 + jj
                    ql = max(0, jt - 4 * g) * 128
                    kg = jt // 4
                    MM(PS.t[:, po, ql:T], Vc[l][:, jt, :], pTs[jj][:, ql:T], jt == 0, jt == ntile - 1,
                       [b_vc[l][kg]] + AR.b(ppg), [PS.bufs[po]])
                    MM(PS.t[:, pd, ql:T], ones_bf[:], pTs[jj][:, ql:T], jt == 0, jt == ntile - 1,
                       [b_cst2] + AR.b(ppg), [PS.bufs[pd]])

            nu = len(units)
            S(0)
            for ui in range(nu):
                if ui + 1 < nu:
                    S(ui + 1)
                E(ui)
                P(ui)
            rden = AR.f32(11, T)
            ACT(rden, psb(pd), AF.Ln, [PS.bufs[pd]], AR.b(11))
            ACT(rden, rden, AF.Exp, AR.b(11), AR.b(11), scale=-1.0)
            PS.put(pd)
            op_ = AR.bf16(12 + par, T)
            TT(op_, psb(po), rden, ALU.mult, [PS.bufs[po]] + AR.b(11), AR.b(12 + par))
            PS.put(po)
            b = PS.get()
            MM(psb(b), wuv_sb[:, hh * 128:(hh + 1) * 128], op_, True, True, [b_wuv] + AR.b(12 + par), [PS.bufs[b]])
            ACT(branch(hh), psb(b), AF.Copy, [PS.bufs[b]], b_branch(hh))
            PS.put(b)
        if nxt_l is not None:
            load_attw(nxt_l)
        fw.phase = 'bout_b'
        branch_out(l, 1, False)
        fw.phase = 'sgu'

        wv_s = [wload(scols(("w_in", l), OFF_SG + D + nn * 512, 512)) for nn in range(2)]
        vn_all = lambda tb: AR.bf16(4 + tb, 1024)
        for tb in range(4):
            bp = PS.get_pair()
            for nn in range(2):
                wv = ring_view(wv_s[nn], 8, 512)
                for k in range(8):
                    MM(PS.t[:, bp + nn, :], h_sb[:, k, tb * 128:(tb + 1) * 128], wv[:, k, :], k == 0, k == 7,
                       RING.b(wv_s[nn]) + [b_h[k]], [PS.bufs[bp + nn]])
            gpg = 0 + 2 * (tb % 2)
            gv = AR.t[:, gpg * 544:gpg * 544 + 1024]
            ACT(gv.rearrange("p (a n) -> p a n", a=2), PS.t[:, bp:bp + 2, :], AF.Gelu,
                [PS.bufs[bp], PS.bufs[bp + 1]], AR.b(gpg, 2))
            PS.put(bp)
            PS.put(bp + 1)
            stt_ = AR.f32(12, 16)
            for i in range(2):
                fw.op(dve, lambda h, i=i, gv=gv, stt_=stt_: h.bn_stats(out=stt_[:, i * 6:(i + 1) * 6], in_=gv[:, i * 512:(i + 1) * 512]),
                      AR.b(gpg, 2), AR.b(12))
            mvv = AR.f32(12, 2, off=16)
            fw.op(dve, lambda h, mvv=mvv, stt_=stt_: h.bn_aggr(out=mvv, in_=stt_[:, 0:12].rearrange("p (a n) -> p a n", a=2)),
                  AR.b(12), AR.b(12))
            rs = AR.f32(12, 1, off=20)
            ACT(rs, mvv[:, 1:2], AF.Sqrt, AR.b(12) + [b_cst2], AR.b(12), bias=eps_ln1[:, 0:1])
            fw.op(dve, lambda h, rs=rs: h.reciprocal(out=rs, in_=rs), AR.b(12), AR.b(12))
            TS(gv, gv, mvv[:, 0:1], rs, ALU.subtract, ALU.mult, AR.b(gpg, 2) + AR.b(12), AR.b(gpg, 2))
            TT(gv, gv, gsg_sb[:], ALU.mult, AR.b(gpg, 2) + [b_gsg], AR.b(gpg, 2))
            TT(vn_all(tb), gv, bsg_sb[:], ALU.add, AR.b(gpg, 2) + [b_bsg], AR.b(4 + tb))
        for half in range(2):
            wu = wload(scols(("w_in", l), OFF_SG + half * 512, 512))
            for gg in range(4):
                gq = half * 4 + gg
                bu = proj8(wu, gg * 128, lambda k: h_sb[:, k, :], lambda k: [b_h[k]])
                ACT(branch(gq), psb(bu), AF.Gelu, [PS.bufs[bu]], b_branch(gq))
                PS.put(bu)
        for gq in range(8):
            bz = PS.get()
            for tb in range(4):
                MM(PS.t[:, bz, tb * 128:(tb + 1) * 128], vn_all(tb)[:, gq * 128:(gq + 1) * 128],
                   wsT_sb[:, gq * 128:(gq + 1) * 128], True, False, AR.b(4 + tb) + [b_wsT], [PS.bufs[bz]])
                MM(PS.t[:, bz, tb * 128:(tb + 1) * 128], ones_bf[0:1, :], bsT_sb[0:1, gq * 128:(gq + 1) * 128],
                   False, True, [b_cst2, b_bsT], [PS.bufs[bz]])
            TT(branch(gq), branch(gq), psb(bz), ALU.mult, b_branch(gq) + [PS.bufs[bz]], b_branch(gq))
            PS.put(bz)
        if nxt_l is not None:
            load_sguw(nxt_l)
        fw.phase = 'bout_c'
        branch_out(l, 2, False)
        fw.phase = 'wo'

        for d in range(8):
            CP(branch(d), merged(d), b_merged(d), b_branch(d))
        for half in range(2):
            wo = wload(scols(("w_o", l), half * 512, 512))
            for dd in range(4):
                d = half * 4 + dd
                by = proj8(wo, dd * 128, lambda k: branch(k), lambda k: b_branch(k))
                STT(XC[0][:, d, :], psb(by), mcol(l, 2, d, s), XC[0][:, d, :], ALU.mult, ALU.add,
                    [PS.bufs[by], b_mod, XC[1][d]], [XC[1][d]])
                PS.put(by)
        fw.phase = 'post_ln_t'
        post_ln(l, V_LNTG, V_LNTB)

    def ffn(l, s, g):
        fw.phase = 'ln_mod_f'
        ln_mod(l, s, 1)
        fw.phase = 'ffn_up'
        nblk = (NF + 3) // 4

        def ffn_stage2(f):
            av = AR.f32(2 * (f % 3), T)
            ag = AR.f32(2 * (f % 3) + 1, T)
            pav, pag = 2 * (f % 3), 2 * (f % 3) + 1
            ACT(ag, ag, AF.Silu, AR.b(pag), AR.b(pag))
            TT(actT(f), ag, av, ALU.mult, AR.b(pag) + AR.b(pav), b_actT(f), eng=pool)

        for fb in range(nblk):
            nfc = min(4, NF - fb * 4)
            wvs = wload(scols(("w_up", l), fb * 512, nfc * 128))
            wgs = wload(scols(("w_up", l), D_FF + fb * 512, nfc * 128))
            pre = {}
            if fb == 0:
                bks = proj8_kouter([(wsl_, ff_ * 128, nfc * 128) for ff_ in range(2) for wsl_ in (wvs, wgs)],
                                   lambda k: h_sb[:, k, :], lambda k: [b_h[k]])
                pre = {(0, 0): bks[0], (0, 1): bks[1], (1, 0): bks[2], (1, 1): bks[3]}
            for ff in range(nfc):
                f = fb * 4 + ff
                items = []
                for which, wsl in ((0, wvs), (1, wgs)):
                    ch = which * NF + f
                    wv = ring_view(wsl, 8, nfc * 128)
                    if (ff, which) in pre:
                        b = pre[(ff, which)]
                    else:
                        b = PS.get()
                        for k in range(8):
                            MM(psb(b), wv[:, k, ff * 128:(ff + 1) * 128], h_sb[:, k, :], k == 0, k == 7,
                               RING.b(wsl) + [b_h[k]], [PS.bufs[b]], signal=(k == 7))
                    pa = 2 * (f % 3) + which
                    items.append((b, pa, AR.f32(pa, T), zhalo[l][:, ch, :], [b_zhalo[l][ch]],
                                  vcol(l, V_CW, ch), vcol(l, V_CW + 2 * NF, ch), vcol(l, V_CW + 4 * NF, ch), vcol(l, V_CB, ch)))
                for (b, pa, acc, zh, bzh, w0, w1, w2, cb) in items:
                    ACT(acc, psb(b), AF.Identity, [PS.bufs[b], b_vecs], AR.b(pa), scale=w2, bias=cb)
                for (b, pa, acc, zh, bzh, w0, w1, w2, cb) in items:
                    STT(acc[:, 1:T], psb(b, 0, T - 1), w1, acc[:, 1:T], ALU.mult, ALU.add,
                        [PS.bufs[b], b_vecs] + AR.b(pa), AR.b(pa))
                for (b, pa, acc, zh, bzh, w0, w1, w2, cb) in items:
                    STT(acc[:, 2:T], psb(b, 0, T - 2), w0, acc[:, 2:T], ALU.mult, ALU.add,
                        [PS.bufs[b], b_vecs] + AR.b(pa), AR.b(pa))
                for (b, pa, acc, zh, bzh, w0, w1, w2, cb) in items:
                    STT(acc[:, 0:1], zh[:, 1:2], w1, acc[:, 0:1], ALU.mult, ALU.add, bzh + [b_vecs] + AR.b(pa), AR.b(pa))
                for (b, pa, acc, zh, bzh, w0, w1, w2, cb) in items:
                    STT(acc[:, 0:2], zh, w0, acc[:, 0:2], ALU.mult, ALU.add, bzh + [b_vecs] + AR.b(pa), AR.b(pa))
                for (b, pa, acc, zh, bzh, w0, w1, w2, cb) in items:
                    ACT(zh, psb(b, T - 2, T), AF.Copy, [PS.bufs[b]], bzh)
                    PS.put(b)
                if f > 0:
                    ffn_stage2(f - 1)
        ffn_stage2(NF - 1)
        fw.phase = 'ffn_down'
        for d in range(8):
            wd = wload(scols(("w_down", l), d * 128, 128))
            wv = RING.bf16(wd, NF * 128).rearrange("p (k n) -> p k n", k=NF)
            b = PS.get()
            for f in range(NF):
                MM(psb(b), wv[:, f, :], actT(f), f == 0, f == NF - 1, RING.b(wd) + b_actT(f), [PS.bufs[b]], signal=(f == NF - 1))
            STT(XC[0][:, d, :], psb(b), mcol(l, 5, d, s), XC[0][:, d, :], ALU.mult, ALU.add,
                [PS.bufs[b], b_mod, XC[1][d]], [XC[1][d]])
            PS.put(b)
        fw.phase = 'post_ln_f'
        post_ln(l, V_LNFG, V_LNFB)

    steps = [(s, g, l) for s in range(nseq) for g in range(ngroups) for l in range(nlayers)]
    groups = [(s, g) for s in range(nseq) for g in range(ngroups)]
    load_wpool(0)
    load_attw(0)
    load_sguw(0)

    def load_x(gi):
        s_, g_ = groups[gi]
        fw.dma(pool, xs_t[gi % 2][:], xT[s_].rearrange("(kc p) n -> p kc n", p=128)[:, :, g_ * T:(g_ + 1) * T],
               writes=xs_b[gi % 2])

    load_x(0)
    for idx, (s, g, l) in enumerate(steps):
        nxt_l = steps[idx + 1][2] if idx + 1 < len(steps) else None
        gi = s * ngroups + g
        XC[0], XC[1] = xs_t[gi % 2], xs_b[gi % 2]
        t0 = g * T
        if l == 0 and g == 0:
            for ll in range(nlayers):
                fw.op(dve, lambda h, ll=ll: h.memset(phalo[ll][:], 0.0), [], [b_phalo[ll]])
                fw.op(dve, lambda h, ll=ll: h.memset(zhalo[ll][:], 0.0), [], b_zhalo[ll])
        mixer(l, s, g, nxt_l)
        if dbg == (s, g, l, "mix"):
            fw.dma(pool, dbg_out, XC[0][:], reads=XC[1])
        if l == nlayers - 1 and gi + 1 < len(groups):
            load_x(gi + 1)
        ffn(l, s, g)
        if dbg == (s, g, l, "ffn"):
            fw.dma(pool, dbg_out, XC[0][:], reads=XC[1])
        if l == nlayers - 1:
            fw.dma(pool, outT[s].rearrange("(kc p) n -> p kc n", p=128)[:, :, t0:t0 + T], XC[0][:], reads=XC[1])
    fw.final_fence(sp)
    fw.emit()
    return nc, fw


def _host_prep(inp):
    f = lambda a: np.ascontiguousarray(np.asarray(a, dtype=np.float32))
    L = DEPTH
    w_in = np.asarray(inp["w_in"], np.float32)
    w_uq = np.asarray(inp["w_uq"], np.float32)
    w_ukv = np.asarray(inp["w_ukv"], np.float32)
    perm = np.concatenate([np.arange(32, 64), np.arange(0, 32)])
    shared = {
        "w_ada": f(inp["w_ada"]),
        "w_in": f(w_in),
        "w_krs": f(w_in[:, :, OFF_KR:OFF_KR + 64][:, :, perm]),
        "w_pool": f(inp["w_pool"]),
        "w_uqn": f(w_uq[:, :, :, 0:128].reshape(L, 256, 1024)),
        "w_uqr": f(w_uq[:, :, :, 128:192].reshape(L, 256, 512)),
        "w_uqrs": f(w_uq[:, :, :, 128:192][:, :, :, perm].reshape(L, 256, 512)),
        "w_ukT": f(np.transpose(w_ukv[:, :, :, 0:128], (0, 3, 2, 1)).reshape(L, 128, 1024)),
        "w_uv": f(w_ukv[:, :, :, 128:256].reshape(L, 128, 1024)),
        "wsT": f(np.transpose(np.asarray(inp["w_s"], np.float32), (0, 3, 1, 2)).reshape(L, 128, 1024)),
        "bsT": f(np.transpose(np.asarray(inp["b_s"], np.float32), (0, 2, 1)).reshape(L, 1, 1024)),
        "gsg": f(np.asarray(inp["g_sg"], np.float32).reshape(L, 1, 1024)),
        "bsg": f(np.asarray(inp["b_sg"], np.float32).reshape(L, 1, 1024)),
        "w_branch": f(inp["w_branch"]),
        "w_o": f(inp["w_o"]),
        "w_up": f(inp["w_up"]),
        "w_down": f(inp["w_down"]),
    }
    pp = lambda v, n: np.transpose(np.asarray(v, np.float32).reshape(L, n, 128), (0, 2, 1))
    vecs = np.zeros((L, 128, NV), np.float32)
    vecs[:, :, V_SPOOL:V_SPOOL + 8] = pp(inp["s_pool"], 8)
    vecs[:, :, V_GQ:V_GQ + 2] = pp(inp["g_q"], 2)
    vecs[:, :, V_GKV:V_GKV + 1] = pp(inp["g_kv"], 1)
    vecs[:, :, V_LNTG:V_LNTG + 8] = pp(inp["ln_t_g"], 8)
    vecs[:, :, V_LNTB:V_LNTB + 8] = pp(inp["ln_t_b"], 8)
    vecs[:, :, V_LNFG:V_LNFG + 8] = pp(inp["ln_f_g"], 8)
    vecs[:, :, V_LNFB:V_LNFB + 8] = pp(inp["ln_f_b"], 8)
    cw = np.asarray(inp["conv_w"], np.float32)
    for k in range(3):
        vecs[:, :, V_CW + k * 44:V_CW + (k + 1) * 44] = pp(cw[:, k], 44)
    vecs[:, :, V_CB:V_CB + 44] = pp(inp["conv_b"], 44)
    vecs[:, :, V_BADA:V_BADA + 48] = pp(inp["b_ada"], 48)
    shared["vecs"] = vecs
    consts = np.zeros((128, NCONST), np.float32)
    consts[:, C_IDENT:C_IDENT + 128] = np.eye(128, dtype=np.float32)
    consts[:, C_MASK:C_MASK + 128] = np.triu(np.ones((128, 128), np.float32))
    freqs = (10000.0 ** (-np.arange(0, 64, 2, dtype=np.float32) / 64)).astype(np.float32)
    consts[0:64, C_FREQ] = np.concatenate([freqs, freqs])
    consts[0:32, C_SGN] = -1.0
    consts[32:64, C_SGN] = 1.0
    for wi, w in enumerate((2, 4, 8, 16)):
        tt = np.arange(16)
        consts[:, C_CORR + wi * 16:C_CORR + (wi + 1) * 16] = (w / np.minimum(tt + 1, w)).astype(np.float32)[None, :]
    shared["consts"] = consts
    return shared


_CACHE = {}


def kernel(**inputs):
    x = np.asarray(inputs["x"], np.float32)
    c = np.asarray(inputs["c"], np.float32)
    pos = np.asarray(inputs["pos"], np.int32)
    shared = _host_prep(inputs)
    key = "full"
    if key not in _CACHE:
        _CACHE[key] = build_program()
    nc, _ = _CACHE[key]
    in_maps = []
    for core in range(NCORES):
        b0 = core * SPC
        m = dict(shared)
        m["xT"] = np.ascontiguousarray(np.transpose(x[b0:b0 + SPC], (0, 2, 1)))
        m["cT"] = np.ascontiguousarray(np.transpose(c[b0:b0 + SPC].reshape(SPC, 8, 128), (2, 1, 0)))
        m["pos"] = np.ascontiguousarray(pos[b0:b0 + SPC])
        in_maps.append(m)
    res = run_bass_kernel_spmd(nc, in_maps, core_ids=list(range(NCORES)))
    out = np.empty((BATCH, SEQ, D), np.float32)
    for core in range(NCORES):
        o = res.results[core]["outT"]
        out[core * SPC:(core + 1) * SPC] = np.transpose(o, (0, 2, 1))
    return out
```
